# Optimizing a Trainium2 kernel written in Bass

```python
import math
import jax, jax.numpy as jnp
from jax import lax
import numpy as np

D_MODEL = 1024
BATCH = 4
SEQ = 4096
DEPTH = 4

CHUNK = 64
N_MIXERS = 2
N_LAYERS_A = (DEPTH + 1) // 2
N_LAYERS_B = DEPTH // 2
FOX_HEADS = 16
FOX_HEAD_DIM = D_MODEL // FOX_HEADS
Q_BLOCK = 128
S5_GROUP = 16
S5_GROUPS = D_MODEL // S5_GROUP
S5_STATE = 64
D_FF = 256 * ((8 * D_MODEL // 3 + 255) // 256)
ALPHA = (2.0 * DEPTH) ** 0.25
BETA = (8.0 * DEPTH) ** -0.25
LN_EPS = 1e-5
NEG_INF = -1e30

kernel_name = "fox_s5_macaron_deepnorm_hybrid"


def layer_norm(x, g, b):
    xf = x.astype(jnp.float32)
    mu = jnp.mean(xf, axis=-1, keepdims=True)
    var = jnp.mean(jnp.square(xf - mu), axis=-1, keepdims=True)
    y = (xf - mu) * lax.rsqrt(var + LN_EPS) * g.astype(jnp.float32) + b.astype(jnp.float32)
    return y.astype(x.dtype)


def swiglu(x, w_in, w_out):
    gate, up = jnp.split(x @ w_in, 2, axis=-1)
    return (jax.nn.silu(gate) * up) @ w_out


def fox_attention(x, w_in, b_f, w_o):
    bsz, seq, d = x.shape
    proj = x @ w_in
    q = proj[..., :d].reshape(bsz, seq, FOX_HEADS, FOX_HEAD_DIM)
    k = proj[..., d:2 * d].reshape(bsz, seq, FOX_HEADS, FOX_HEAD_DIM)
    v = proj[..., 2 * d:3 * d].reshape(bsz, seq, FOX_HEADS, FOX_HEAD_DIM)
    f_logit = proj[..., 3 * d:].astype(jnp.float32) + b_f.astype(jnp.float32)
    log_f = jax.nn.log_sigmoid(f_logit)
    cum = jnp.cumsum(log_f, axis=1).transpose(0, 2, 1)
    n_blk = seq // Q_BLOCK
    q_blocks = q.reshape(bsz, n_blk, Q_BLOCK, FOX_HEADS, FOX_HEAD_DIM).transpose(1, 0, 2, 3, 4)
    c_blocks = cum.reshape(bsz, FOX_HEADS, n_blk, Q_BLOCK).transpose(2, 0, 1, 3)
    starts = jnp.arange(n_blk, dtype=jnp.int32) * Q_BLOCK
    k_pos = jnp.arange(seq, dtype=jnp.int32)
    scale = 1.0 / math.sqrt(FOX_HEAD_DIM)

    def one_block(args):
        q_i, c_i, start = args
        s = jnp.einsum('bqhd,bkhd->bhqk', q_i, k).astype(jnp.float32) * scale
        s = s + (c_i[..., :, None] - cum[:, :, None, :])
        q_pos = start + jnp.arange(Q_BLOCK, dtype=jnp.int32)
        mask = k_pos[None, :] <= q_pos[:, None]
        s = jnp.where(mask, s, NEG_INF)
        p = jax.nn.softmax(s, axis=-1).astype(v.dtype)
        return jnp.einsum('bhqk,bkhd->bqhd', p, v)

    o = lax.map(one_block, (q_blocks, c_blocks, starts))
    o = o.transpose(1, 0, 2, 3, 4).reshape(bsz, seq, d)
    return o @ w_o


def s5_mixer(x, a_re, a_im, log_dt, b_re, b_im, c_re, c_im, d_skip, w_out):
    bsz, seq, d = x.shape
    u = x.reshape(bsz, seq, S5_GROUPS, S5_GROUP).astype(jnp.float32)
    lr = a_re.astype(jnp.float32)
    li = a_im.astype(jnp.float32)
    dt = jnp.exp(log_dt.astype(jnp.float32))[:, None]
    mag = jnp.exp(lr * dt)
    ang = li * dt
    lb_re = mag * jnp.cos(ang)
    lb_im = mag * jnp.sin(ang)
    den = lr * lr + li * li
    nr = lb_re - 1.0
    ni = lb_im
    z_re = (nr * lr + ni * li) / den
    z_im = (ni * lr - nr * li) / den
    br = b_re.astype(jnp.float32)
    bi = b_im.astype(jnp.float32)
    bb_re = z_re[..., None] * br - z_im[..., None] * bi
    bb_im = z_re[..., None] * bi + z_im[..., None] * br
    bu_re = jnp.einsum('bsgh,gph->bsgp', u, bb_re)
    bu_im = jnp.einsum('bsgh,gph->bsgp', u, bb_im)
    a_re_s = jnp.broadcast_to(lb_re[None, None], (1, seq, S5_GROUPS, S5_STATE))
    a_im_s = jnp.broadcast_to(lb_im[None, None], (1, seq, S5_GROUPS, S5_STATE))

    def combine(e1, e2):
        ar1, ai1, hr1, hi1 = e1
        ar2, ai2, hr2, hi2 = e2
        return (ar2 * ar1 - ai2 * ai1,
                ar2 * ai1 + ai2 * ar1,
                ar2 * hr1 - ai2 * hi1 + hr2,
                ar2 * hi1 + ai2 * hr1 + hi2)

    _, _, h_re, h_im = lax.associative_scan(combine, (a_re_s, a_im_s, bu_re, bu_im), axis=1)
    y = (jnp.einsum('ghp,bsgp->bsgh', c_re.astype(jnp.float32), h_re)
         - jnp.einsum('ghp,bsgp->bsgh', c_im.astype(jnp.float32), h_im)
         + d_skip.astype(jnp.float32) * u)
    y = jax.nn.gelu(y.reshape(bsz, seq, d)).astype(x.dtype)
    val, gate = jnp.split(y @ w_out, 2, axis=-1)
    return val * jax.nn.sigmoid(gate)


def setup_inputs(seed: int = 0) -> dict:
    key = jax.random.key(seed)
    ks = jax.random.split(key, 32)
    f32 = jnp.float32
    D, F, H = D_MODEL, D_FF, FOX_HEADS
    G, P, GH = S5_GROUPS, S5_STATE, S5_GROUP

    def nrm(k, shape, fan_in, scale=1.0):
        return jax.random.normal(k, shape, f32) * (scale * fan_in ** -0.5)

    def gain(k, shape):
        return 1.0 + 0.02 * jax.random.normal(k, shape, f32)

    def bias(k, shape):
        return 0.02 * jax.random.normal(k, shape, f32)

    x = jax.random.normal(ks[0], (BATCH, SEQ, D), f32)
    ffn1_w_in = nrm(ks[1], (DEPTH, D, 2 * F), D)
    ffn1_w_out = nrm(ks[2], (DEPTH, F, D), F, BETA)
    ln1_g = gain(ks[3], (DEPTH, D))
    ln1_b = bias(ks[4], (DEPTH, D))
    lnm_g = gain(ks[5], (DEPTH, D))
    lnm_b = bias(ks[6], (DEPTH, D))
    ffn2_w_in = nrm(ks[7], (DEPTH, D, 2 * F), D)
    ffn2_w_out = nrm(ks[8], (DEPTH, F, D), F, BETA)
    ln2_g = gain(ks[9], (DEPTH, D))
    ln2_b = bias(ks[10], (DEPTH, D))

    fox_w_in = jnp.concatenate([
        nrm(ks[11], (N_LAYERS_A, D, 2 * D), D),
        nrm(ks[12], (N_LAYERS_A, D, D), D, BETA),
        nrm(ks[13], (N_LAYERS_A, D, H), D, 0.5),
    ], axis=-1)
    fox_b_f = (jnp.linspace(1.0, 6.0, H, dtype=f32)[None, :]
               + 0.1 * jax.random.normal(ks[14], (N_LAYERS_A, H), f32))
    fox_w_o = nrm(ks[15], (N_LAYERS_A, D, D), D, BETA)

    s5_a_re = -0.5 * jnp.exp(0.05 * jax.random.normal(ks[16], (N_LAYERS_B, G, P), f32))
    s5_a_im = (jnp.pi * jnp.arange(P, dtype=f32))[None, None, :] \
        + 1e-3 * jax.random.normal(ks[17], (N_LAYERS_B, G, P), f32)
    s5_log_dt = jax.random.uniform(ks[18], (N_LAYERS_B, G), f32,
                                   minval=math.log(1e-3), maxval=math.log(1e-1))
    s5_b_re = nrm(ks[19], (N_LAYERS_B, G, P, GH), 2 * GH)
    s5_b_im = nrm(ks[20], (N_LAYERS_B, G, P, GH), 2 * GH)
    s5_c_re = nrm(ks[21], (N_LAYERS_B, G, GH, P), 2 * P)
    s5_c_im = nrm(ks[22], (N_LAYERS_B, G, GH, P), 2 * P)
    s5_d = jax.random.normal(ks[23], (N_LAYERS_B, G, GH), f32)
    s5_w_out = jnp.concatenate([
        nrm(ks[24], (N_LAYERS_B, D, D), D, BETA),
        nrm(ks[25], (N_LAYERS_B, D, D), D),
    ], axis=-1)

    return {"x": x,
            "ffn1_w_in": ffn1_w_in, "ffn1_w_out": ffn1_w_out, "ln1_g": ln1_g, "ln1_b": ln1_b,
            "lnm_g": lnm_g, "lnm_b": lnm_b,
            "ffn2_w_in": ffn2_w_in, "ffn2_w_out": ffn2_w_out, "ln2_g": ln2_g, "ln2_b": ln2_b,
            "fox_w_in": fox_w_in, "fox_b_f": fox_b_f, "fox_w_o": fox_w_o,
            "s5_a_re": s5_a_re, "s5_a_im": s5_a_im, "s5_log_dt": s5_log_dt,
            "s5_b_re": s5_b_re, "s5_b_im": s5_b_im, "s5_c_re": s5_c_re, "s5_c_im": s5_c_im,
            "s5_d": s5_d, "s5_w_out": s5_w_out}


def reference(x, ffn1_w_in, ffn1_w_out, ln1_g, ln1_b, lnm_g, lnm_b,
              ffn2_w_in, ffn2_w_out, ln2_g, ln2_b,
              fox_w_in, fox_b_f, fox_w_o,
              s5_a_re, s5_a_im, s5_log_dt, s5_b_re, s5_b_im, s5_c_re, s5_c_im,
              s5_d, s5_w_out):
    for i in range(DEPTH):
        x = layer_norm(ALPHA * x + 0.5 * swiglu(x, ffn1_w_in[i], ffn1_w_out[i]), ln1_g[i], ln1_b[i])
        j = i // N_MIXERS
        if i % N_MIXERS == 0:
            m = fox_attention(x, fox_w_in[j], fox_b_f[j], fox_w_o[j])
        else:
            m = s5_mixer(x, s5_a_re[j], s5_a_im[j], s5_log_dt[j], s5_b_re[j], s5_b_im[j],
                         s5_c_re[j], s5_c_im[j], s5_d[j], s5_w_out[j])
        x = layer_norm(ALPHA * x + m, lnm_g[i], lnm_b[i])
        x = layer_norm(ALPHA * x + 0.5 * swiglu(x, ffn2_w_in[i], ffn2_w_out[i]), ln2_g[i], ln2_b[i])
    return x
```

```python
import contextlib
import math
import numpy as np
import concourse.bass as bass
import concourse.mybir as mybir
from concourse.bass_utils import run_bass_kernel_spmd

F32 = mybir.dt.float32
BF16 = mybir.dt.bfloat16
AF = mybir.ActivationFunctionType
ALU = mybir.AluOpType

D = 1024
NDC = 8
FF = 2816
NFC = 22
NH = 16
HD = 64
DEPTH = 4
ALPHA = (2.0 * DEPTH) ** 0.25
LN_EPS = 1e-5
TS = 1024
TB = 512


class Res:
    __slots__ = ("name", "lw", "readers", "slot", "epoch", "last_tok")

    def __init__(self, name=""):
        self.name = name
        self.lw = None
        self.readers = []
        self.slot = None
        self.epoch = -1
        self.last_tok = None


class Op:
    __slots__ = ("eng", "idx", "fn", "deps", "flag", "dma_tok", "waits")

    def __init__(self, eng, idx, fn):
        self.eng = eng
        self.idx = idx
        self.fn = fn
        self.deps = []
        self.flag = False
        self.dma_tok = None
        self.waits = None


COMPUTE = ("pe", "act", "dve", "pool")


class Sched:
    def __init__(self, nc):
        self.nc = nc
        self.ops = {e: [] for e in ("pe", "act", "dve", "pool", "sp")}
        self.slot_cnt = []
        self.slot_kind = []
        self.free_slots = []
        self.epoch = 0
        self.bar = {e: [] for e in self.ops}

    def _collect(self, op, reads, writes):
        deps = []
        for r in reads:
            if r.lw is not None:
                deps.append(r.lw)
        for w in writes:
            if w.lw is not None:
                deps.append(w.lw)
            deps.extend(w.readers)
        if self.bar[op.eng]:
            deps.extend(self.bar[op.eng])
            self.bar[op.eng] = []
        op.deps = deps
        for r in reads:
            r.readers.append(op)
        for w in writes:
            w.lw = op
            w.readers = []

    def op(self, eng, fn, reads=(), writes=()):
        o = Op(eng, len(self.ops[eng]), fn)
        self._collect(o, reads, writes)
        self.ops[eng].append(o)
        return o

    def dma(self, queue, fn, dst, reads=(), writes=()):
        o = Op(queue, len(self.ops[queue]), fn)
        ws = list(writes)
        if dst not in ws:
            ws.append(dst)
        self._collect(o, reads, ws)
        kind = "sw" if queue == "pool" else "hw"
        if dst.epoch != self.epoch or self.slot_kind[dst.slot] != kind:
            fl = [i for i in self.free_slots if self.slot_kind[i] == kind]
            if fl:
                dst.slot = fl[-1]
                self.free_slots.remove(dst.slot)
            else:
                dst.slot = len(self.slot_cnt)
                self.slot_cnt.append(0)
                self.slot_kind.append(kind)
            dst.epoch = self.epoch
        self.slot_cnt[dst.slot] += 16
        o.dma_tok = (dst.slot, self.slot_cnt[dst.slot])
        dst.last_tok = o.dma_tok
        self.ops[queue].append(o)
        return o

    def barrier(self):
        last = []
        for e, lst in self.ops.items():
            for o in reversed(lst):
                if o.dma_tok is None:
                    last.append(o)
                    break
        seen = set()
        for e, lst in self.ops.items():
            for o in reversed(lst):
                if o.dma_tok is not None and o.dma_tok[0] not in seen:
                    seen.add(o.dma_tok[0])
                    last.append(o)
        for e in self.ops:
            self.bar[e] = list(last)
        self.epoch += 1
        self.free_slots = list(range(len(self.slot_cnt)))

    def emit(self, final_wait=()):
        nc = self.nc
        for e, lst in self.ops.items():
            seen_eng = {}
            seen_dma = {}
            for o in lst:
                need_eng = {}
                need_dma = {}
                for d in o.deps:
                    if d.dma_tok is not None:
                        r, v = d.dma_tok
                        if seen_dma.get(r, 0) < v and need_dma.get(r, 0) < v:
                            need_dma[r] = v
                    else:
                        if d.eng == e and (e == "pe" or d.idx >= o.idx):
                            continue
                        if seen_eng.get(d.eng, -1) < d.idx and need_eng.get(d.eng, -1) < d.idx:
                            need_eng[d.eng] = d.idx
                for s, i in need_eng.items():
                    seen_eng[s] = i
                    self.ops[s][i].flag = True
                for r, v in need_dma.items():
                    seen_dma[r] = v
                o.waits = (need_eng, need_dma)
        cnt = {}
        for e, lst in self.ops.items():
            c = 0
            arr = []
            for o in lst:
                if o.flag:
                    c += 1
                arr.append(c)
            cnt[e] = arr
        with contextlib.ExitStack() as st:
            esem = {e: st.enter_context(nc.semaphore("s_" + e)) for e in COMPUTE}
            dsem = [st.enter_context(nc.semaphore("d%d" % i)) for i in range(len(self.slot_cnt))]
            block = st.enter_context(nc.Block())
            ops = self.ops

            def run(engname, eng):
                for o in ops[engname]:
                    need_eng, need_dma = o.waits
                    for s, i in need_eng.items():
                        eng.wait_ge(esem[s], cnt[s][i])
                    for r, v in need_dma.items():
                        eng.wait_ge(dsem[r], v)
                    ins = o.fn(eng)
                    if o.dma_tok is not None:
                        ins.then_inc(dsem[o.dma_tok[0]], 16)
                    elif o.flag:
                        ins.then_inc(esem[engname], 1)
                if engname == "sp":
                    for r in final_wait:
                        eng.wait_ge(dsem[r.last_tok[0]], r.last_tok[1])

            @block.tensor
            def _(eng):
                run("pe", eng)

            @block.scalar
            def _(eng):
                run("act", eng)

            @block.vector
            def _(eng):
                run("dve", eng)

            @block.gpsimd
            def _(eng):
                run("pool", eng)

            @block.sync
            def _(eng):
                run("sp", eng)


class Rot:
    def __init__(self, items):
        self.items = items
        self.i = 0

    def next(self):
        it = self.items[self.i % len(self.items)]
        self.i += 1
        return it


class Prefetch:
    def __init__(self):
        self.tasks = []
        self.issued = 0

    def add(self, fn):
        self.tasks.append(fn)
        return len(self.tasks) - 1

    def need(self, idx, ahead):
        upto = min(len(self.tasks), idx + 1 + ahead)
        while self.issued < upto:
            self.tasks[self.issued]()
            self.issued += 1


class MK:
    def __init__(self, T, plan):
        self.T = T
        self.plan = plan
        self.nc = nc = bass.Bass("TRN2", target_bir_lowering=False)
        self.S = Sched(nc)
        self.st = contextlib.ExitStack()
        dt = nc.dram_tensor
        self.x_in = dt("xT", [NDC, 128, T], F32, kind="ExternalInput").ap()
        self.y_out = dt("yT", [NDC, 128, T], F32, kind="ExternalOutput").ap()
        self.scr = [dt("scr%d" % i, [NDC, 128, T], F32, kind="Internal").ap() for i in range(2)]
        self.lnp_d = dt("lnp", [128, DEPTH * 3 * 2 * NDC], F32, kind="ExternalInput").ap()
        self.w = {}
        need = set(p[0] for p in plan)
        self.nffn = 1 + max([p[1] * 2 + p[2] - 1 for p in plan if p[0] == "ffn"], default=-1)
        self.nfox = 1 + max([p[1] // 2 for p in plan if p[0] == "fox"], default=-1)
        self.ns5 = 1 + max([p[1] // 2 for p in plan if p[0] == "s5"], default=-1)
        if "ffn" in need:
            self.w["ffn_win"] = dt("ffn_win", [self.nffn, NFC, 128, 2 * 8 * 128], F32, kind="ExternalInput").ap()
            self.w["ffn_wout"] = dt("ffn_wout", [self.nffn, NDC, 128, NFC * 128], F32, kind="ExternalInput").ap()
        if "fox" in need:
            self.w["fox_wqkv"] = dt("fox_wqkv", [self.nfox, 8, 128, 3 * 8 * 128], F32, kind="ExternalInput").ap()
            self.w["fox_wf"] = dt("fox_wf", [self.nfox, 128, 8 * NH], F32, kind="ExternalInput").ap()
            self.w["fox_bf"] = dt("fox_bf", [self.nfox, NH, 1], F32, kind="ExternalInput").ap()
            self.w["fox_wo"] = dt("fox_wo", [self.nfox, 128, 8 * D], F32, kind="ExternalInput").ap()
            self.cs_d = dt("cs_d", [NH, 6, T], BF16, kind="Internal").ap()
            self.o_d = dt("o_d", [8, 128, T], BF16, kind="Internal").ap()
            self.r_cs_d = Res("cs_d")
            self.r_o_d = Res("o_d")
        if "s5" in need:
            self.w["s5_pl"] = dt("s5_pl", [self.ns5, 128, 96], F32, kind="ExternalInput").ap()
            self.w["s5_B"] = dt("s5_B", [self.ns5, 2, 128, 32 * 128], F32, kind="ExternalInput").ap()
            self.w["s5_C"] = dt("s5_C", [self.ns5, 2, 128, 32 * 128], F32, kind="ExternalInput").ap()
            self.w["s5_d"] = dt("s5_d", [self.ns5, 128, 8], F32, kind="ExternalInput").ap()
            self.w["s5_wo"] = dt("s5_wo", [self.ns5, 128, 8 * 2 * D], F32, kind="ExternalInput").ap()
            self.yg_d = dt("yg_d", [8, 128, T], BF16, kind="Internal").ap()
            self.r_yg_d = Res("yg_d")
        self.r_x_in = Res("x_in")
        self.r_y = Res("y_out")
        self.r_scr = [Res("scr0"), Res("scr1")]

    def sb(self, st, name, shape, dtype):
        return st.enter_context(self.nc.sbuf_tensor(name, shape, dtype))

    def build(self):
        nc, S = self.nc, self.S
        with contextlib.ExitStack() as st:
            self.ps = []
            for i in range(8):
                t = st.enter_context(nc.psum_tensor("ps%d" % i, [128, TB], F32))
                self.ps.append((t, Res("ps%d" % i)))
            self.lnp = self.sb(st, "lnp_sb", [128, DEPTH * 3 * 2 * NDC], F32)
            self.r_lnp = Res("lnp")
            S.dma("sp", lambda e: e.dma_start(out=self.lnp[:], in_=self.lnp_d), self.r_lnp)
            self.ones = self.sb(st, "ones_bf", [128, 128], BF16)
            self.r_ones = Res("ones")
            S.op("dve", lambda e: e.memset(self.ones[:], 1.0), writes=[self.r_ones])

            cur, rcur = self.x_in, self.r_x_in
            for pi, ph in enumerate(self.plan):
                last = pi == len(self.plan) - 1
                if last:
                    dst, rdst = self.y_out, self.r_y
                else:
                    dst, rdst = self.scr[pi % 2], self.r_scr[pi % 2]
                if ph[0] == "ffn":
                    self.ffn_phase(cur, rcur, dst, rdst, ph[1], ph[2])
                elif ph[0] == "fox":
                    self.fox_phase(cur, rcur, dst, rdst, ph[1])
                elif ph[0] == "s5":
                    self.s5_phase(cur, rcur, dst, rdst, ph[1])
                S.barrier()
                cur, rcur = dst, rdst
            S.emit(final_wait=[self.r_y])
        return nc

    def ln_setup(self, st, pfx):
        L = {}
        L["wb"] = self.sb(st, pfx + "wb", [128, NDC, TB], BF16)
        L["w2b"] = self.sb(st, pfx + "w2b", [128, NDC, TB], BF16)
        L["mean"] = self.sb(st, pfx + "mean", [128, TB], F32)
        L["msq"] = self.sb(st, pfx + "msq", [128, TB], F32)
        L["var"] = self.sb(st, pfx + "var", [128, TB], F32)
        L["rstd"] = self.sb(st, pfx + "rstd", [128, TB], F32)
        for k in ("mean2", "msq2", "var2", "rstd2"):
            L[k] = self.sb(st, pfx + k, [128, TB], F32)
        for k in ("wb", "w2b", "mean", "msq", "var", "rstd", "mean2", "msq2", "var2", "rstd2"):
            L["r_" + k] = Res(pfx + k)
        return L

    def ln_block(self, L, wv, r_w, lnidx, eps, dst, rdst, tok0, psS, psQ, k=0):
        for f in self.ln_pieces(L, wv, r_w, lnidx, eps, dst, rdst, tok0, psS, psQ, k):
            f()

    def ln_pieces(self, L, wv, r_w, lnidx, eps, dst, rdst, tok0, psS, psQ, k=0):
        S = self.S
        wb, w2b = L["wb"], L["w2b"]
        sfx = "" if k == 0 else "2"
        mean, msq, var, rstd = L["mean" + sfx], L["msq" + sfx], L["var" + sfx], L["rstd" + sfx]
        r_mean, r_msq, r_var, r_rstd = L["r_mean" + sfx], L["r_msq" + sfx], L["r_var" + sfx], L["r_rstd" + sfx]
        r_dc = [Res("wdc") for _ in range(NDC)]

        def conv():
            S.op("dve", lambda e: e.tensor_copy(wb[:], wv), reads=[r_w], writes=[L["r_wb"]])
            S.op("act", lambda e: e.activation(w2b[:], wv, AF.Square), reads=[r_w], writes=[L["r_w2b"]])

        def stats():
            (pS, rS), (pQ, rQ) = psS(), psQ()
            for dc in range(NDC):
                S.op("pe", lambda e, dc=dc: e.matmul(pS[:], lhsT=self.ones[:], rhs=wb[:, dc, :],
                                                      start=(dc == 0), stop=(dc == NDC - 1)),
                     reads=[L["r_wb"], self.r_ones], writes=[rS])
            for dc in range(NDC):
                S.op("pe", lambda e, dc=dc: e.matmul(pQ[:], lhsT=self.ones[:], rhs=w2b[:, dc, :],
                                                      start=(dc == 0), stop=(dc == NDC - 1)),
                     reads=[L["r_w2b"], self.r_ones], writes=[rQ])
            S.op("dve", lambda e: e.tensor_scalar(mean[:], pS[:], 1.0 / D, None, ALU.mult),
                 reads=[rS], writes=[r_mean])
            S.op("dve", lambda e: e.tensor_tensor(msq[:], mean[:], mean[:], ALU.mult),
                 reads=[r_mean], writes=[r_msq])
            S.op("dve", lambda e: e.tensor_scalar(var[:], pQ[:], 1.0 / D, eps, ALU.mult, ALU.add),
                 reads=[rQ], writes=[r_var])
            S.op("dve", lambda e: e.tensor_tensor(var[:], var[:], msq[:], ALU.subtract),
                 reads=[r_var, r_msq], writes=[r_var])
            S.op("act", lambda e: e.activation(rstd[:], var[:], AF.Ln), reads=[r_var], writes=[r_rstd])
            S.op("act", lambda e: e.activation(rstd[:], rstd[:], AF.Exp, scale=-0.5),
                 reads=[r_rstd], writes=[r_rstd])

        gofs = (lnidx * 2) * NDC
        bofs = (lnidx * 2 + 1) * NDC

        def norm(dc):
            v = wv[:, dc, :]
            S.op("dve", lambda e: e.tensor_tensor(v, v, mean[:], ALU.subtract),
                 reads=[r_w, r_mean], writes=[r_dc[dc]])
            S.op("dve", lambda e: e.tensor_tensor(v, v, rstd[:], ALU.mult),
                 reads=[r_dc[dc], r_rstd], writes=[r_dc[dc]])
            g = self.lnp[:, gofs + dc:gofs + dc + 1]
            b = self.lnp[:, bofs + dc:bofs + dc + 1]
            S.op("act", lambda e: e.activation(v, v, AF.Identity, bias=b, scale=g),
                 reads=[r_dc[dc], self.r_lnp], writes=[r_dc[dc]])
            if dc == NDC - 1:
                dv = dst[:, :, tok0:tok0 + TB].rearrange("c p t -> p c t")
                S.dma("sp", lambda e: e.dma_start(out=dv, in_=wv), rdst, reads=r_dc, writes=[r_w])
        return [conv, stats] + [(lambda dc=dc: norm(dc)) for dc in range(NDC)]

    def ffn_phase(self, src, rsrc, dst, rdst, layer, which):
        S, T = self.S, self.T
        widx = layer * 2 + (which - 1)
        lnidx = layer * 3 + (0 if which == 1 else 2)
        win = self.w["ffn_win"]
        wout = self.w["ffn_wout"]
        nsb = T // TS
        ntb = TS // TB
        with contextlib.ExitStack() as st:
            pfx = "f%d_" % widx
            xs = self.sb(st, pfx + "xs", [128, NDC, TS], F32)
            r_xs = [Res("xs%d" % i) for i in range(ntb)]
            xbf = [(self.sb(st, pfx + "xbf%d" % i, [128, NDC, TS], BF16), Res("xbf%d" % i)) for i in range(2)]
            aT = self.sb(st, pfx + "aT", [128, NFC, TS], BF16)
            r_aT = [[Res("aT") for _ in range(ntb)] for _ in range(NFC)]
            wgu = Rot([(self.sb(st, pfx + "wgu%d" % i, [128, 2, 8, 128], BF16), Res("wgu%d" % i)) for i in range(4)])
            wo = Rot([(self.sb(st, pfx + "wo%d" % i, [128, NFC, 128], BF16), Res("wo%d" % i)) for i in range(3)])
            sg = Rot([(self.sb(st, pfx + "sg%d" % i, [128, TB], BF16), Res("sg%d" % i)) for i in range(2)])
            L = self.ln_setup(st, pfx)
            psA = Rot(self.ps[0:6])
            psB = Rot(self.ps[6:8])

            pf = Prefetch()
            plan = []
            for sbk in range(nsb):
                info = {}
                t0 = sbk * TS
                xb, r_xb = xbf[sbk % 2]
                info["xbf"] = (xb, r_xb)

                def ld_xbf(xb=xb, r_xb=r_xb, t0=t0):
                    sv = src[:, :, t0:t0 + TS].rearrange("c p t -> p c t")
                    S.dma("pool", lambda e: e.dma_start(out=xb[:], in_=sv), r_xb, reads=[rsrc])
                info["t_xbf"] = pf.add(ld_xbf)
                info["wgu"] = []
                for fc in range(NFC):
                    buf, rb = wgu.next()

                    def ld_w(buf=buf, rb=rb, fc=fc):
                        S.dma("pool", lambda e: e.dma_start(
                            out=buf[:].rearrange("p a k m -> p (a k m)"), in_=win[widx, fc]), rb)
                    info["wgu"].append((pf.add(ld_w), buf, rb))
                    if fc == 1:
                        def ld_xs(t0=t0):
                            for tb in range(ntb):
                                sv = src[:, :, t0 + tb * TB:t0 + (tb + 1) * TB].rearrange("c p t -> p c t")
                                S.dma("sp", lambda e, sv=sv, tb=tb: e.dma_start(
                                    out=xs[:, :, tb * TB:(tb + 1) * TB], in_=sv), r_xs[tb], reads=[rsrc])
                        info["ld_xs"] = ld_xs
                info["wo"] = []
                for dc in range(NDC):
                    buf, rb = wo.next()

                    def ld_wo(buf=buf, rb=rb, dc=dc):
                        S.dma("pool", lambda e: e.dma_start(
                            out=buf[:].rearrange("p f m -> p (f m)"), in_=wout[widx, dc]), rb)
                    info["wo"].append((pf.add(ld_wo), buf, rb))
                plan.append(info)

            pending_ln = []
            for sbk in range(nsb):
                info = plan[sbk]
                t0 = sbk * TS
                xb, r_xb = info["xbf"]
                pf.need(info["t_xbf"], 2)
                for fc in range(NFC):
                    tid, wbuf, rwb = info["wgu"][fc]
                    pf.need(tid, 3)
                    if fc >= 1:
                        for _ in range(2):
                            if pending_ln:
                                pending_ln.pop(0)()
                    if fc == 12:
                        assert not pending_ln
                        info["ld_xs"]()
                    for tb in range(ntb):
                        ts_ = slice(tb * TB, (tb + 1) * TB)
                        pg, rpg = psA.next()
                        pu, rpu = psA.next()
                        for gu, (pp, rpp) in enumerate(((pg, rpg), (pu, rpu))):
                            for kc in range(8):
                                S.op("pe", lambda e, pp=pp, gu=gu, kc=kc, ts_=ts_, wbuf=wbuf, xb=xb: e.matmul(
                                    pp[:], lhsT=wbuf[:, gu, kc, :], rhs=xb[:, kc, ts_],
                                    start=(kc == 0), stop=(kc == 7)),
                                    reads=[rwb, r_xb], writes=[rpp])
                        sgb, rsg = sg.next()
                        S.op("act", lambda e, sgb=sgb, pg=pg: e.activation(sgb[:], pg[:], AF.Silu),
                             reads=[rpg], writes=[rsg])
                        S.op("dve", lambda e, sgb=sgb, pu=pu, fc=fc, ts_=ts_: e.tensor_tensor(
                            aT[:, fc, ts_], sgb[:], pu[:], ALU.mult),
                            reads=[rsg, rpu], writes=[r_aT[fc][tb]])
                for dc in range(NDC):
                    tid, wbuf, rwb = info["wo"][dc]
                    pf.need(tid, 2)
                    for tb in range(ntb):
                        ts_ = slice(tb * TB, (tb + 1) * TB)
                        py, rpy = psA.next()
                        for fc in range(NFC):
                            S.op("pe", lambda e, py=py, fc=fc, ts_=ts_, wbuf=wbuf: e.matmul(
                                py[:], lhsT=wbuf[:, fc, :], rhs=aT[:, fc, ts_],
                                start=(fc == 0), stop=(fc == NFC - 1)),
                                reads=[rwb, r_aT[fc][tb]], writes=[rpy])
                        xv = xs[:, dc, ts_]
                        S.op("dve", lambda e, xv=xv, py=py: e.scalar_tensor_tensor(
                            xv, xv, 2.0 * ALPHA, py[:], ALU.mult, ALU.add),
                            reads=[rpy, r_xs[tb]], writes=[r_xs[tb]])
                while pending_ln:
                    pending_ln.pop(0)()
                for tb in range(ntb):
                    ts_ = slice(tb * TB, (tb + 1) * TB)
                    pcs = self.ln_pieces(L, xs[:, :, ts_], r_xs[tb], lnidx, 4.0 * LN_EPS, dst, rdst,
                                         t0 + tb * TB, psB.next, psB.next, k=tb % 2)
                    if tb == 0:
                        first = pcs
                    else:
                        pending_ln.extend([first[0], first[1], pcs[0], first[2], pcs[1]] + first[3:] + pcs[2:])
                if sbk + 1 < nsb:
                    pf.need(plan[sbk + 1]["t_xbf"], 2)
            while pending_ln:
                pending_ln.pop(0)()


    def fox_phase(self, src, rsrc, dst, rdst, layer):
        S, T, nc = self.S, self.T, self.nc
        li = layer // 2
        lnidx = layer * 3 + 1
        NTB = T // TB
        NKB = T // 128
        wqkv, wf_d, bf_d, wo_d = self.w["fox_wqkv"], self.w["fox_wf"], self.w["fox_bf"], self.w["fox_wo"]
        cs_d, o_d, r_cs_d, r_o_d = self.cs_d, self.o_d, self.r_cs_d, self.r_o_d
        pfx = "x%d_" % layer

        stx = contextlib.ExitStack()
        xb_all = self.sb(stx, pfx + "xball", [128, 8, T], BF16)
        r_xall = [Res("xall") for _ in range(NTB)]
        for tb in range(NTB):
            sv = src[:, :, tb * TB:(tb + 1) * TB].rearrange("c p t -> p c t")
            S.dma("pool", lambda e, sv=sv, tb=tb: e.dma_start(out=xb_all[:, :, tb * TB:(tb + 1) * TB], in_=sv),
                  r_xall[tb], reads=[rsrc])

        with contextlib.ExitStack() as st:
            wf = self.sb(st, pfx + "wf", [128, 8, NH], BF16)
            r_wf = Res("wf")
            bf = self.sb(st, pfx + "bf", [NH, 1], F32)
            r_bf = Res("bf")
            lsp = self.sb(st, pfx + "lsp", [NH, T], F32)
            r_lsp = Res("lsp")
            ncm = self.sb(st, pfx + "ncm", [NH, T], F32)
            r_ncm = Res("ncm")
            one = self.sb(st, pfx + "one", [NH, T], F32)
            r_one = Res("one")
            r1 = self.sb(st, pfx + "r1", [NH, T], F32)
            r_r1 = Res("r1")
            csb = self.sb(st, pfx + "csb", [NH, 6, T], BF16)
            r_csb = Res("csb")
            etmp = Rot([(self.sb(st, pfx + "etmp%d" % i, [NH, TB], F32), Res("etmp")) for i in range(2)])
            S.dma("pool", lambda e: e.dma_start(out=wf[:].rearrange("p k h -> p (k h)"), in_=wf_d[li]), r_wf)
            S.dma("sp", lambda e: e.dma_start(out=bf[:], in_=bf_d[li]), r_bf)
            S.op("dve", lambda e: e.tensor_scalar(bf[:], bf[:], -1.0, None, ALU.mult), reads=[r_bf], writes=[r_bf])
            S.op("dve", lambda e: e.memset(one[:], 1.0), writes=[r_one])
            psr = Rot(self.ps[0:2])
            for tb in range(NTB):
                xb, r_xb = xb_all[:, :, tb * TB:(tb + 1) * TB], r_xall[tb]
                pp, rpp = psr.next()
                for kc in range(8):
                    S.op("pe", lambda e, pp=pp, kc=kc, xb=xb: e.matmul(
                        pp[0:NH, :], lhsT=wf[:, kc, :], rhs=xb[:, kc, :], start=(kc == 0), stop=(kc == 7)),
                        reads=[r_wf, r_xb], writes=[rpp])
                et, ret = etmp.next()
                S.op("act", lambda e, et=et, pp=pp: e.activation(et[:], pp[0:NH, :], AF.Exp, bias=bf[:, 0:1], scale=-1.0),
                     reads=[rpp, r_bf], writes=[ret])
                S.op("act", lambda e, et=et, tb=tb: e.activation(lsp[:, tb * TB:(tb + 1) * TB], et[:], AF.Ln, bias=1.0),
                     reads=[ret], writes=[r_lsp])
            S.op("dve", lambda e: e.tensor_tensor_scan(ncm[:], one[:], lsp[:], 0.0, ALU.mult, ALU.add),
                 reads=[r_one, r_lsp], writes=[r_ncm])
            S.op("dve", lambda e: e.tensor_copy(csb[:, 3, :], ncm[:]), reads=[r_ncm], writes=[r_csb])
            S.op("dve", lambda e: e.tensor_tensor(r1[:], ncm[:], csb[:, 3, :], ALU.subtract),
                 reads=[r_ncm, r_csb], writes=[r_r1])
            S.op("dve", lambda e: e.tensor_copy(csb[:, 4, :], r1[:]), reads=[r_r1], writes=[r_csb])
            S.op("dve", lambda e: e.tensor_tensor(r1[:], r1[:], csb[:, 4, :], ALU.subtract),
                 reads=[r_r1, r_csb], writes=[r_r1])
            S.op("dve", lambda e: e.tensor_copy(csb[:, 5, :], r1[:]), reads=[r_r1], writes=[r_csb])
            S.op("dve", lambda e: e.tensor_scalar(csb[:, 0:3, :], csb[:, 3:6, :], -1.0, None, ALU.mult),
                 reads=[r_csb], writes=[r_csb])
            S.dma("sp", lambda e: e.dma_start(out=cs_d, in_=csb[:]), r_cs_d, reads=[r_csb])
        S.barrier()

        with contextlib.ExitStack() as st:
            wq = Rot([(self.sb(st, pfx + "wq%d" % i, [128, 3, 8, 128], BF16), Res("wq")) for i in range(2)])
            QK = {}
            for nm in ("QA", "QB", "KA", "KB"):
                QK[nm] = (self.sb(st, pfx + nm, [128, T], BF16), Res(nm))
            VA = self.sb(st, pfx + "VA", [128, NKB, 128], BF16)
            VB = self.sb(st, pfx + "VB", [128, NKB, 128], BF16)
            r_VA, r_VB = Res("VA"), Res("VB")
            Eb = Rot([(self.sb(st, pfx + "E%d" % i, [128, TB], BF16), Res("E")) for i in range(4)])
            oT = Rot([(self.sb(st, pfx + "oT%d" % i, [128, T], BF16), Res("oT")) for i in range(2)])
            rc = Rot([(self.sb(st, pfx + "rc%d" % i, [128, TB], F32), Res("rc")) for i in range(2)])
            tri = self.sb(st, pfx + "tri", [128, 128], BF16)
            r_tri = Res("tri")
            S.op("pool", lambda e: e.memset(tri[:], 1.0), writes=[r_tri])
            S.op("pool", lambda e: e.affine_select(tri[:], tri[:], [[1, 128]], ALU.is_ge, 0.0, base=0,
                                                   channel_multiplier=-1), reads=[r_tri], writes=[r_tri])
            S.op("dve", lambda e: e.memset(VA[:], 1.0), writes=[r_VA])
            S.op("dve", lambda e: e.memset(VB[:], 1.0), writes=[r_VB])
            for nm in ("QA", "QB", "KA", "KB"):
                t_, r_ = QK[nm]
                S.op("dve", lambda e, t_=t_: e.memset(t_[64:70, :], 1.0), writes=[r_])
            psS = Rot(self.ps[0:4])
            psO = Rot(self.ps[4:6])
            psP = Rot(self.ps[6:8])

            import os
            dbg = os.environ.get("FOXDBG", "")
            for j in range(0 if dbg == "noA3" else 8):
                wqb, r_wq = wq.next()
                S.dma("pool", lambda e, wqb=wqb, j=j: e.dma_start(
                    out=wqb[:].rearrange("p s k m -> p (s k m)"), in_=wqkv[li, j]), r_wq)
                for hh, (qn, kn) in enumerate((("QA", "KA"), ("QB", "KB"))):
                    h = 2 * j + hh
                    qt, rq = QK[qn]
                    kt, rk = QK[kn]
                    S.dma("sp", lambda e, qt=qt, h=h: e.dma_start(out=qt[64:67, :], in_=cs_d[h, 0:3, :]),
                          rq, reads=[r_cs_d])
                    S.dma("sp", lambda e, kt=kt, h=h: e.dma_start(out=kt[67:70, :], in_=cs_d[h, 3:6, :]),
                          rk, reads=[r_cs_d])
                for tb in range(NTB):
                    xb, r_xb = xb_all[:, :, tb * TB:(tb + 1) * TB], r_xall[tb]
                    tsl = slice(tb * TB, (tb + 1) * TB)
                    for s_, (na, nb_) in enumerate((("QA", "QB"), ("KA", "KB"))):
                        pp, rpp = psP.next()
                        for kc in range(8):
                            S.op("pe", lambda e, pp=pp, kc=kc, xb=xb, s_=s_, wqb=wqb: e.matmul(
                                pp[:], lhsT=wqb[:, s_, kc, :], rhs=xb[:, kc, :], start=(kc == 0), stop=(kc == 7)),
                                reads=[r_wq, r_xb], writes=[rpp])
                        ta, ra = QK[na]
                        tb_, rb2 = QK[nb_]
                        sc = 0.125 if s_ == 0 else 1.0
                        S.op("dve", lambda e, ta=ta, pp=pp, sc=sc, tsl=tsl: e.tensor_scalar(
                            ta[0:64, tsl], pp[0:64, :], sc, None, ALU.mult), reads=[rpp], writes=[ra])
                        S.op("dve", lambda e, tb_=tb_, pp=pp, sc=sc, tsl=tsl: e.tensor_scalar(
                            tb_[0:64, tsl], pp[64:128, :], sc, None, ALU.mult), reads=[rpp], writes=[rb2])
                    pp, rpp = psP.next()
                    for i4 in range(4):
                        for kc in range(8):
                            S.op("pe", lambda e, pp=pp, kc=kc, xb=xb, i4=i4, wqb=wqb: e.matmul(
                                pp[:, i4 * 128:(i4 + 1) * 128], lhsT=xb[:, kc, i4 * 128:(i4 + 1) * 128],
                                rhs=wqb[:, 2, kc, :], start=(kc == 0), stop=(kc == 7)),
                                reads=[r_wq, r_xb], writes=[rpp])
                    pv = pp[:].rearrange("p (a m) -> p a m", a=4)
                    S.op("dve", lambda e, pv=pv, tb=tb: e.tensor_copy(VA[:, tb * 4:(tb + 1) * 4, 0:64], pv[:, :, 0:64]),
                         reads=[rpp], writes=[r_VA])
                    S.op("dve", lambda e, pv=pv, tb=tb: e.tensor_copy(VB[:, tb * 4:(tb + 1) * 4, 64:128], pv[:, :, 64:128]),
                         reads=[rpp], writes=[r_VB])
                oTb, r_oT = oT.next()
                steps = []
                for hh in range(2):
                    for qb in range(NTB):
                        nkb = 4 * (qb + 1)
                        for kb in range(nkb):
                            steps.append((hh, qb, kb, nkb))
                state = {}

                def emit_S(i):
                    hh, qb, kb, nkb = steps[i]
                    qt, rq = QK["QA" if hh == 0 else "QB"]
                    kt, rk = QK["KA" if hh == 0 else "KB"]
                    r = kb - 4 * qb
                    c0 = 128 * r if r > 0 else 0
                    ps_, rps = psS.next()
                    S.op("pe", lambda e: e.matmul(ps_[:, c0:TB], lhsT=kt[0:70, kb * 128:(kb + 1) * 128],
                                                  rhs=qt[0:70, qb * TB + c0:(qb + 1) * TB], start=True, stop=True),
                         reads=[rq, rk], writes=[rps])
                    state[i] = (ps_, rps, c0, r)

                def emit_rest(i, oTb=oTb, r_oT=r_oT):
                    hh, qb, kb, nkb = steps[i]
                    ps_, rps, c0, r = state.pop(i)
                    Et, rE = Eb.next()
                    S.op("act", lambda e: e.activation(Et[:, c0:TB], ps_[:, c0:TB], AF.Exp),
                         reads=[rps], writes=[rE])
                    if r >= 0:
                        S.op("dve", lambda e: e.tensor_tensor(Et[:, c0:c0 + 128], Et[:, c0:c0 + 128], tri[:], ALU.mult),
                             reads=[rE, r_tri], writes=[rE])
                    if kb == 0:
                        state["o", hh, qb] = psO.next()
                    po, rpo = state["o", hh, qb]
                    Vt, rV = (VA, r_VA) if hh == 0 else (VB, r_VB)
                    S.op("pe", lambda e: e.matmul(po[:, c0:TB], lhsT=Vt[:, kb, :], rhs=Et[:, c0:TB],
                                                  start=(kb == 0), stop=(kb == nkb - 1), skip_group_check=True),
                         reads=[rV, rE], writes=[rpo])
                    if kb == nkb - 1:
                        rct, rrc = rc.next()
                        orow = slice(0, 64) if hh == 0 else slice(64, 128)
                        drow = slice(64, 128) if hh == 0 else slice(0, 64)
                        S.op("dve", lambda e: e.reciprocal(rct[drow, :], po[drow, :]), reads=[rpo], writes=[rrc])
                        S.op("dve", lambda e: e.tensor_tensor(oTb[orow, qb * TB:(qb + 1) * TB], po[orow, :], rct[drow, :], ALU.mult),
                             reads=[rpo, rrc], writes=[r_oT])
                        del state["o", hh, qb]

                LOOK = 2
                if dbg == "noattn":
                    steps = []
                n = len(steps)
                for i in range(min(LOOK, n)):
                    emit_S(i)
                for i in range(n):
                    if i + LOOK < n:
                        emit_S(i + LOOK)
                    emit_rest(i)
                S.dma("sp", lambda e, oTb=oTb, j=j: e.dma_start(out=o_d[j], in_=oTb[:]), r_o_d, reads=[r_oT])
        S.barrier()

        with contextlib.ExitStack() as st:
            wo = self.sb(st, pfx + "wo", [128, 8, D], BF16)
            r_wo = Res("wo")
            S.dma("pool", lambda e: e.dma_start(out=wo[:].rearrange("p j n -> p (j n)"), in_=wo_d[li]), r_wo)
            obr = Rot([(self.sb(st, pfx + "ob%d" % i, [128, 8, TB], BF16), Res("ob")) for i in range(3)])
            xsr = Rot([(self.sb(st, pfx + "xs%d" % i, [128, 8, TB], F32), Res("xs")) for i in range(3)])
            L = self.ln_setup(st, pfx)
            psA = Rot(self.ps[0:6])
            psB = Rot(self.ps[6:8])

            def ld(tb):
                ob, r_ob = obr.next()
                xs, r_xs = xsr.next()
                ov = o_d[:, :, tb * TB:(tb + 1) * TB].rearrange("c p t -> p c t")
                sv = src[:, :, tb * TB:(tb + 1) * TB].rearrange("c p t -> p c t")
                S.dma("sp", lambda e: e.dma_start(out=ob[:], in_=ov), r_ob, reads=[r_o_d])
                S.dma("sp", lambda e: e.dma_start(out=xs[:], in_=sv), r_xs, reads=[rsrc])
                return ob, r_ob, xs, r_xs
            nxt = ld(0)
            pend = []
            for tb in range(NTB):
                ob, r_ob, xs, r_xs = nxt
                if tb + 1 < NTB:
                    nxt = ld(tb + 1)
                for dc in range(NDC):
                    pp, rpp = psA.next()
                    for j in range(8):
                        S.op("pe", lambda e, pp=pp, j=j, dc=dc, ob=ob: e.matmul(
                            pp[:], lhsT=wo[:, j, dc * 128:(dc + 1) * 128], rhs=ob[:, j, :],
                            start=(j == 0), stop=(j == 7)), reads=[r_wo, r_ob], writes=[rpp])
                    xv = xs[:, dc, :]
                    S.op("dve", lambda e, xv=xv, pp=pp: e.scalar_tensor_tensor(
                        xv, xv, ALPHA, pp[:], ALU.mult, ALU.add), reads=[rpp, r_xs], writes=[r_xs])
                    for _ in range(2):
                        if pend:
                            pend.pop(0)()
                while pend:
                    pend.pop(0)()
                pend = self.ln_pieces(L, xs[:], r_xs, lnidx, LN_EPS, dst, rdst, tb * TB, psB.next, psB.next, k=tb % 2)
            while pend:
                pend.pop(0)()
        stx.close()

    def rr_sin(self, out, ph, nf, ni, shift, r_out, r_ph, r_nf, r_ni, extra=None):
        if extra is not None:
            self.rr_sin(extra[0], extra[1], extra[2], extra[3], shift, r_out, r_ph, r_nf, r_ni)
        S = self.S
        TWO_PI = 2.0 * math.pi
        if shift != 0.0:
            S.op("dve", lambda e: e.tensor_scalar(ph, ph, shift, None, ALU.add), reads=[r_ph], writes=[r_ph])
        S.op("dve", lambda e: e.tensor_scalar(nf, ph, 1.0 / TWO_PI, None, ALU.mult), reads=[r_ph], writes=[r_nf])
        S.op("dve", lambda e: e.tensor_copy(ni, nf), reads=[r_nf], writes=[r_ni])
        S.op("dve", lambda e: e.tensor_copy(nf, ni), reads=[r_ni], writes=[r_nf])
        S.op("dve", lambda e: e.scalar_tensor_tensor(ph, nf, -TWO_PI, ph, ALU.mult, ALU.add),
             reads=[r_nf, r_ph], writes=[r_ph])
        S.op("dve", lambda e: e.tensor_scalar(nf, ph, math.pi, TWO_PI, ALU.is_gt, ALU.mult),
             reads=[r_ph], writes=[r_nf])
        S.op("dve", lambda e: e.tensor_tensor(ph, ph, nf, ALU.subtract), reads=[r_nf, r_ph], writes=[r_ph])
        PS = 3.141592
        S.op("dve", lambda e: e.tensor_scalar(ph, ph, PS, -PS, ALU.min, ALU.max), reads=[r_ph], writes=[r_ph])
        S.op("act", lambda e: e.activation(out, ph, AF.Sin), reads=[r_ph], writes=[r_out])

    def s5_phase(self, src, rsrc, dst, rdst, layer):
        S, T, nc = self.S, self.T, self.nc
        li = layer // 2
        lnidx = layer * 3 + 1
        LC = 256
        NCH = T // LC
        NTB = T // TB
        I32 = mybir.dt.int32
        pl_d, B_d, C_d, dd_d, wo_d = (self.w["s5_pl"], self.w["s5_B"], self.w["s5_C"],
                                      self.w["s5_d"], self.w["s5_wo"])
        yg_d, r_yg_d = self.yg_d, self.r_yg_d
        pfx = "s%d_" % layer
        with contextlib.ExitStack() as st0:
            tc = self.sb(st0, pfx + "tc", [128, 32, LC], BF16)
            ts = self.sb(st0, pfx + "ts", [128, 32, LC], BF16)
            r_tc, r_ts = Res("tc"), Res("ts")
            rotc = self.sb(st0, pfx + "rotc", [128, 32], F32)
            rots = self.sb(st0, pfx + "rots", [128, 32], F32)
            Bre = self.sb(st0, pfx + "Bre", [128, 32, 128], BF16)
            Bim = self.sb(st0, pfx + "Bim", [128, 32, 128], BF16)
            r_B = Res("B")
            Cpr = self.sb(st0, pfx + "Cpr", [128, 32, 128], BF16)
            Cpi = self.sb(st0, pfx + "Cpi", [128, 32, 128], BF16)
            Cprn = self.sb(st0, pfx + "Cprn", [128, 32, 128], BF16)
            r_Cp = Res("Cp")
            rpl = self.sb(st0, pfx + "rpl", [128, 32], F32)
            r_rpl = Res("rpl")
            dsb = self.sb(st0, pfx + "dsb", [128, 8], F32)
            r_dsb = Res("dsb")
            ini_re = self.sb(st0, pfx + "inire", [128, 32], F32)
            ini_im = self.sb(st0, pfx + "iniim", [128, 32], F32)
            r_ini = Res("ini")
            S.dma("pool", lambda e: e.dma_start(out=Bre[:].rearrange("p g m -> p (g m)"), in_=B_d[li, 0]), r_B)
            S.dma("pool", lambda e: e.dma_start(out=Bim[:].rearrange("p g m -> p (g m)"), in_=B_d[li, 1]), r_B)
            S.dma("sp", lambda e: e.dma_start(out=dsb[:], in_=dd_d[li]), r_dsb)
            S.op("dve", lambda e: e.memset(ini_re[:], 0.0), writes=[r_ini])
            S.op("dve", lambda e: e.memset(ini_im[:], 0.0), writes=[r_ini])

            with contextlib.ExitStack() as st:
                def small(name, dt_=F32, n=32):
                    return self.sb(st, pfx + name, [128, n], dt_), Res(name)
                pl, r_pl = small("pl", F32, 96)
                S.dma("sp", lambda e: e.dma_start(out=pl[:], in_=pl_d[li]), r_pl)
                are, aim, ldt = pl[:, 0:32], pl[:, 32:64], pl[:, 64:96]
                dtt, r_dtt = small("dtt")
                th, r_th = small("th")
                cs, r_cs = small("cs")
                sn, r_sn = small("sn")
                ph, r_ph = small("ph")
                nf, r_nf = small("nf")
                ni, r_ni = small("ni", I32)
                t1, r_t1 = small("t1")
                t2, r_t2 = small("t2")
                den, r_den = small("den")
                zr, r_zr = small("zr")
                zi, r_zi = small("zi")
                nzr, r_nzr = small("nzr")
                S.op("act", lambda e: e.activation(dtt[:], ldt, AF.Exp), reads=[r_pl], writes=[r_dtt])
                S.op("dve", lambda e: e.tensor_tensor(th[:], aim, dtt[:], ALU.mult), reads=[r_pl, r_dtt], writes=[r_th])
                S.op("dve", lambda e: e.tensor_tensor(t1[:], are, dtt[:], ALU.mult), reads=[r_pl, r_dtt], writes=[r_t1])
                S.op("act", lambda e: e.activation(rpl[:], t1[:], AF.Exp), reads=[r_t1], writes=[r_rpl])
                S.op("dve", lambda e: e.tensor_copy(ph[:], th[:]), reads=[r_th], writes=[r_ph])
                self.rr_sin(sn[:], ph[:], nf[:], ni[:], 0.0, r_sn, r_ph, r_nf, r_ni)
                S.op("dve", lambda e: e.tensor_copy(ph[:], th[:]), reads=[r_th, r_sn], writes=[r_ph])
                self.rr_sin(cs[:], ph[:], nf[:], ni[:], math.pi / 2, r_cs, r_ph, r_nf, r_ni)
                S.op("dve", lambda e: e.tensor_tensor(cs[:], cs[:], rpl[:], ALU.mult), reads=[r_cs, r_rpl], writes=[r_cs])
                S.op("dve", lambda e: e.tensor_scalar(cs[:], cs[:], -1.0, None, ALU.add), reads=[r_cs], writes=[r_cs])
                S.op("dve", lambda e: e.tensor_tensor(sn[:], sn[:], rpl[:], ALU.mult), reads=[r_sn, r_rpl], writes=[r_sn])
                S.op("dve", lambda e: e.tensor_tensor(den[:], are, are, ALU.mult), reads=[r_pl], writes=[r_den])
                S.op("dve", lambda e: e.tensor_tensor(t1[:], aim, aim, ALU.mult), reads=[r_pl, r_rpl], writes=[r_t1])
                S.op("dve", lambda e: e.tensor_tensor(den[:], den[:], t1[:], ALU.add), reads=[r_den, r_t1], writes=[r_den])
                S.op("dve", lambda e: e.reciprocal(den[:], den[:]), reads=[r_den], writes=[r_den])
                S.op("dve", lambda e: e.tensor_tensor(t1[:], cs[:], are, ALU.mult), reads=[r_cs, r_pl, r_den], writes=[r_t1])
                S.op("dve", lambda e: e.tensor_tensor(t2[:], sn[:], aim, ALU.mult), reads=[r_sn, r_pl], writes=[r_t2])
                S.op("dve", lambda e: e.tensor_tensor(t1[:], t1[:], t2[:], ALU.add), reads=[r_t1, r_t2], writes=[r_t1])
                S.op("dve", lambda e: e.tensor_tensor(zr[:], t1[:], den[:], ALU.mult), reads=[r_t1, r_den], writes=[r_zr])
                S.op("dve", lambda e: e.tensor_scalar(nzr[:], zr[:], -1.0, None, ALU.mult), reads=[r_zr], writes=[r_nzr])
                S.op("dve", lambda e: e.tensor_tensor(t1[:], sn[:], are, ALU.mult), reads=[r_sn, r_pl, r_zr], writes=[r_t1])
                S.op("dve", lambda e: e.tensor_tensor(t2[:], cs[:], aim, ALU.mult), reads=[r_cs, r_pl, r_t1], writes=[r_t2])
                S.op("dve", lambda e: e.tensor_tensor(t1[:], t1[:], t2[:], ALU.subtract), reads=[r_t1, r_t2], writes=[r_t1])
                S.op("dve", lambda e: e.tensor_tensor(zi[:], t1[:], den[:], ALU.mult), reads=[r_t1, r_den], writes=[r_zi])
                Cre = self.sb(st, pfx + "Cre", [128, 32, 128], F32)
                Cim = self.sb(st, pfx + "Cim", [128, 32, 128], F32)
                r_C = Res("C")
                S.dma("sp", lambda e: e.dma_start(out=Cre[:].rearrange("p g m -> p (g m)"), in_=C_d[li, 0]), r_C)
                S.dma("sp", lambda e: e.dma_start(out=Cim[:].rearrange("p g m -> p (g m)"), in_=C_d[li, 1]), r_C)
                tq = [(self.sb(st, pfx + "tq%d" % i, [128, 128], F32), Res("tq")) for i in range(2)]
                for gp in range(32):
                    eng = "dve"
                    tt, r_tt = tq[gp % 2]
                    S.op(eng, lambda e, gp=gp, tt=tt: e.tensor_scalar(tt[:], Cim[:, gp, :], zi[:, gp:gp + 1], None, ALU.mult),
                         reads=[r_C, r_zi], writes=[r_tt])
                    S.op(eng, lambda e, gp=gp, tt=tt: e.scalar_tensor_tensor(
                        Cpr[:, gp, :], Cre[:, gp, :], zr[:, gp:gp + 1], tt[:], ALU.mult, ALU.subtract),
                        reads=[r_C, r_zr, r_tt], writes=[r_Cp])
                    S.op(eng, lambda e, gp=gp, tt=tt: e.tensor_scalar(tt[:], Cre[:, gp, :], zi[:, gp:gp + 1], None, ALU.mult),
                         reads=[r_C, r_zi, r_Cp], writes=[r_tt])
                    S.op(eng, lambda e, gp=gp, tt=tt: e.scalar_tensor_tensor(
                        Cpi[:, gp, :], Cim[:, gp, :], nzr[:, gp:gp + 1], tt[:], ALU.mult, ALU.subtract),
                        reads=[r_C, r_nzr, r_tt], writes=[r_Cp])
                S.op("dve", lambda e: e.tensor_scalar(Cprn[:], Cpr[:], -1.0, None, ALU.mult), reads=[r_Cp], writes=[r_Cp])
                ii = self.sb(st, pfx + "ii", [128, LC + 1], I32)
                io = self.sb(st, pfx + "io", [128, LC + 1], F32)
                r_io = Res("io")
                S.op("pool", lambda e: e.iota(ii[:], [[1, LC + 1]], base=0, channel_multiplier=0), writes=[r_io])
                S.op("pool", lambda e: e.tensor_copy(io[:], ii[:]), reads=[r_io], writes=[r_io])
                GB = 8
                phb = self.sb(st, pfx + "phb", [128, GB, LC + 1], F32)
                nfb = self.sb(st, pfx + "nfb", [128, GB, LC + 1], F32)
                nib = self.sb(st, pfx + "nib", [128, GB, LC + 1], I32)
                r_phb, r_nfb, r_nib = Res("phb"), Res("nfb"), Res("nib")
                for g0 in range(0, 32, GB):
                    for tab, rot, r_tab, shift in ((ts, rots, r_ts, 0.0), (tc, rotc, r_tc, math.pi / 2)):
                        for k in range(GB):
                            S.op("dve", lambda e, k=k, g0=g0: e.tensor_scalar(
                                phb[:, k, :], io[:], th[:, g0 + k:g0 + k + 1], None, ALU.mult),
                                reads=[r_io, r_th, r_tab], writes=[r_phb])
                        self.rr_sin(tab[:, g0:g0 + GB, :], phb[:, :, 0:LC], nfb[:, :, 0:LC], nib[:, :, 0:LC], shift,
                                    r_tab, r_phb, r_nfb, r_nib,
                                    extra=(rot[:, g0:g0 + GB], phb[:, :, LC], nfb[:, :, LC], nib[:, :, LC]))
            S.barrier()

            with contextlib.ExitStack() as st:
                ubr = Rot([(self.sb(st, pfx + "ub%d" % i, [128, T], BF16), Res("ub")) for i in range(2)])
                u32r = Rot([(self.sb(st, pfx + "u32%d" % i, [128, T], F32), Res("u32")) for i in range(2)])
                ygr = Rot([(self.sb(st, pfx + "yg%d" % i, [128, T], BF16), Res("yg")) for i in range(2)])
                Wk = Rot([[(self.sb(st, pfx + "w%d_%d" % (i, k), [128, 2, LC], F32), Res("w")) for k in range(4)]
                          for i in range(4)])
                Wb = Rot([[(self.sb(st, pfx + "wb%d_%d" % (i, k), [128, 2, LC], BF16), Res("wb")) for k in range(8)]
                          for i in range(4)])
                yvr = Rot([(self.sb(st, pfx + "yv%d" % i, [128, LC], F32), Res("yv")) for i in range(2)])
                cc = self.sb(st, pfx + "cc", [128, 8], F32)
                r_cc = Res("cc")
                psBU = Rot(self.ps[0:6])
                psY = Rot(self.ps[6:8])

                def ld_tile(jt):
                    ub, r_ub = ubr.next()
                    u32, r_u32 = u32r.next()
                    S.dma("pool", lambda e: e.dma_start(out=ub[:], in_=src[jt]), r_ub, reads=[rsrc])
                    S.dma("sp", lambda e: e.dma_start(out=u32[:], in_=src[jt]), r_u32, reads=[rsrc])
                    return ub, r_ub, u32, r_u32

                tiles = {}

                def get_tile(jt):
                    if jt not in tiles and jt < 8:
                        tiles[jt] = ld_tile(jt) + ygr.next()
                    return tiles.get(jt)

                def make_unit(jt, c, q):
                    U = {}
                    gp0 = 4 * jt + 2 * q
                    csl = slice(c * LC, (c + 1) * LC)
                    tcv = tc[:, gp0:gp0 + 2, :]
                    tsv = ts[:, gp0:gp0 + 2, :]

                    def f0():
                        ub, r_ub, u32, r_u32, yg, r_yg = get_tile(jt)
                        if c == min(2, NCH - 1) and q == 0:
                            get_tile(jt + 1)
                        U["k"] = Wk.next()
                        U["b"] = Wb.next()
                        (pre, rpre), (pim, rpim) = psBU.next(), psBU.next()
                        for g_ in range(2):
                            gp = gp0 + g_
                            S.op("pe", lambda e, g_=g_, gp=gp: e.matmul(pre[:, g_ * LC:(g_ + 1) * LC], lhsT=Bre[:, gp, :],
                                                                       rhs=ub[:, csl], start=True, stop=True),
                                 reads=[r_B, r_ub], writes=[rpre])
                            S.op("pe", lambda e, g_=g_, gp=gp: e.matmul(pim[:, g_ * LC:(g_ + 1) * LC], lhsT=Bim[:, gp, :],
                                                                       rhs=ub[:, csl], start=True, stop=True),
                                 reads=[r_B, r_ub], writes=[rpim])
                        prev = pre[:].rearrange("p (a t) -> p a t", a=2)
                        pimv = pim[:].rearrange("p (a t) -> p a t", a=2)
                        (breb, r_breb), (bimb, r_bimb) = U["b"][0], U["b"][1]
                        S.op("act", lambda e: e.activation(breb[:], prev, AF.Copy), reads=[rpre], writes=[r_breb])
                        S.op("act", lambda e: e.activation(bimb[:], pimv, AF.Copy), reads=[rpim], writes=[r_bimb])

                    def f1():
                        (G1, rG1), (G2, rG2), _, _ = U["k"]
                        ((breb, r_breb), (bimb, r_bimb), (Ab, rAb), (Bb, rBb), (Cb, rCb), (Db, rDb), _, _) = U["b"]
                        S.op("dve", lambda e: e.tensor_tensor(Ab[:], breb[:], tcv, ALU.mult), reads=[r_breb, r_tc], writes=[rAb])
                        S.op("dve", lambda e: e.tensor_tensor(Bb[:], bimb[:], tsv, ALU.mult), reads=[r_bimb, r_ts], writes=[rBb])
                        S.op("dve", lambda e: e.tensor_tensor(Cb[:], bimb[:], tcv, ALU.mult), reads=[r_bimb, r_tc], writes=[rCb])
                        S.op("dve", lambda e: e.tensor_tensor(Db[:], breb[:], tsv, ALU.mult), reads=[r_breb, r_ts], writes=[rDb])
                        S.op("pool", lambda e: e.tensor_tensor(G1[:], Ab[:], Bb[:], ALU.add), reads=[rAb, rBb], writes=[rG1])
                        S.op("pool", lambda e: e.tensor_tensor(G2[:], Cb[:], Db[:], ALU.subtract), reads=[rCb, rDb], writes=[rG2])

                    def f2():
                        (G1, rG1), (G2, rG2), (S1, rS1), (S2, rS2) = U["k"]
                        (s1b, r_s1b), (s2b, r_s2b) = U["b"][6], U["b"][7]
                        for g_ in range(2):
                            gp = gp0 + g_
                            rb = rpl[:, gp:gp + 1].to_broadcast([128, LC])
                            S.op("dve", lambda e, g_=g_, gp=gp, rb=rb: e.tensor_tensor_scan(
                                S1[:, g_, :], rb, G1[:, g_, :], ini_re[:, gp:gp + 1], ALU.mult, ALU.add),
                                reads=[rG1, r_rpl, r_ini], writes=[rS1])
                            S.op("dve", lambda e, g_=g_, gp=gp, rb=rb: e.tensor_tensor_scan(
                                S2[:, g_, :], rb, G2[:, g_, :], ini_im[:, gp:gp + 1], ALU.mult, ALU.add),
                                reads=[rG2, r_rpl, r_ini], writes=[rS2])
                        if c + 1 < NCH:
                            glr = S1[:, :, LC - 1]
                            gli = S2[:, :, LC - 1]
                            rc_ = rotc[:, gp0:gp0 + 2]
                            rs_ = rots[:, gp0:gp0 + 2]
                            S.op("pool", lambda e: e.tensor_tensor(cc[:, 0:2], glr, rc_, ALU.mult), reads=[rS1, r_tc], writes=[r_cc])
                            S.op("pool", lambda e: e.tensor_tensor(cc[:, 2:4], gli, rs_, ALU.mult), reads=[rS2, r_ts], writes=[r_cc])
                            S.op("pool", lambda e: e.tensor_tensor(ini_re[:, gp0:gp0 + 2], cc[:, 0:2], cc[:, 2:4], ALU.subtract),
                                 reads=[r_cc], writes=[r_ini])
                            S.op("pool", lambda e: e.tensor_tensor(cc[:, 4:6], glr, rs_, ALU.mult), reads=[rS1, r_ts], writes=[r_cc])
                            S.op("pool", lambda e: e.tensor_tensor(cc[:, 6:8], gli, rc_, ALU.mult), reads=[rS2, r_tc], writes=[r_cc])
                            S.op("pool", lambda e: e.tensor_tensor(ini_im[:, gp0:gp0 + 2], cc[:, 4:6], cc[:, 6:8], ALU.add),
                                 reads=[r_cc], writes=[r_ini])
                        S.op("act", lambda e: e.activation(s1b[:], S1[:], AF.Copy), reads=[rS1], writes=[r_s1b])
                        S.op("act", lambda e: e.activation(s2b[:], S2[:], AF.Copy), reads=[rS2], writes=[r_s2b])

                    def f3():
                        ub, r_ub, u32, r_u32, yg, r_yg = get_tile(jt)
                        (_, _, (Ab, rAb), (Bb, rBb), (Cb, rCb), (Db, rDb), (s1b, r_s1b), (s2b, r_s2b)) = U["b"]
                        S.op("dve", lambda e: e.tensor_tensor(Ab[:], s1b[:], tcv, ALU.mult), reads=[r_s1b, r_tc], writes=[rAb])
                        S.op("pool", lambda e: e.tensor_tensor(Cb[:], s2b[:], tsv, ALU.mult), reads=[r_s2b, r_ts], writes=[rCb])
                        S.op("dve", lambda e: e.tensor_tensor(Bb[:], s1b[:], tsv, ALU.mult), reads=[r_s1b, r_ts], writes=[rBb])
                        S.op("pool", lambda e: e.tensor_tensor(Db[:], s2b[:], tcv, ALU.mult), reads=[r_s2b, r_tc], writes=[rDb])
                        if q == 0:
                            ystate[jt, c] = psY.next()
                        py, rpy = ystate[jt, c]
                        terms = ((Cpr, Ab, rAb), (Cprn, Cb, rCb), (Cpi, Bb, rBb), (Cpi, Db, rDb))
                        for g_ in range(2):
                            gp = gp0 + g_
                            for ti, (Wt, Xt, rXt) in enumerate(terms):
                                first = (q == 0 and g_ == 0 and ti == 0)
                                last = (q == 1 and g_ == 1 and ti == 3)
                                S.op("pe", lambda e, g_=g_, gp=gp, first=first, last=last, Wt=Wt, Xt=Xt: e.matmul(
                                    py[:, 0:LC], lhsT=Wt[:, gp, :], rhs=Xt[:, g_, :], start=first, stop=last),
                                    reads=[r_Cp, rXt], writes=[rpy])
                        if q == 1:
                            yv, r_yv = yvr.next()
                            S.op("dve", lambda e: e.scalar_tensor_tensor(
                                yv[:], u32[:, csl], dsb[:, jt:jt + 1], py[:, 0:LC], ALU.mult, ALU.add),
                                reads=[r_u32, r_dsb, rpy], writes=[r_yv])
                            S.op("act", lambda e: e.activation(yg[:, csl], yv[:], AF.Gelu_apprx_tanh),
                                 reads=[r_yv], writes=[r_yg])
                            del ystate[jt, c]
                            if c == NCH - 1:
                                S.dma("sp", lambda e: e.dma_start(out=yg_d[jt], in_=yg[:]), r_yg_d, reads=[r_yg])
                    return [f0, f1, f2, f3]

                ystate = {}
                units = [make_unit(jt, c, q) for jt in range(8) for c in range(NCH) for q in range(2)]
                NST = 4
                for step in range(len(units) + NST - 1):
                    for st_ in range(NST):
                        n = step - st_
                        if 0 <= n < len(units):
                            units[n][st_]()
            S.barrier()

        with contextlib.ExitStack() as st:
            wo = self.sb(st, pfx + "wo", [128, 8, 2 * D], BF16)
            r_wo = Res("wo")
            S.dma("pool", lambda e: e.dma_start(out=wo[:].rearrange("p j n -> p (j n)"), in_=wo_d[li]), r_wo)
            ybr = Rot([(self.sb(st, pfx + "yb%d" % i, [128, 8, TB], BF16), Res("yb")) for i in range(3)])
            xsr = Rot([(self.sb(st, pfx + "xs%d" % i, [128, 8, TB], F32), Res("xs")) for i in range(3)])
            sgr = Rot([(self.sb(st, pfx + "sg%d" % i, [128, TB], F32), Res("sg")) for i in range(2)])
            L = self.ln_setup(st, pfx)
            psA = Rot(self.ps[0:6])
            psB = Rot(self.ps[6:8])

            def ld(tb):
                yb, r_yb = ybr.next()
                xs, r_xs = xsr.next()
                yv_ = yg_d[:, :, tb * TB:(tb + 1) * TB].rearrange("c p t -> p c t")
                sv = src[:, :, tb * TB:(tb + 1) * TB].rearrange("c p t -> p c t")
                S.dma("sp", lambda e: e.dma_start(out=yb[:], in_=yv_), r_yb, reads=[r_yg_d])
                S.dma("sp", lambda e: e.dma_start(out=xs[:], in_=sv), r_xs, reads=[rsrc])
                return yb, r_yb, xs, r_xs
            nxt = ld(0)
            pend = []
            for tb in range(NTB):
                yb, r_yb, xs, r_xs = nxt
                if tb + 1 < NTB:
                    nxt = ld(tb + 1)
                for dc in range(NDC):
                    pv, rpv = psA.next()
                    pg, rpg = psA.next()
                    for half, (pp, rpp) in enumerate(((pv, rpv), (pg, rpg))):
                        for j in range(8):
                            S.op("pe", lambda e, pp=pp, j=j, dc=dc, yb=yb, half=half: e.matmul(
                                pp[:], lhsT=wo[:, j, half * D + dc * 128:half * D + (dc + 1) * 128], rhs=yb[:, j, :],
                                start=(j == 0), stop=(j == 7)), reads=[r_wo, r_yb], writes=[rpp])
                    sg, r_sg = sgr.next()
                    S.op("act", lambda e, sg=sg, pg=pg: e.activation(sg[:], pg[:], AF.Sigmoid), reads=[rpg], writes=[r_sg])
                    S.op("dve", lambda e, sg=sg, pv=pv: e.tensor_tensor(sg[:], pv[:], sg[:], ALU.mult),
                         reads=[rpv, r_sg], writes=[r_sg])
                    xv = xs[:, dc, :]
                    S.op("dve", lambda e, xv=xv, sg=sg: e.scalar_tensor_tensor(
                        xv, xv, ALPHA, sg[:], ALU.mult, ALU.add), reads=[r_sg, r_xs], writes=[r_xs])
                    for _ in range(2):
                        if pend:
                            pend.pop(0)()
                while pend:
                    pend.pop(0)()
                pend = self.ln_pieces(L, xs[:], r_xs, lnidx, LN_EPS, dst, rdst, tb * TB, psB.next, psB.next, k=tb % 2)
            while pend:
                pend.pop(0)()


def prep_ffn_weights(w_in_list, w_out_list):
    wins, wouts = [], []
    for w_in, w_out in zip(w_in_list, w_out_list):
        a = np.asarray(w_in).reshape(8, 128, 2, NFC, 128)
        a = a.transpose(3, 1, 2, 0, 4).reshape(NFC, 128, 2 * 8 * 128)
        wins.append(a)
        b = np.asarray(w_out).reshape(NFC, 128, NDC, 128)
        b = b.transpose(2, 1, 0, 3).reshape(NDC, 128, NFC * 128)
        wouts.append(b)
    return np.ascontiguousarray(np.stack(wins)), np.ascontiguousarray(np.stack(wouts))


def prep_lnp(ln1_g, ln1_b, lnm_g, lnm_b, ln2_g, ln2_b):
    arr = np.zeros((DEPTH, 3, 2, NDC, 128), np.float32)
    for l in range(DEPTH):
        for i, (g, b) in enumerate(((ln1_g, ln1_b), (lnm_g, lnm_b), (ln2_g, ln2_b))):
            arr[l, i, 0] = np.asarray(g[l]).reshape(NDC, 128)
            arr[l, i, 1] = np.asarray(b[l]).reshape(NDC, 128)
    return np.ascontiguousarray(arr.reshape(DEPTH * 3 * 2 * NDC, 128).T)


def x_to_dev(xb):
    T = xb.shape[0]
    return np.ascontiguousarray(np.asarray(xb).T.reshape(NDC, 128, T))


def x_from_dev(y):
    T = y.shape[-1]
    return np.ascontiguousarray(y.reshape(D, T).T)


def prep_fox_weights(w_in_list, b_f_list, w_o_list):
    wqkv, wf, bf, wo = [], [], [], []
    for w_in, b_f, w_o in zip(w_in_list, b_f_list, w_o_list):
        w_in = np.asarray(w_in)
        a = w_in[:, :3 * D].reshape(8, 128, 3, 8, 128)
        wqkv.append(a.transpose(3, 1, 2, 0, 4).reshape(8, 128, 3 * 8 * 128))
        f = w_in[:, 3 * D:].reshape(8, 128, NH)
        wf.append(f.transpose(1, 0, 2).reshape(128, 8 * NH))
        bf.append(np.asarray(b_f).reshape(NH, 1))
        o = np.asarray(w_o).reshape(8, 128, D)
        wo.append(o.transpose(1, 0, 2).reshape(128, 8 * D))
    c = np.ascontiguousarray
    return c(np.stack(wqkv)), c(np.stack(wf)), c(np.stack(bf)), c(np.stack(wo))


def prep_s5_weights(a_re_l, a_im_l, log_dt_l, b_re_l, b_im_l, c_re_l, c_im_l, d_l, w_out_l):
    pls, Bs, Cs, ds, wos = [], [], [], [], []
    for a_re, a_im, log_dt, b_re, b_im, c_re, c_im, d, w_out in zip(
            a_re_l, a_im_l, log_dt_l, b_re_l, b_im_l, c_re_l, c_im_l, d_l, w_out_l):
        def PL(a):
            return np.asarray(a).reshape(32, 2, 64).transpose(1, 2, 0).reshape(128, 32)
        ldt = np.repeat(np.asarray(log_dt).reshape(64, 1), 64, axis=1)
        pls.append(np.concatenate([PL(a_re), PL(a_im), PL(ldt)], axis=1))
        Bb = np.zeros((2, 32, 128, 128), np.float32)
        Cb = np.zeros((2, 32, 128, 128), np.float32)
        for k, (bb, cc) in enumerate(((b_re, c_re), (b_im, c_im))):
            bb = np.asarray(bb)
            cc = np.asarray(cc)
            for g in range(64):
                gp, g2, g8 = g // 2, g % 2, g % 8
                Bb[k, gp, g8 * 16:(g8 + 1) * 16, g2 * 64:(g2 + 1) * 64] = bb[g].T
                Cb[k, gp, g2 * 64:(g2 + 1) * 64, g8 * 16:(g8 + 1) * 16] = cc[g].T
        Bs.append(Bb.transpose(0, 2, 1, 3).reshape(2, 128, 32 * 128))
        Cs.append(Cb.transpose(0, 2, 1, 3).reshape(2, 128, 32 * 128))
        ds.append(np.asarray(d).reshape(8, 128).T)
        wos.append(np.asarray(w_out).reshape(8, 128, 2 * D).transpose(1, 0, 2).reshape(128, 8 * 2 * D))
    c = np.ascontiguousarray
    return c(np.stack(pls)), c(np.stack(Bs)), c(np.stack(Cs)), c(np.stack(ds)), c(np.stack(wos))


def full_plan():
    plan = []
    for l in range(DEPTH):
        plan.append(("ffn", l, 1))
        plan.append(("fox", l) if l % 2 == 0 else ("s5", l))
        plan.append(("ffn", l, 2))
    return plan


def kernel(x, ffn1_w_in, ffn1_w_out, ln1_g, ln1_b, lnm_g, lnm_b,
           ffn2_w_in, ffn2_w_out, ln2_g, ln2_b,
           fox_w_in, fox_b_f, fox_w_o,
           s5_a_re, s5_a_im, s5_log_dt, s5_b_re, s5_b_im, s5_c_re, s5_c_im,
           s5_d, s5_w_out):
    x = np.asarray(x, dtype=np.float32)
    nb, T, _ = x.shape
    mk = MK(T, full_plan())
    nc = mk.build()
    f32 = lambda a: np.asarray(a, dtype=np.float32)
    w_in_l, w_out_l = [], []
    for l in range(DEPTH):
        w_in_l += [f32(ffn1_w_in[l]), f32(ffn2_w_in[l])]
        w_out_l += [f32(ffn1_w_out[l]), f32(ffn2_w_out[l])]
    win, wout = prep_ffn_weights(w_in_l, w_out_l)
    lnp = prep_lnp(f32(ln1_g), f32(ln1_b), f32(lnm_g), f32(lnm_b), f32(ln2_g), f32(ln2_b))
    nA = fox_w_in.shape[0]
    wqkv, wf, bf, wo = prep_fox_weights([f32(fox_w_in[i]) for i in range(nA)],
                                        [f32(fox_b_f[i]) for i in range(nA)],
                                        [f32(fox_w_o[i]) for i in range(nA)])
    nB = s5_a_re.shape[0]
    L = lambda a: [f32(a[i]) for i in range(nB)]
    pl, Bs, Cs, ds, wos = prep_s5_weights(L(s5_a_re), L(s5_a_im), L(s5_log_dt), L(s5_b_re), L(s5_b_im),
                                          L(s5_c_re), L(s5_c_im), L(s5_d), L(s5_w_out))
    shared = {"lnp": lnp, "ffn_win": win, "ffn_wout": wout,
              "fox_wqkv": wqkv, "fox_wf": wf, "fox_bf": bf, "fox_wo": wo,
              "s5_pl": pl, "s5_B": Bs, "s5_C": Cs, "s5_d": ds, "s5_wo": wos}
    in_maps = []
    for b in range(nb):
        m = dict(shared)
        m["xT"] = x_to_dev(x[b])
        in_maps.append(m)
    res = run_bass_kernel_spmd(nc, in_maps, core_ids=list(range(nb)))
    out = np.stack([x_from_dev(np.asarray(r["yT"])) for r in res.results]).astype(np.float32)
    return out
```

```python
import contextlib
import math
import numpy as np
import concourse.bass as bass
import concourse.mybir as mybir
from concourse.bass_utils import run_bass_kernel_spmd

F32 = mybir.dt.float32
BF16 = mybir.dt.bfloat16
AF = mybir.ActivationFunctionType
ALU = mybir.AluOpType

D = 1024
NDC = 8
FF = 2816
NFC = 22
NH = 16
HD = 64
DEPTH = 4
ALPHA = (2.0 * DEPTH) ** 0.25
LN_EPS = 1e-5
TS = 1024
TB = 512


class Res:
    __slots__ = ("name", "lw", "readers", "slot", "epoch", "last_tok")

    def __init__(self, name=""):
        self.name = name
        self.lw = None
        self.readers = []
        self.slot = None
        self.epoch = -1
        self.last_tok = None


class Op:
    __slots__ = ("eng", "idx", "fn", "deps", "flag", "dma_tok", "waits")

    def __init__(self, eng, idx, fn):
        self.eng = eng
        self.idx = idx
        self.fn = fn
        self.deps = []
        self.flag = False
        self.dma_tok = None
        self.waits = None


COMPUTE = ("pe", "act", "dve", "pool")


class Sched:
    def __init__(self, nc):
        self.nc = nc
        self.ops = {e: [] for e in ("pe", "act", "dve", "pool", "sp")}
        self.slot_cnt = []
        self.slot_kind = []
        self.free_slots = []
        self.epoch = 0
        self.bar = {e: [] for e in self.ops}

    def _collect(self, op, reads, writes):
        deps = []
        for r in reads:
            if r.lw is not None:
                deps.append(r.lw)
        for w in writes:
            if w.lw is not None:
                deps.append(w.lw)
            deps.extend(w.readers)
        if self.bar[op.eng]:
            deps.extend(self.bar[op.eng])
            self.bar[op.eng] = []
        op.deps = deps
        for r in reads:
            r.readers.append(op)
        for w in writes:
            w.lw = op
            w.readers = []

    def op(self, eng, fn, reads=(), writes=()):
        o = Op(eng, len(self.ops[eng]), fn)
        self._collect(o, reads, writes)
        self.ops[eng].append(o)
        return o

    def dma(self, queue, fn, dst, reads=(), writes=()):
        o = Op(queue, len(self.ops[queue]), fn)
        ws = list(writes)
        if dst not in ws:
            ws.append(dst)
        self._collect(o, reads, ws)
        kind = "sw" if queue == "pool" else "hw"
        if dst.epoch != self.epoch or self.slot_kind[dst.slot] != kind:
            fl = [i for i in self.free_slots if self.slot_kind[i] == kind]
            if fl:
                dst.slot = fl[-1]
                self.free_slots.remove(dst.slot)
            else:
                dst.slot = len(self.slot_cnt)
                self.slot_cnt.append(0)
                self.slot_kind.append(kind)
            dst.epoch = self.epoch
        self.slot_cnt[dst.slot] += 16
        o.dma_tok = (dst.slot, self.slot_cnt[dst.slot])
        dst.last_tok = o.dma_tok
        self.ops[queue].append(o)
        return o

    def barrier(self):
        last = []
        for e, lst in self.ops.items():
            for o in reversed(lst):
                if o.dma_tok is None:
                    last.append(o)
                    break
        seen = set()
        for e, lst in self.ops.items():
            for o in reversed(lst):
                if o.dma_tok is not None and o.dma_tok[0] not in seen:
                    seen.add(o.dma_tok[0])
                    last.append(o)
        for e in self.ops:
            self.bar[e] = list(last)
        self.epoch += 1
        self.free_slots = list(range(len(self.slot_cnt)))

    def emit(self, final_wait=()):
        nc = self.nc
        for e, lst in self.ops.items():
            seen_eng = {}
            seen_dma = {}
            for o in lst:
                need_eng = {}
                need_dma = {}
                for d in o.deps:
                    if d.dma_tok is not None:
                        r, v = d.dma_tok
                        if seen_dma.get(r, 0) < v and need_dma.get(r, 0) < v:
                            need_dma[r] = v
                    else:
                        if d.eng == e and (e == "pe" or d.idx >= o.idx):
                            continue
                        if seen_eng.get(d.eng, -1) < d.idx and need_eng.get(d.eng, -1) < d.idx:
                            need_eng[d.eng] = d.idx
                for s, i in need_eng.items():
                    seen_eng[s] = i
                    self.ops[s][i].flag = True
                for r, v in need_dma.items():
                    seen_dma[r] = v
                o.waits = (need_eng, need_dma)
        cnt = {}
        for e, lst in self.ops.items():
            c = 0
            arr = []
            for o in lst:
                if o.flag:
                    c += 1
                arr.append(c)
            cnt[e] = arr
        with contextlib.ExitStack() as st:
            esem = {e: st.enter_context(nc.semaphore("s_" + e)) for e in COMPUTE}
            dsem = [st.enter_context(nc.semaphore("d%d" % i)) for i in range(len(self.slot_cnt))]
            block = st.enter_context(nc.Block())
            ops = self.ops

            def run(engname, eng):
                for o in ops[engname]:
                    need_eng, need_dma = o.waits
                    for s, i in need_eng.items():
                        eng.wait_ge(esem[s], cnt[s][i])
                    for r, v in need_dma.items():
                        eng.wait_ge(dsem[r], v)
                    ins = o.fn(eng)
                    if o.dma_tok is not None:
                        ins.then_inc(dsem[o.dma_tok[0]], 16)
                    elif o.flag:
                        ins.then_inc(esem[engname], 1)
                if engname == "sp":
                    for r in final_wait:
                        eng.wait_ge(dsem[r.last_tok[0]], r.last_tok[1])

            @block.tensor
            def _(eng):
                run("pe", eng)

            @block.scalar
            def _(eng):
                run("act", eng)

            @block.vector
            def _(eng):
                run("dve", eng)

            @block.gpsimd
            def _(eng):
                run("pool", eng)

            @block.sync
            def _(eng):
                run("sp", eng)


class Rot:
    def __init__(self, items):
        self.items = items
        self.i = 0

    def next(self):
        it = self.items[self.i % len(self.items)]
        self.i += 1
        return it


class Prefetch:
    def __init__(self):
        self.tasks = []
        self.issued = 0

    def add(self, fn):
        self.tasks.append(fn)
        return len(self.tasks) - 1

    def need(self, idx, ahead):
        upto = min(len(self.tasks), idx + 1 + ahead)
        while self.issued < upto:
            self.tasks[self.issued]()
            self.issued += 1


class MK:
    def __init__(self, T, plan):
        self.T = T
        self.plan = plan
        self.nc = nc = bass.Bass("TRN2", target_bir_lowering=False)
        self.S = Sched(nc)
        self.st = contextlib.ExitStack()
        dt = nc.dram_tensor
        self.x_in = dt("xT", [NDC, 128, T], F32, kind="ExternalInput").ap()
        self.y_out = dt("yT", [NDC, 128, T], F32, kind="ExternalOutput").ap()
        self.scr = [dt("scr%d" % i, [NDC, 128, T], F32, kind="Internal").ap() for i in range(2)]
        self.lnp_d = dt("lnp", [128, DEPTH * 3 * 2 * NDC], F32, kind="ExternalInput").ap()
        self.w = {}
        need = set(p[0] for p in plan)
        self.nffn = 1 + max([p[1] * 2 + p[2] - 1 for p in plan if p[0] == "ffn"], default=-1)
        self.nfox = 1 + max([p[1] // 2 for p in plan if p[0] == "fox"], default=-1)
        self.ns5 = 1 + max([p[1] // 2 for p in plan if p[0] == "s5"], default=-1)
        if "ffn" in need:
            self.w["ffn_win"] = dt("ffn_win", [self.nffn, NFC, 128, 2 * 8 * 128], F32, kind="ExternalInput").ap()
            self.w["ffn_wout"] = dt("ffn_wout", [self.nffn, NDC, 128, NFC * 128], F32, kind="ExternalInput").ap()
        if "fox" in need:
            self.w["fox_wqkv"] = dt("fox_wqkv", [self.nfox, 8, 128, 3 * 8 * 128], F32, kind="ExternalInput").ap()
            self.w["fox_wf"] = dt("fox_wf", [self.nfox, 128, 8 * NH], F32, kind="ExternalInput").ap()
            self.w["fox_bf"] = dt("fox_bf", [self.nfox, NH, 1], F32, kind="ExternalInput").ap()
            self.w["fox_wo"] = dt("fox_wo", [self.nfox, 128, 8 * D], F32, kind="ExternalInput").ap()
            self.cs_d = dt("cs_d", [NH, 6, T], BF16, kind="Internal").ap()
            self.o_d = dt("o_d", [8, 128, T], BF16, kind="Internal").ap()
            self.r_cs_d = Res("cs_d")
            self.r_o_d = Res("o_d")
        if "s5" in need:
            self.w["s5_pl"] = dt("s5_pl", [self.ns5, 128, 96], F32, kind="ExternalInput").ap()
            self.w["s5_B"] = dt("s5_B", [self.ns5, 2, 128, 32 * 128], F32, kind="ExternalInput").ap()
            self.w["s5_C"] = dt("s5_C", [self.ns5, 2, 128, 32 * 128], F32, kind="ExternalInput").ap()
            self.w["s5_d"] = dt("s5_d", [self.ns5, 128, 8], F32, kind="ExternalInput").ap()
            self.w["s5_wo"] = dt("s5_wo", [self.ns5, 128, 8 * 2 * D], F32, kind="ExternalInput").ap()
            self.yg_d = dt("yg_d", [8, 128, T], BF16, kind="Internal").ap()
            self.r_yg_d = Res("yg_d")
        self.r_x_in = Res("x_in")
        self.r_y = Res("y_out")
        self.r_scr = [Res("scr0"), Res("scr1")]

    def sb(self, st, name, shape, dtype):
        return st.enter_context(self.nc.sbuf_tensor(name, shape, dtype))

    def build(self):
        nc, S = self.nc, self.S
        with contextlib.ExitStack() as st:
            self.ps = []
            for i in range(8):
                t = st.enter_context(nc.psum_tensor("ps%d" % i, [128, TB], F32))
                self.ps.append((t, Res("ps%d" % i)))
            self.lnp = self.sb(st, "lnp_sb", [128, DEPTH * 3 * 2 * NDC], F32)
            self.r_lnp = Res("lnp")
            S.dma("sp", lambda e: e.dma_start(out=self.lnp[:], in_=self.lnp_d), self.r_lnp)
            self.ones = self.sb(st, "ones_bf", [128, 128], BF16)
            self.r_ones = Res("ones")
            S.op("dve", lambda e: e.memset(self.ones[:], 1.0), writes=[self.r_ones])

            cur, rcur = self.x_in, self.r_x_in
            for pi, ph in enumerate(self.plan):
                last = pi == len(self.plan) - 1
                if last:
                    dst, rdst = self.y_out, self.r_y
                else:
                    dst, rdst = self.scr[pi % 2], self.r_scr[pi % 2]
                if ph[0] == "ffn":
                    self.ffn_phase(cur, rcur, dst, rdst, ph[1], ph[2])
                elif ph[0] == "fox":
                    self.fox_phase(cur, rcur, dst, rdst, ph[1])
                elif ph[0] == "s5":
                    self.s5_phase(cur, rcur, dst, rdst, ph[1])
                S.barrier()
                cur, rcur = dst, rdst
            S.emit(final_wait=[self.r_y])
        return nc

    def ln_setup(self, st, pfx):
        L = {}
        L["wb"] = self.sb(st, pfx + "wb", [128, NDC, TB], BF16)
        L["w2b"] = self.sb(st, pfx + "w2b", [128, NDC, TB], BF16)
        L["mean"] = self.sb(st, pfx + "mean", [128, TB], F32)
        L["msq"] = self.sb(st, pfx + "msq", [128, TB], F32)
        L["var"] = self.sb(st, pfx + "var", [128, TB], F32)
        L["rstd"] = self.sb(st, pfx + "rstd", [128, TB], F32)
        for k in ("mean2", "msq2", "var2", "rstd2"):
            L[k] = self.sb(st, pfx + k, [128, TB], F32)
        for k in ("wb", "w2b", "mean", "msq", "var", "rstd", "mean2", "msq2", "var2", "rstd2"):
            L["r_" + k] = Res(pfx + k)
        return L

    def ln_block(self, L, wv, r_w, lnidx, eps, dst, rdst, tok0, psS, psQ, k=0):
        for f in self.ln_pieces(L, wv, r_w, lnidx, eps, dst, rdst, tok0, psS, psQ, k):
            f()

    def ln_pieces(self, L, wv, r_w, lnidx, eps, dst, rdst, tok0, psS, psQ, k=0):
        S = self.S
        wb, w2b = L["wb"], L["w2b"]
        sfx = "" if k == 0 else "2"
        mean, msq, var, rstd = L["mean" + sfx], L["msq" + sfx], L["var" + sfx], L["rstd" + sfx]
        r_mean, r_msq, r_var, r_rstd = L["r_mean" + sfx], L["r_msq" + sfx], L["r_var" + sfx], L["r_rstd" + sfx]
        r_dc = [Res("wdc") for _ in range(NDC)]

        def conv():
            S.op("dve", lambda e: e.tensor_copy(wb[:], wv), reads=[r_w], writes=[L["r_wb"]])
            S.op("act", lambda e: e.activation(w2b[:], wv, AF.Square), reads=[r_w], writes=[L["r_w2b"]])

        def stats():
            (pS, rS), (pQ, rQ) = psS(), psQ()
            for dc in range(NDC):
                S.op("pe", lambda e, dc=dc: e.matmul(pS[:], lhsT=self.ones[:], rhs=wb[:, dc, :],
                                                      start=(dc == 0), stop=(dc == NDC - 1)),
                     reads=[L["r_wb"], self.r_ones], writes=[rS])
            for dc in range(NDC):
                S.op("pe", lambda e, dc=dc: e.matmul(pQ[:], lhsT=self.ones[:], rhs=w2b[:, dc, :],
                                                      start=(dc == 0), stop=(dc == NDC - 1)),
                     reads=[L["r_w2b"], self.r_ones], writes=[rQ])
            S.op("dve", lambda e: e.tensor_scalar(mean[:], pS[:], 1.0 / D, None, ALU.mult),
                 reads=[rS], writes=[r_mean])
            S.op("dve", lambda e: e.tensor_tensor(msq[:], mean[:], mean[:], ALU.mult),
                 reads=[r_mean], writes=[r_msq])
            S.op("dve", lambda e: e.tensor_scalar(var[:], pQ[:], 1.0 / D, eps, ALU.mult, ALU.add),
                 reads=[rQ], writes=[r_var])
            S.op("dve", lambda e: e.tensor_tensor(var[:], var[:], msq[:], ALU.subtract),
                 reads=[r_var, r_msq], writes=[r_var])
            S.op("act", lambda e: e.activation(rstd[:], var[:], AF.Ln), reads=[r_var], writes=[r_rstd])
            S.op("act", lambda e: e.activation(rstd[:], rstd[:], AF.Exp, scale=-0.5),
                 reads=[r_rstd], writes=[r_rstd])

        gofs = (lnidx * 2) * NDC
        bofs = (lnidx * 2 + 1) * NDC

        def norm(dc):
            v = wv[:, dc, :]
            S.op("dve", lambda e: e.tensor_tensor(v, v, mean[:], ALU.subtract),
                 reads=[r_w, r_mean], writes=[r_dc[dc]])
            S.op("dve", lambda e: e.tensor_tensor(v, v, rstd[:], ALU.mult),
                 reads=[r_dc[dc], r_rstd], writes=[r_dc[dc]])
            g = self.lnp[:, gofs + dc:gofs + dc + 1]
            b = self.lnp[:, bofs + dc:bofs + dc + 1]
            S.op("act", lambda e: e.activation(v, v, AF.Identity, bias=b, scale=g),
                 reads=[r_dc[dc], self.r_lnp], writes=[r_dc[dc]])
            if dc == NDC - 1:
                dv = dst[:, :, tok0:tok0 + TB].rearrange("c p t -> p c t")
                S.dma("sp", lambda e: e.dma_start(out=dv, in_=wv), rdst, reads=r_dc, writes=[r_w])
        return [conv, stats] + [(lambda dc=dc: norm(dc)) for dc in range(NDC)]

    def ffn_phase(self, src, rsrc, dst, rdst, layer, which):
        S, T = self.S, self.T
        widx = layer * 2 + (which - 1)
        lnidx = layer * 3 + (0 if which == 1 else 2)
        win = self.w["ffn_win"]
        wout = self.w["ffn_wout"]
        nsb = T // TS
        ntb = TS // TB
        with contextlib.ExitStack() as st:
            pfx = "f%d_" % widx
            xs = self.sb(st, pfx + "xs", [128, NDC, TS], F32)
            r_xs = [Res("xs%d" % i) for i in range(ntb)]
            xbf = [(self.sb(st, pfx + "xbf%d" % i, [128, NDC, TS], BF16), [Res("xbf%d" % i) for _ in range(ntb)])
                   for i in range(2)]
            aT = self.sb(st, pfx + "aT", [128, NFC, TS], BF16)
            r_aT = [[Res("aT") for _ in range(ntb)] for _ in range(NFC)]
            wgu = Rot([(self.sb(st, pfx + "wgu%d" % i, [128, 2, 8, 128], BF16), Res("wgu%d" % i)) for i in range(4)])
            wo = Rot([(self.sb(st, pfx + "wo%d" % i, [128, NFC, 128], BF16), Res("wo%d" % i)) for i in range(3)])
            sg = Rot([(self.sb(st, pfx + "sg%d" % i, [128, TB], BF16), Res("sg%d" % i)) for i in range(4)])
            L = self.ln_setup(st, pfx)
            psA = Rot(self.ps[0:6])
            psB = Rot(self.ps[6:8])

            pf = Prefetch()
            plan = []
            for sbk in range(nsb):
                info = {}
                t0 = sbk * TS
                xb, r_xb = xbf[sbk % 2]
                info["xbf"] = (xb, r_xb)

                def ld_xbf(xb=xb, r_xb=r_xb, t0=t0):
                    for tb in range(ntb):
                        sv = src[:, :, t0 + tb * TB:t0 + (tb + 1) * TB].rearrange("c p t -> p c t")
                        S.dma("pool", lambda e, sv=sv, tb=tb: e.dma_start(out=xb[:, :, tb * TB:(tb + 1) * TB], in_=sv),
                              r_xb[tb], reads=[rsrc])
                info["t_xbf"] = pf.add(ld_xbf)
                info["wgu"] = []
                for fc in range(NFC):
                    buf, rb = wgu.next()

                    def ld_w(buf=buf, rb=rb, fc=fc):
                        S.dma("pool", lambda e: e.dma_start(
                            out=buf[:].rearrange("p a k m -> p (a k m)"), in_=win[widx, fc]), rb)
                    info["wgu"].append((pf.add(ld_w), buf, rb))
                    if fc == 1:
                        def ld_xs(t0=t0):
                            for tb in range(ntb):
                                sv = src[:, :, t0 + tb * TB:t0 + (tb + 1) * TB].rearrange("c p t -> p c t")
                                S.dma("sp", lambda e, sv=sv, tb=tb: e.dma_start(
                                    out=xs[:, :, tb * TB:(tb + 1) * TB], in_=sv), r_xs[tb], reads=[rsrc])
                        info["ld_xs"] = ld_xs
                info["wo"] = []
                for dc in range(NDC):
                    buf, rb = wo.next()

                    def ld_wo(buf=buf, rb=rb, dc=dc):
                        S.dma("pool", lambda e: e.dma_start(
                            out=buf[:].rearrange("p f m -> p (f m)"), in_=wout[widx, dc]), rb)
                    info["wo"].append((pf.add(ld_wo), buf, rb))
                plan.append(info)

            pending_ln = []
            for sbk in range(nsb):
                info = plan[sbk]
                t0 = sbk * TS
                xb, r_xb = info["xbf"]
                pf.need(info["t_xbf"], 2)
                for fc in range(NFC):
                    tid, wbuf, rwb = info["wgu"][fc]
                    pf.need(tid, 3)
                    if fc >= 1:
                        for _ in range(2):
                            if pending_ln:
                                pending_ln.pop(0)()
                    if fc == 12:
                        assert not pending_ln
                        info["ld_xs"]()
                    for tb in range(ntb):
                        ts_ = slice(tb * TB, (tb + 1) * TB)
                        pg, rpg = psA.next()
                        pu, rpu = psA.next()
                        for gu, (pp, rpp) in enumerate(((pg, rpg), (pu, rpu))):
                            for kc in range(8):
                                S.op("pe", lambda e, pp=pp, gu=gu, kc=kc, ts_=ts_, wbuf=wbuf, xb=xb: e.matmul(
                                    pp[:], lhsT=wbuf[:, gu, kc, :], rhs=xb[:, kc, ts_],
                                    start=(kc == 0), stop=(kc == 7)),
                                    reads=[rwb, r_xb[tb]], writes=[rpp])
                        sgb, rsg = sg.next()
                        S.op("act", lambda e, sgb=sgb, pg=pg: e.activation(sgb[:], pg[:], AF.Silu),
                             reads=[rpg], writes=[rsg])
                        S.op("dve", lambda e, sgb=sgb, pu=pu, fc=fc, ts_=ts_: e.tensor_tensor(
                            aT[:, fc, ts_], sgb[:], pu[:], ALU.mult),
                            reads=[rsg, rpu], writes=[r_aT[fc][tb]])
                for dc in range(NDC):
                    tid, wbuf, rwb = info["wo"][dc]
                    pf.need(tid, 2)
                    for tb in range(ntb):
                        ts_ = slice(tb * TB, (tb + 1) * TB)
                        py, rpy = psA.next()
                        for fc in range(NFC):
                            S.op("pe", lambda e, py=py, fc=fc, ts_=ts_, wbuf=wbuf: e.matmul(
                                py[:], lhsT=wbuf[:, fc, :], rhs=aT[:, fc, ts_],
                                start=(fc == 0), stop=(fc == NFC - 1)),
                                reads=[rwb, r_aT[fc][tb]], writes=[rpy])
                        xv = xs[:, dc, ts_]
                        S.op("dve", lambda e, xv=xv, py=py: e.scalar_tensor_tensor(
                            xv, xv, 2.0 * ALPHA, py[:], ALU.mult, ALU.add),
                            reads=[rpy, r_xs[tb]], writes=[r_xs[tb]])
                while pending_ln:
                    pending_ln.pop(0)()
                for tb in range(ntb):
                    ts_ = slice(tb * TB, (tb + 1) * TB)
                    pcs = self.ln_pieces(L, xs[:, :, ts_], r_xs[tb], lnidx, 4.0 * LN_EPS, dst, rdst,
                                         t0 + tb * TB, psB.next, psB.next, k=tb % 2)
                    if tb == 0:
                        first = pcs
                    else:
                        pending_ln.extend([first[0], first[1], pcs[0], first[2], pcs[1]] + first[3:] + pcs[2:])
                if sbk + 1 < nsb:
                    pf.need(plan[sbk + 1]["t_xbf"], 2)
            while pending_ln:
                pending_ln.pop(0)()


    def fox_phase(self, src, rsrc, dst, rdst, layer):
        S, T, nc = self.S, self.T, self.nc
        li = layer // 2
        lnidx = layer * 3 + 1
        NTB = T // TB
        NKB = T // 128
        wqkv, wf_d, bf_d, wo_d = self.w["fox_wqkv"], self.w["fox_wf"], self.w["fox_bf"], self.w["fox_wo"]
        cs_d, o_d, r_cs_d, r_o_d = self.cs_d, self.o_d, self.r_cs_d, self.r_o_d
        pfx = "x%d_" % layer

        stx = contextlib.ExitStack()
        xb_all = self.sb(stx, pfx + "xball", [128, 8, T], BF16)
        r_xall = [Res("xall") for _ in range(NTB)]
        for tb in range(NTB):
            sv = src[:, :, tb * TB:(tb + 1) * TB].rearrange("c p t -> p c t")
            S.dma("pool", lambda e, sv=sv, tb=tb: e.dma_start(out=xb_all[:, :, tb * TB:(tb + 1) * TB], in_=sv),
                  r_xall[tb], reads=[rsrc])

        with contextlib.ExitStack() as st:
            wf = self.sb(st, pfx + "wf", [128, 8, NH], BF16)
            r_wf = Res("wf")
            bf = self.sb(st, pfx + "bf", [NH, 1], F32)
            r_bf = Res("bf")
            lsp = self.sb(st, pfx + "lsp", [NH, T], F32)
            r_lsp = Res("lsp")
            ncm = self.sb(st, pfx + "ncm", [NH, T], F32)
            r_ncm = Res("ncm")
            one = self.sb(st, pfx + "one", [NH, T], F32)
            r_one = Res("one")
            r1 = self.sb(st, pfx + "r1", [NH, T], F32)
            r_r1 = Res("r1")
            csb = self.sb(st, pfx + "csb", [NH, 6, T], BF16)
            r_csb = Res("csb")
            etmp = Rot([(self.sb(st, pfx + "etmp%d" % i, [NH, TB], F32), Res("etmp")) for i in range(2)])
            S.dma("pool", lambda e: e.dma_start(out=wf[:].rearrange("p k h -> p (k h)"), in_=wf_d[li]), r_wf)
            S.dma("sp", lambda e: e.dma_start(out=bf[:], in_=bf_d[li]), r_bf)
            S.op("dve", lambda e: e.tensor_scalar(bf[:], bf[:], -1.0, None, ALU.mult), reads=[r_bf], writes=[r_bf])
            S.op("dve", lambda e: e.memset(one[:], 1.0), writes=[r_one])
            psr = Rot(self.ps[0:2])
            for tb in range(NTB):
                xb, r_xb = xb_all[:, :, tb * TB:(tb + 1) * TB], r_xall[tb]
                pp, rpp = psr.next()
                for kc in range(8):
                    S.op("pe", lambda e, pp=pp, kc=kc, xb=xb: e.matmul(
                        pp[0:NH, :], lhsT=wf[:, kc, :], rhs=xb[:, kc, :], start=(kc == 0), stop=(kc == 7)),
                        reads=[r_wf, r_xb], writes=[rpp])
                et, ret = etmp.next()
                S.op("act", lambda e, et=et, pp=pp: e.activation(et[:], pp[0:NH, :], AF.Exp, bias=bf[:, 0:1], scale=-1.0),
                     reads=[rpp, r_bf], writes=[ret])
                S.op("act", lambda e, et=et, tb=tb: e.activation(lsp[:, tb * TB:(tb + 1) * TB], et[:], AF.Ln, bias=1.0),
                     reads=[ret], writes=[r_lsp])
            S.op("dve", lambda e: e.tensor_tensor_scan(ncm[:], one[:], lsp[:], 0.0, ALU.mult, ALU.add),
                 reads=[r_one, r_lsp], writes=[r_ncm])
            S.op("dve", lambda e: e.tensor_copy(csb[:, 3, :], ncm[:]), reads=[r_ncm], writes=[r_csb])
            S.op("dve", lambda e: e.tensor_tensor(r1[:], ncm[:], csb[:, 3, :], ALU.subtract),
                 reads=[r_ncm, r_csb], writes=[r_r1])
            S.op("dve", lambda e: e.tensor_copy(csb[:, 4, :], r1[:]), reads=[r_r1], writes=[r_csb])
            S.op("dve", lambda e: e.tensor_tensor(r1[:], r1[:], csb[:, 4, :], ALU.subtract),
                 reads=[r_r1, r_csb], writes=[r_r1])
            S.op("dve", lambda e: e.tensor_copy(csb[:, 5, :], r1[:]), reads=[r_r1], writes=[r_csb])
            S.op("dve", lambda e: e.tensor_scalar(csb[:, 0:3, :], csb[:, 3:6, :], -1.0, None, ALU.mult),
                 reads=[r_csb], writes=[r_csb])
            S.dma("sp", lambda e: e.dma_start(out=cs_d, in_=csb[:]), r_cs_d, reads=[r_csb])
        S.barrier()

        with contextlib.ExitStack() as st:
            wq = Rot([(self.sb(st, pfx + "wq%d" % i, [128, 3, 8, 128], BF16), Res("wq")) for i in range(2)])
            QK = {}
            for nm in ("QA", "QB", "KA", "KB"):
                QK[nm] = (self.sb(st, pfx + nm, [128, T], BF16), Res(nm))
            VA = self.sb(st, pfx + "VA", [128, NKB, 128], BF16)
            VB = self.sb(st, pfx + "VB", [128, NKB, 128], BF16)
            r_VA, r_VB = Res("VA"), Res("VB")
            Eb = Rot([(self.sb(st, pfx + "E%d" % i, [128, TB], BF16), Res("E")) for i in range(6)])
            oT = Rot([(self.sb(st, pfx + "oT%d" % i, [128, T], BF16), Res("oT")) for i in range(2)])
            rc = Rot([(self.sb(st, pfx + "rc%d" % i, [128, TB], F32), Res("rc")) for i in range(2)])
            tri = self.sb(st, pfx + "tri", [128, 128], BF16)
            r_tri = Res("tri")
            S.op("pool", lambda e: e.memset(tri[:], 1.0), writes=[r_tri])
            S.op("pool", lambda e: e.affine_select(tri[:], tri[:], [[1, 128]], ALU.is_ge, 0.0, base=0,
                                                   channel_multiplier=-1), reads=[r_tri], writes=[r_tri])
            S.op("dve", lambda e: e.memset(VA[:], 1.0), writes=[r_VA])
            S.op("dve", lambda e: e.memset(VB[:], 1.0), writes=[r_VB])
            for nm in ("QA", "QB", "KA", "KB"):
                t_, r_ = QK[nm]
                S.op("dve", lambda e, t_=t_: e.memset(t_[64:70, :], 1.0), writes=[r_])
            psS = Rot(self.ps[0:4])
            psO = Rot(self.ps[4:6])
            psP = Rot(self.ps[6:8])

            import os
            dbg = os.environ.get("FOXDBG", "")
            for j in range(0 if dbg == "noA3" else 8):
                wqb, r_wq = wq.next()
                S.dma("pool", lambda e, wqb=wqb, j=j: e.dma_start(
                    out=wqb[:].rearrange("p s k m -> p (s k m)"), in_=wqkv[li, j]), r_wq)
                for hh, (qn, kn) in enumerate((("QA", "KA"), ("QB", "KB"))):
                    h = 2 * j + hh
                    qt, rq = QK[qn]
                    kt, rk = QK[kn]
                    S.dma("sp", lambda e, qt=qt, h=h: e.dma_start(out=qt[64:67, :], in_=cs_d[h, 0:3, :]),
                          rq, reads=[r_cs_d])
                    S.dma("sp", lambda e, kt=kt, h=h: e.dma_start(out=kt[67:70, :], in_=cs_d[h, 3:6, :]),
                          rk, reads=[r_cs_d])
                for tb in range(NTB):
                    xb, r_xb = xb_all[:, :, tb * TB:(tb + 1) * TB], r_xall[tb]
                    tsl = slice(tb * TB, (tb + 1) * TB)
                    for s_, (na, nb_) in enumerate((("QA", "QB"), ("KA", "KB"))):
                        pp, rpp = psP.next()
                        for kc in range(8):
                            S.op("pe", lambda e, pp=pp, kc=kc, xb=xb, s_=s_, wqb=wqb: e.matmul(
                                pp[:], lhsT=wqb[:, s_, kc, :], rhs=xb[:, kc, :], start=(kc == 0), stop=(kc == 7)),
                                reads=[r_wq, r_xb], writes=[rpp])
                        ta, ra = QK[na]
                        tb_, rb2 = QK[nb_]
                        sc = 0.125 if s_ == 0 else 1.0
                        S.op("dve", lambda e, ta=ta, pp=pp, sc=sc, tsl=tsl: e.tensor_scalar(
                            ta[0:64, tsl], pp[0:64, :], sc, None, ALU.mult), reads=[rpp], writes=[ra])
                        S.op("dve", lambda e, tb_=tb_, pp=pp, sc=sc, tsl=tsl: e.tensor_scalar(
                            tb_[0:64, tsl], pp[64:128, :], sc, None, ALU.mult), reads=[rpp], writes=[rb2])
                    pp, rpp = psP.next()
                    for i4 in range(4):
                        for kc in range(8):
                            S.op("pe", lambda e, pp=pp, kc=kc, xb=xb, i4=i4, wqb=wqb: e.matmul(
                                pp[:, i4 * 128:(i4 + 1) * 128], lhsT=xb[:, kc, i4 * 128:(i4 + 1) * 128],
                                rhs=wqb[:, 2, kc, :], start=(kc == 0), stop=(kc == 7)),
                                reads=[r_wq, r_xb], writes=[rpp])
                    pv = pp[:].rearrange("p (a m) -> p a m", a=4)
                    S.op("dve", lambda e, pv=pv, tb=tb: e.tensor_copy(VA[:, tb * 4:(tb + 1) * 4, 0:64], pv[:, :, 0:64]),
                         reads=[rpp], writes=[r_VA])
                    S.op("dve", lambda e, pv=pv, tb=tb: e.tensor_copy(VB[:, tb * 4:(tb + 1) * 4, 64:128], pv[:, :, 64:128]),
                         reads=[rpp], writes=[r_VB])
                oTb, r_oT = oT.next()
                steps = []
                for hh in range(2):
                    for qb in range(NTB):
                        nkb = 4 * (qb + 1)
                        for kb in range(nkb):
                            steps.append((hh, qb, kb, nkb))
                state = {}

                def emit_S(i):
                    hh, qb, kb, nkb = steps[i]
                    qt, rq = QK["QA" if hh == 0 else "QB"]
                    kt, rk = QK["KA" if hh == 0 else "KB"]
                    r = kb - 4 * qb
                    c0 = 128 * r if r > 0 else 0
                    ps_, rps = psS.next()
                    S.op("pe", lambda e: e.matmul(ps_[:, c0:TB], lhsT=kt[0:70, kb * 128:(kb + 1) * 128],
                                                  rhs=qt[0:70, qb * TB + c0:(qb + 1) * TB], start=True, stop=True),
                         reads=[rq, rk], writes=[rps])
                    state[i] = (ps_, rps, c0, r)

                def emit_rest(i, oTb=oTb, r_oT=r_oT):
                    hh, qb, kb, nkb = steps[i]
                    ps_, rps, c0, r = state.pop(i)
                    Et, rE = Eb.next()
                    S.op("act", lambda e: e.activation(Et[:, c0:TB], ps_[:, c0:TB], AF.Exp),
                         reads=[rps], writes=[rE])
                    if r >= 0:
                        S.op("dve", lambda e: e.tensor_tensor(Et[:, c0:c0 + 128], Et[:, c0:c0 + 128], tri[:], ALU.mult),
                             reads=[rE, r_tri], writes=[rE])
                    if kb == 0:
                        state["o", hh, qb] = psO.next()
                    po, rpo = state["o", hh, qb]
                    Vt, rV = (VA, r_VA) if hh == 0 else (VB, r_VB)
                    S.op("pe", lambda e: e.matmul(po[:, c0:TB], lhsT=Vt[:, kb, :], rhs=Et[:, c0:TB],
                                                  start=(kb == 0), stop=(kb == nkb - 1), skip_group_check=True),
                         reads=[rV, rE], writes=[rpo])
                    if kb == nkb - 1:
                        rct, rrc = rc.next()
                        orow = slice(0, 64) if hh == 0 else slice(64, 128)
                        drow = slice(64, 128) if hh == 0 else slice(0, 64)
                        S.op("dve", lambda e: e.reciprocal(rct[drow, :], po[drow, :]), reads=[rpo], writes=[rrc])
                        S.op("dve", lambda e: e.tensor_tensor(oTb[orow, qb * TB:(qb + 1) * TB], po[orow, :], rct[drow, :], ALU.mult),
                             reads=[rpo, rrc], writes=[r_oT])
                        del state["o", hh, qb]

                LOOK = 3
                if dbg == "noattn":
                    steps = []
                n = len(steps)
                for i in range(min(LOOK, n)):
                    emit_S(i)
                for i in range(n):
                    if i + LOOK < n:
                        emit_S(i + LOOK)
                    emit_rest(i)
                S.dma("sp", lambda e, oTb=oTb, j=j: e.dma_start(out=o_d[j], in_=oTb[:]), r_o_d, reads=[r_oT])
        S.barrier()

        with contextlib.ExitStack() as st:
            wo = self.sb(st, pfx + "wo", [128, 8, D], BF16)
            r_wo = Res("wo")
            S.dma("pool", lambda e: e.dma_start(out=wo[:].rearrange("p j n -> p (j n)"), in_=wo_d[li]), r_wo)
            obr = Rot([(self.sb(st, pfx + "ob%d" % i, [128, 8, TB], BF16), Res("ob")) for i in range(3)])
            xsr = Rot([(self.sb(st, pfx + "xs%d" % i, [128, 8, TB], F32), Res("xs")) for i in range(3)])
            L = self.ln_setup(st, pfx)
            psA = Rot(self.ps[0:6])
            psB = Rot(self.ps[6:8])

            def ld(tb):
                ob, r_ob = obr.next()
                xs, r_xs = xsr.next()
                ov = o_d[:, :, tb * TB:(tb + 1) * TB].rearrange("c p t -> p c t")
                sv = src[:, :, tb * TB:(tb + 1) * TB].rearrange("c p t -> p c t")
                S.dma("sp", lambda e: e.dma_start(out=ob[:], in_=ov), r_ob, reads=[r_o_d])
                S.dma("sp", lambda e: e.dma_start(out=xs[:], in_=sv), r_xs, reads=[rsrc])
                return ob, r_ob, xs, r_xs
            nxt = ld(0)
            pend = []
            for tb in range(NTB):
                ob, r_ob, xs, r_xs = nxt
                if tb + 1 < NTB:
                    nxt = ld(tb + 1)
                for dc in range(NDC):
                    pp, rpp = psA.next()
                    for j in range(8):
                        S.op("pe", lambda e, pp=pp, j=j, dc=dc, ob=ob: e.matmul(
                            pp[:], lhsT=wo[:, j, dc * 128:(dc + 1) * 128], rhs=ob[:, j, :],
                            start=(j == 0), stop=(j == 7)), reads=[r_wo, r_ob], writes=[rpp])
                    xv = xs[:, dc, :]
                    S.op("dve", lambda e, xv=xv, pp=pp: e.scalar_tensor_tensor(
                        xv, xv, ALPHA, pp[:], ALU.mult, ALU.add), reads=[rpp, r_xs], writes=[r_xs])
                    for _ in range(2):
                        if pend:
                            pend.pop(0)()
                while pend:
                    pend.pop(0)()
                pend = self.ln_pieces(L, xs[:], r_xs, lnidx, LN_EPS, dst, rdst, tb * TB, psB.next, psB.next, k=tb % 2)
            while pend:
                pend.pop(0)()
        stx.close()

    def rr_sin(self, out, ph, nf, ni, shift, r_out, r_ph, r_nf, r_ni, extra=None):
        if extra is not None:
            self.rr_sin(extra[0], extra[1], extra[2], extra[3], shift, r_out, r_ph, r_nf, r_ni)
        S = self.S
        TWO_PI = 2.0 * math.pi
        if shift != 0.0:
            S.op("dve", lambda e: e.tensor_scalar(ph, ph, shift, None, ALU.add), reads=[r_ph], writes=[r_ph])
        S.op("dve", lambda e: e.tensor_scalar(nf, ph, 1.0 / TWO_PI, None, ALU.mult), reads=[r_ph], writes=[r_nf])
        S.op("dve", lambda e: e.tensor_copy(ni, nf), reads=[r_nf], writes=[r_ni])
        S.op("dve", lambda e: e.tensor_copy(nf, ni), reads=[r_ni], writes=[r_nf])
        S.op("dve", lambda e: e.scalar_tensor_tensor(ph, nf, -TWO_PI, ph, ALU.mult, ALU.add),
             reads=[r_nf, r_ph], writes=[r_ph])
        S.op("dve", lambda e: e.tensor_scalar(nf, ph, math.pi, TWO_PI, ALU.is_gt, ALU.mult),
             reads=[r_ph], writes=[r_nf])
        S.op("dve", lambda e: e.tensor_tensor(ph, ph, nf, ALU.subtract), reads=[r_nf, r_ph], writes=[r_ph])
        PS = 3.141592
        S.op("dve", lambda e: e.tensor_scalar(ph, ph, PS, -PS, ALU.min, ALU.max), reads=[r_ph], writes=[r_ph])
        S.op("act", lambda e: e.activation(out, ph, AF.Sin), reads=[r_ph], writes=[r_out])

    def s5_phase(self, src, rsrc, dst, rdst, layer):
        S, T, nc = self.S, self.T, self.nc
        li = layer // 2
        lnidx = layer * 3 + 1
        LC = 256
        NCH = T // LC
        NTB = T // TB
        I32 = mybir.dt.int32
        pl_d, B_d, C_d, dd_d, wo_d = (self.w["s5_pl"], self.w["s5_B"], self.w["s5_C"],
                                      self.w["s5_d"], self.w["s5_wo"])
        yg_d, r_yg_d = self.yg_d, self.r_yg_d
        pfx = "s%d_" % layer
        with contextlib.ExitStack() as st0:
            tc = self.sb(st0, pfx + "tc", [128, 32, LC], BF16)
            ts = self.sb(st0, pfx + "ts", [128, 32, LC], BF16)
            r_tc, r_ts = Res("tc"), Res("ts")
            rotc = self.sb(st0, pfx + "rotc", [128, 32], F32)
            rots = self.sb(st0, pfx + "rots", [128, 32], F32)
            Bre = self.sb(st0, pfx + "Bre", [128, 32, 128], BF16)
            Bim = self.sb(st0, pfx + "Bim", [128, 32, 128], BF16)
            r_B = Res("B")
            Cpr = self.sb(st0, pfx + "Cpr", [128, 32, 128], BF16)
            Cpi = self.sb(st0, pfx + "Cpi", [128, 32, 128], BF16)
            Cprn = self.sb(st0, pfx + "Cprn", [128, 32, 128], BF16)
            r_Cp = Res("Cp")
            rpl = self.sb(st0, pfx + "rpl", [128, 32], F32)
            r_rpl = Res("rpl")
            dsb = self.sb(st0, pfx + "dsb", [128, 8], F32)
            r_dsb = Res("dsb")
            ini_re = self.sb(st0, pfx + "inire", [128, 32], F32)
            ini_im = self.sb(st0, pfx + "iniim", [128, 32], F32)
            r_ini = Res("ini")
            S.dma("pool", lambda e: e.dma_start(out=Bre[:].rearrange("p g m -> p (g m)"), in_=B_d[li, 0]), r_B)
            S.dma("pool", lambda e: e.dma_start(out=Bim[:].rearrange("p g m -> p (g m)"), in_=B_d[li, 1]), r_B)
            S.dma("sp", lambda e: e.dma_start(out=dsb[:], in_=dd_d[li]), r_dsb)
            S.op("dve", lambda e: e.memset(ini_re[:], 0.0), writes=[r_ini])
            S.op("dve", lambda e: e.memset(ini_im[:], 0.0), writes=[r_ini])

            with contextlib.ExitStack() as st:
                def small(name, dt_=F32, n=32):
                    return self.sb(st, pfx + name, [128, n], dt_), Res(name)
                pl, r_pl = small("pl", F32, 96)
                S.dma("sp", lambda e: e.dma_start(out=pl[:], in_=pl_d[li]), r_pl)
                are, aim, ldt = pl[:, 0:32], pl[:, 32:64], pl[:, 64:96]
                dtt, r_dtt = small("dtt")
                th, r_th = small("th")
                cs, r_cs = small("cs")
                sn, r_sn = small("sn")
                ph, r_ph = small("ph")
                nf, r_nf = small("nf")
                ni, r_ni = small("ni", I32)
                t1, r_t1 = small("t1")
                t2, r_t2 = small("t2")
                den, r_den = small("den")
                zr, r_zr = small("zr")
                zi, r_zi = small("zi")
                nzr, r_nzr = small("nzr")
                S.op("act", lambda e: e.activation(dtt[:], ldt, AF.Exp), reads=[r_pl], writes=[r_dtt])
                S.op("dve", lambda e: e.tensor_tensor(th[:], aim, dtt[:], ALU.mult), reads=[r_pl, r_dtt], writes=[r_th])
                S.op("dve", lambda e: e.tensor_tensor(t1[:], are, dtt[:], ALU.mult), reads=[r_pl, r_dtt], writes=[r_t1])
                S.op("act", lambda e: e.activation(rpl[:], t1[:], AF.Exp), reads=[r_t1], writes=[r_rpl])
                S.op("dve", lambda e: e.tensor_copy(ph[:], th[:]), reads=[r_th], writes=[r_ph])
                self.rr_sin(sn[:], ph[:], nf[:], ni[:], 0.0, r_sn, r_ph, r_nf, r_ni)
                S.op("dve", lambda e: e.tensor_copy(ph[:], th[:]), reads=[r_th, r_sn], writes=[r_ph])
                self.rr_sin(cs[:], ph[:], nf[:], ni[:], math.pi / 2, r_cs, r_ph, r_nf, r_ni)
                S.op("dve", lambda e: e.tensor_tensor(cs[:], cs[:], rpl[:], ALU.mult), reads=[r_cs, r_rpl], writes=[r_cs])
                S.op("dve", lambda e: e.tensor_scalar(cs[:], cs[:], -1.0, None, ALU.add), reads=[r_cs], writes=[r_cs])
                S.op("dve", lambda e: e.tensor_tensor(sn[:], sn[:], rpl[:], ALU.mult), reads=[r_sn, r_rpl], writes=[r_sn])
                S.op("dve", lambda e: e.tensor_tensor(den[:], are, are, ALU.mult), reads=[r_pl], writes=[r_den])
                S.op("dve", lambda e: e.tensor_tensor(t1[:], aim, aim, ALU.mult), reads=[r_pl, r_rpl], writes=[r_t1])
                S.op("dve", lambda e: e.tensor_tensor(den[:], den[:], t1[:], ALU.add), reads=[r_den, r_t1], writes=[r_den])
                S.op("dve", lambda e: e.reciprocal(den[:], den[:]), reads=[r_den], writes=[r_den])
                S.op("dve", lambda e: e.tensor_tensor(t1[:], cs[:], are, ALU.mult), reads=[r_cs, r_pl, r_den], writes=[r_t1])
                S.op("dve", lambda e: e.tensor_tensor(t2[:], sn[:], aim, ALU.mult), reads=[r_sn, r_pl], writes=[r_t2])
                S.op("dve", lambda e: e.tensor_tensor(t1[:], t1[:], t2[:], ALU.add), reads=[r_t1, r_t2], writes=[r_t1])
                S.op("dve", lambda e: e.tensor_tensor(zr[:], t1[:], den[:], ALU.mult), reads=[r_t1, r_den], writes=[r_zr])
                S.op("dve", lambda e: e.tensor_scalar(nzr[:], zr[:], -1.0, None, ALU.mult), reads=[r_zr], writes=[r_nzr])
                S.op("dve", lambda e: e.tensor_tensor(t1[:], sn[:], are, ALU.mult), reads=[r_sn, r_pl, r_zr], writes=[r_t1])
                S.op("dve", lambda e: e.tensor_tensor(t2[:], cs[:], aim, ALU.mult), reads=[r_cs, r_pl, r_t1], writes=[r_t2])
                S.op("dve", lambda e: e.tensor_tensor(t1[:], t1[:], t2[:], ALU.subtract), reads=[r_t1, r_t2], writes=[r_t1])
                S.op("dve", lambda e: e.tensor_tensor(zi[:], t1[:], den[:], ALU.mult), reads=[r_t1, r_den], writes=[r_zi])
                Cre = self.sb(st, pfx + "Cre", [128, 32, 128], F32)
                Cim = self.sb(st, pfx + "Cim", [128, 32, 128], F32)
                r_C = Res("C")
                S.dma("sp", lambda e: e.dma_start(out=Cre[:].rearrange("p g m -> p (g m)"), in_=C_d[li, 0]), r_C)
                S.dma("sp", lambda e: e.dma_start(out=Cim[:].rearrange("p g m -> p (g m)"), in_=C_d[li, 1]), r_C)
                tq = [(self.sb(st, pfx + "tq%d" % i, [128, 128], F32), Res("tq")) for i in range(2)]
                for gp in range(32):
                    eng = "dve"
                    tt, r_tt = tq[gp % 2]
                    S.op(eng, lambda e, gp=gp, tt=tt: e.tensor_scalar(tt[:], Cim[:, gp, :], zi[:, gp:gp + 1], None, ALU.mult),
                         reads=[r_C, r_zi], writes=[r_tt])
                    S.op(eng, lambda e, gp=gp, tt=tt: e.scalar_tensor_tensor(
                        Cpr[:, gp, :], Cre[:, gp, :], zr[:, gp:gp + 1], tt[:], ALU.mult, ALU.subtract),
                        reads=[r_C, r_zr, r_tt], writes=[r_Cp])
                    S.op(eng, lambda e, gp=gp, tt=tt: e.tensor_scalar(tt[:], Cre[:, gp, :], zi[:, gp:gp + 1], None, ALU.mult),
                         reads=[r_C, r_zi, r_Cp], writes=[r_tt])
                    S.op(eng, lambda e, gp=gp, tt=tt: e.scalar_tensor_tensor(
                        Cpi[:, gp, :], Cim[:, gp, :], nzr[:, gp:gp + 1], tt[:], ALU.mult, ALU.subtract),
                        reads=[r_C, r_nzr, r_tt], writes=[r_Cp])
                S.op("dve", lambda e: e.tensor_scalar(Cprn[:], Cpr[:], -1.0, None, ALU.mult), reads=[r_Cp], writes=[r_Cp])
                ii = self.sb(st, pfx + "ii", [128, LC + 1], I32)
                io = self.sb(st, pfx + "io", [128, LC + 1], F32)
                r_io = Res("io")
                S.op("pool", lambda e: e.iota(ii[:], [[1, LC + 1]], base=0, channel_multiplier=0), writes=[r_io])
                S.op("pool", lambda e: e.tensor_copy(io[:], ii[:]), reads=[r_io], writes=[r_io])
                GB = 8
                phb = self.sb(st, pfx + "phb", [128, GB, LC + 1], F32)
                nfb = self.sb(st, pfx + "nfb", [128, GB, LC + 1], F32)
                nib = self.sb(st, pfx + "nib", [128, GB, LC + 1], I32)
                r_phb, r_nfb, r_nib = Res("phb"), Res("nfb"), Res("nib")
                TWO_PI = 2.0 * math.pi
                PS = 3.141592
                pv_, nv_, iv_ = phb[:], nfb[:], nib[:]
                for g0 in range(0, 32, GB):
                    for k in range(GB):
                        S.op("dve", lambda e, k=k, g0=g0: e.tensor_scalar(
                            phb[:, k, :], io[:], th[:, g0 + k:g0 + k + 1], None, ALU.mult),
                            reads=[r_io, r_th, r_ts, r_tc], writes=[r_phb])
                    S.op("dve", lambda e: e.tensor_scalar(nv_, pv_, 1.0 / TWO_PI, None, ALU.mult), reads=[r_phb], writes=[r_nfb])
                    S.op("dve", lambda e: e.tensor_copy(iv_, nv_), reads=[r_nfb], writes=[r_nib])
                    S.op("dve", lambda e: e.tensor_copy(nv_, iv_), reads=[r_nib], writes=[r_nfb])
                    S.op("dve", lambda e: e.scalar_tensor_tensor(pv_, nv_, -TWO_PI, pv_, ALU.mult, ALU.add),
                         reads=[r_nfb, r_phb], writes=[r_phb])
                    S.op("dve", lambda e: e.tensor_scalar(nv_, pv_, math.pi, TWO_PI, ALU.is_gt, ALU.mult),
                         reads=[r_phb], writes=[r_nfb])
                    S.op("dve", lambda e: e.tensor_tensor(pv_, pv_, nv_, ALU.subtract), reads=[r_nfb, r_phb], writes=[r_phb])
                    S.op("dve", lambda e: e.tensor_scalar(pv_, pv_, PS, -PS, ALU.min, ALU.max), reads=[r_phb], writes=[r_phb])
                    S.op("act", lambda e, g0=g0: e.activation(ts[:, g0:g0 + GB, :], phb[:, :, 0:LC], AF.Sin),
                         reads=[r_phb], writes=[r_ts])
                    S.op("act", lambda e, g0=g0: e.activation(rots[:, g0:g0 + GB], phb[:, :, LC], AF.Sin),
                         reads=[r_phb], writes=[r_ts])
                    S.op("dve", lambda e: e.tensor_scalar(pv_, pv_, math.pi / 2, None, ALU.add), reads=[r_phb], writes=[r_phb])
                    S.op("dve", lambda e: e.tensor_scalar(nv_, pv_, math.pi, TWO_PI, ALU.is_gt, ALU.mult),
                         reads=[r_phb], writes=[r_nfb])
                    S.op("dve", lambda e: e.tensor_tensor(pv_, pv_, nv_, ALU.subtract), reads=[r_nfb, r_phb], writes=[r_phb])
                    S.op("dve", lambda e: e.tensor_scalar(pv_, pv_, PS, -PS, ALU.min, ALU.max), reads=[r_phb], writes=[r_phb])
                    S.op("act", lambda e, g0=g0: e.activation(tc[:, g0:g0 + GB, :], phb[:, :, 0:LC], AF.Sin),
                         reads=[r_phb], writes=[r_tc])
                    S.op("act", lambda e, g0=g0: e.activation(rotc[:, g0:g0 + GB], phb[:, :, LC], AF.Sin),
                         reads=[r_phb], writes=[r_tc])
            S.barrier()

            with contextlib.ExitStack() as st:
                ubr = Rot([(self.sb(st, pfx + "ub%d" % i, [128, T], BF16), Res("ub")) for i in range(2)])
                u32r = Rot([(self.sb(st, pfx + "u32%d" % i, [128, T], F32), Res("u32")) for i in range(2)])
                ygr = Rot([(self.sb(st, pfx + "yg%d" % i, [128, T], BF16), Res("yg")) for i in range(2)])
                Wk = Rot([[(self.sb(st, pfx + "w%d_%d" % (i, k), [128, 2, LC], F32), Res("w")) for k in range(4)]
                          for i in range(4)])
                Wb = Rot([[(self.sb(st, pfx + "wb%d_%d" % (i, k), [128, 2, LC], BF16), Res("wb")) for k in range(8)]
                          for i in range(4)])
                yvr = Rot([(self.sb(st, pfx + "yv%d" % i, [128, LC], F32), Res("yv")) for i in range(2)])
                cc = self.sb(st, pfx + "cc", [128, 8], F32)
                r_cc = Res("cc")
                psBU = Rot(self.ps[0:6])
                psY = Rot(self.ps[6:8])

                def ld_tile(jt):
                    ub, r_ub = ubr.next()
                    u32, r_u32 = u32r.next()
                    S.dma("pool", lambda e: e.dma_start(out=ub[:], in_=src[jt]), r_ub, reads=[rsrc])
                    S.dma("sp", lambda e: e.dma_start(out=u32[:], in_=src[jt]), r_u32, reads=[rsrc])
                    return ub, r_ub, u32, r_u32

                tiles = {}

                def get_tile(jt):
                    if jt not in tiles and jt < 8:
                        tiles[jt] = ld_tile(jt) + ygr.next()
                    return tiles.get(jt)

                def make_unit(jt, c, q):
                    U = {}
                    gp0 = 4 * jt + 2 * q
                    csl = slice(c * LC, (c + 1) * LC)
                    tcv = tc[:, gp0:gp0 + 2, :]
                    tsv = ts[:, gp0:gp0 + 2, :]

                    def f0():
                        ub, r_ub, u32, r_u32, yg, r_yg = get_tile(jt)
                        if c == min(2, NCH - 1) and q == 0:
                            get_tile(jt + 1)
                        U["k"] = Wk.next()
                        U["b"] = Wb.next()
                        (pre, rpre), (pim, rpim) = psBU.next(), psBU.next()
                        for g_ in range(2):
                            gp = gp0 + g_
                            S.op("pe", lambda e, g_=g_, gp=gp: e.matmul(pre[:, g_ * LC:(g_ + 1) * LC], lhsT=Bre[:, gp, :],
                                                                       rhs=ub[:, csl], start=True, stop=True),
                                 reads=[r_B, r_ub], writes=[rpre])
                            S.op("pe", lambda e, g_=g_, gp=gp: e.matmul(pim[:, g_ * LC:(g_ + 1) * LC], lhsT=Bim[:, gp, :],
                                                                       rhs=ub[:, csl], start=True, stop=True),
                                 reads=[r_B, r_ub], writes=[rpim])
                        prev = pre[:].rearrange("p (a t) -> p a t", a=2)
                        pimv = pim[:].rearrange("p (a t) -> p a t", a=2)
                        (breb, r_breb), (bimb, r_bimb) = U["b"][0], U["b"][1]
                        S.op("act", lambda e: e.activation(breb[:], prev, AF.Copy), reads=[rpre], writes=[r_breb])
                        S.op("act", lambda e: e.activation(bimb[:], pimv, AF.Copy), reads=[rpim], writes=[r_bimb])

                    def f1():
                        (G1, rG1), (G2, rG2), _, _ = U["k"]
                        ((breb, r_breb), (bimb, r_bimb), (Ab, rAb), (Bb, rBb), (Cb, rCb), (Db, rDb), _, _) = U["b"]
                        S.op("dve", lambda e: e.tensor_tensor(Ab[:], breb[:], tcv, ALU.mult), reads=[r_breb, r_tc], writes=[rAb])
                        S.op("dve", lambda e: e.tensor_tensor(Bb[:], bimb[:], tsv, ALU.mult), reads=[r_bimb, r_ts], writes=[rBb])
                        S.op("dve", lambda e: e.tensor_tensor(Cb[:], bimb[:], tcv, ALU.mult), reads=[r_bimb, r_tc], writes=[rCb])
                        S.op("dve", lambda e: e.tensor_tensor(Db[:], breb[:], tsv, ALU.mult), reads=[r_breb, r_ts], writes=[rDb])
                        S.op("pool", lambda e: e.tensor_tensor(G1[:], Ab[:], Bb[:], ALU.add), reads=[rAb, rBb], writes=[rG1])
                        S.op("pool", lambda e: e.tensor_tensor(G2[:], Cb[:], Db[:], ALU.subtract), reads=[rCb, rDb], writes=[rG2])

                    def f2():
                        (G1, rG1), (G2, rG2), (S1, rS1), (S2, rS2) = U["k"]
                        (s1b, r_s1b), (s2b, r_s2b) = U["b"][6], U["b"][7]
                        for g_ in range(2):
                            gp = gp0 + g_
                            rb = rpl[:, gp:gp + 1].to_broadcast([128, LC])
                            S.op("dve", lambda e, g_=g_, gp=gp, rb=rb: e.tensor_tensor_scan(
                                S1[:, g_, :], rb, G1[:, g_, :], ini_re[:, gp:gp + 1], ALU.mult, ALU.add),
                                reads=[rG1, r_rpl, r_ini], writes=[rS1])
                            S.op("dve", lambda e, g_=g_, gp=gp, rb=rb: e.tensor_tensor_scan(
                                S2[:, g_, :], rb, G2[:, g_, :], ini_im[:, gp:gp + 1], ALU.mult, ALU.add),
                                reads=[rG2, r_rpl, r_ini], writes=[rS2])
                        if c + 1 < NCH:
                            glr = S1[:, :, LC - 1]
                            gli = S2[:, :, LC - 1]
                            rc_ = rotc[:, gp0:gp0 + 2]
                            rs_ = rots[:, gp0:gp0 + 2]
                            S.op("pool", lambda e: e.tensor_tensor(cc[:, 0:2], glr, rc_, ALU.mult), reads=[rS1, r_tc], writes=[r_cc])
                            S.op("pool", lambda e: e.tensor_tensor(cc[:, 2:4], gli, rs_, ALU.mult), reads=[rS2, r_ts], writes=[r_cc])
                            S.op("pool", lambda e: e.tensor_tensor(ini_re[:, gp0:gp0 + 2], cc[:, 0:2], cc[:, 2:4], ALU.subtract),
                                 reads=[r_cc], writes=[r_ini])
                            S.op("pool", lambda e: e.tensor_tensor(cc[:, 4:6], glr, rs_, ALU.mult), reads=[rS1, r_ts], writes=[r_cc])
                            S.op("pool", lambda e: e.tensor_tensor(cc[:, 6:8], gli, rc_, ALU.mult), reads=[rS2, r_tc], writes=[r_cc])
                            S.op("pool", lambda e: e.tensor_tensor(ini_im[:, gp0:gp0 + 2], cc[:, 4:6], cc[:, 6:8], ALU.add),
                                 reads=[r_cc], writes=[r_ini])
                        S.op("act", lambda e: e.activation(s1b[:], S1[:], AF.Copy), reads=[rS1], writes=[r_s1b])
                        S.op("act", lambda e: e.activation(s2b[:], S2[:], AF.Copy), reads=[rS2], writes=[r_s2b])

                    def f3():
                        ub, r_ub, u32, r_u32, yg, r_yg = get_tile(jt)
                        (_, _, (Ab, rAb), (Bb, rBb), (Cb, rCb), (Db, rDb), (s1b, r_s1b), (s2b, r_s2b)) = U["b"]
                        S.op("dve", lambda e: e.tensor_tensor(Ab[:], s1b[:], tcv, ALU.mult), reads=[r_s1b, r_tc], writes=[rAb])
                        S.op("pool", lambda e: e.tensor_tensor(Cb[:], s2b[:], tsv, ALU.mult), reads=[r_s2b, r_ts], writes=[rCb])
                        S.op("dve", lambda e: e.tensor_tensor(Bb[:], s1b[:], tsv, ALU.mult), reads=[r_s1b, r_ts], writes=[rBb])
                        S.op("pool", lambda e: e.tensor_tensor(Db[:], s2b[:], tcv, ALU.mult), reads=[r_s2b, r_tc], writes=[rDb])
                        if q == 0:
                            ystate[jt, c] = psY.next()
                        py, rpy = ystate[jt, c]
                        terms = ((Cpr, Ab, rAb), (Cprn, Cb, rCb), (Cpi, Bb, rBb), (Cpi, Db, rDb))
                        for g_ in range(2):
                            gp = gp0 + g_
                            for ti, (Wt, Xt, rXt) in enumerate(terms):
                                first = (q == 0 and g_ == 0 and ti == 0)
                                last = (q == 1 and g_ == 1 and ti == 3)
                                S.op("pe", lambda e, g_=g_, gp=gp, first=first, last=last, Wt=Wt, Xt=Xt: e.matmul(
                                    py[:, 0:LC], lhsT=Wt[:, gp, :], rhs=Xt[:, g_, :], start=first, stop=last),
                                    reads=[r_Cp, rXt], writes=[rpy])
                        if q == 1:
                            yv, r_yv = yvr.next()
                            S.op("dve", lambda e: e.scalar_tensor_tensor(
                                yv[:], u32[:, csl], dsb[:, jt:jt + 1], py[:, 0:LC], ALU.mult, ALU.add),
                                reads=[r_u32, r_dsb, rpy], writes=[r_yv])
                            S.op("act", lambda e: e.activation(yg[:, csl], yv[:], AF.Gelu_apprx_tanh),
                                 reads=[r_yv], writes=[r_yg])
                            del ystate[jt, c]
                            if c == NCH - 1:
                                S.dma("sp", lambda e: e.dma_start(out=yg_d[jt], in_=yg[:]), r_yg_d, reads=[r_yg])
                    return [f0, f1, f2, f3]

                ystate = {}
                units = [make_unit(jt, c, q) for jt in range(8) for c in range(NCH) for q in range(2)]
                NST = 4
                for step in range(len(units) + NST - 1):
                    for st_ in range(NST):
                        n = step - st_
                        if 0 <= n < len(units):
                            units[n][st_]()
            S.barrier()

        with contextlib.ExitStack() as st:
            wo = self.sb(st, pfx + "wo", [128, 8, 2 * D], BF16)
            r_wo = Res("wo")
            S.dma("pool", lambda e: e.dma_start(out=wo[:].rearrange("p j n -> p (j n)"), in_=wo_d[li]), r_wo)
            ybr = Rot([(self.sb(st, pfx + "yb%d" % i, [128, 8, TB], BF16), Res("yb")) for i in range(3)])
            xsr = Rot([(self.sb(st, pfx + "xs%d" % i, [128, 8, TB], F32), Res("xs")) for i in range(3)])
            sgr = Rot([(self.sb(st, pfx + "sg%d" % i, [128, TB], F32), Res("sg")) for i in range(2)])
            L = self.ln_setup(st, pfx)
            psA = Rot(self.ps[0:6])
            psB = Rot(self.ps[6:8])

            def ld(tb):
                yb, r_yb = ybr.next()
                xs, r_xs = xsr.next()
                yv_ = yg_d[:, :, tb * TB:(tb + 1) * TB].rearrange("c p t -> p c t")
                sv = src[:, :, tb * TB:(tb + 1) * TB].rearrange("c p t -> p c t")
                S.dma("sp", lambda e: e.dma_start(out=yb[:], in_=yv_), r_yb, reads=[r_yg_d])
                S.dma("sp", lambda e: e.dma_start(out=xs[:], in_=sv), r_xs, reads=[rsrc])
                return yb, r_yb, xs, r_xs
            nxt = ld(0)
            pend = []
            for tb in range(NTB):
                yb, r_yb, xs, r_xs = nxt
                if tb + 1 < NTB:
                    nxt = ld(tb + 1)
                for dc in range(NDC):
                    pv, rpv = psA.next()
                    pg, rpg = psA.next()
                    for half, (pp, rpp) in enumerate(((pv, rpv), (pg, rpg))):
                        for j in range(8):
                            S.op("pe", lambda e, pp=pp, j=j, dc=dc, yb=yb, half=half: e.matmul(
                                pp[:], lhsT=wo[:, j, half * D + dc * 128:half * D + (dc + 1) * 128], rhs=yb[:, j, :],
                                start=(j == 0), stop=(j == 7)), reads=[r_wo, r_yb], writes=[rpp])
                    sg, r_sg = sgr.next()
                    S.op("act", lambda e, sg=sg, pg=pg: e.activation(sg[:], pg[:], AF.Sigmoid), reads=[rpg], writes=[r_sg])
                    S.op("dve", lambda e, sg=sg, pv=pv: e.tensor_tensor(sg[:], pv[:], sg[:], ALU.mult),
                         reads=[rpv, r_sg], writes=[r_sg])
                    xv = xs[:, dc, :]
                    S.op("dve", lambda e, xv=xv, sg=sg: e.scalar_tensor_tensor(
                        xv, xv, ALPHA, sg[:], ALU.mult, ALU.add), reads=[r_sg, r_xs], writes=[r_xs])
                    for _ in range(2):
                        if pend:
                            pend.pop(0)()
                while pend:
                    pend.pop(0)()
                pend = self.ln_pieces(L, xs[:], r_xs, lnidx, LN_EPS, dst, rdst, tb * TB, psB.next, psB.next, k=tb % 2)
            while pend:
                pend.pop(0)()


def prep_ffn_weights(w_in_list, w_out_list):
    wins, wouts = [], []
    for w_in, w_out in zip(w_in_list, w_out_list):
        a = np.asarray(w_in).reshape(8, 128, 2, NFC, 128)
        a = a.transpose(3, 1, 2, 0, 4).reshape(NFC, 128, 2 * 8 * 128)
        wins.append(a)
        b = np.asarray(w_out).reshape(NFC, 128, NDC, 128)
        b = b.transpose(2, 1, 0, 3).reshape(NDC, 128, NFC * 128)
        wouts.append(b)
    return np.ascontiguousarray(np.stack(wins)), np.ascontiguousarray(np.stack(wouts))


def prep_lnp(ln1_g, ln1_b, lnm_g, lnm_b, ln2_g, ln2_b):
    arr = np.zeros((DEPTH, 3, 2, NDC, 128), np.float32)
    for l in range(DEPTH):
        for i, (g, b) in enumerate(((ln1_g, ln1_b), (lnm_g, lnm_b), (ln2_g, ln2_b))):
            arr[l, i, 0] = np.asarray(g[l]).reshape(NDC, 128)
            arr[l, i, 1] = np.asarray(b[l]).reshape(NDC, 128)
    return np.ascontiguousarray(arr.reshape(DEPTH * 3 * 2 * NDC, 128).T)


def x_to_dev(xb):
    T = xb.shape[0]
    return np.ascontiguousarray(np.asarray(xb).T.reshape(NDC, 128, T))


def x_from_dev(y):
    T = y.shape[-1]
    return np.ascontiguousarray(y.reshape(D, T).T)


def prep_fox_weights(w_in_list, b_f_list, w_o_list):
    wqkv, wf, bf, wo = [], [], [], []
    for w_in, b_f, w_o in zip(w_in_list, b_f_list, w_o_list):
        w_in = np.asarray(w_in)
        a = w_in[:, :3 * D].reshape(8, 128, 3, 8, 128)
        wqkv.append(a.transpose(3, 1, 2, 0, 4).reshape(8, 128, 3 * 8 * 128))
        f = w_in[:, 3 * D:].reshape(8, 128, NH)
        wf.append(f.transpose(1, 0, 2).reshape(128, 8 * NH))
        bf.append(np.asarray(b_f).reshape(NH, 1))
        o = np.asarray(w_o).reshape(8, 128, D)
        wo.append(o.transpose(1, 0, 2).reshape(128, 8 * D))
    c = np.ascontiguousarray
    return c(np.stack(wqkv)), c(np.stack(wf)), c(np.stack(bf)), c(np.stack(wo))


def prep_s5_weights(a_re_l, a_im_l, log_dt_l, b_re_l, b_im_l, c_re_l, c_im_l, d_l, w_out_l):
    pls, Bs, Cs, ds, wos = [], [], [], [], []
    for a_re, a_im, log_dt, b_re, b_im, c_re, c_im, d, w_out in zip(
            a_re_l, a_im_l, log_dt_l, b_re_l, b_im_l, c_re_l, c_im_l, d_l, w_out_l):
        def PL(a):
            return np.asarray(a).reshape(32, 2, 64).transpose(1, 2, 0).reshape(128, 32)
        ldt = np.repeat(np.asarray(log_dt).reshape(64, 1), 64, axis=1)
        pls.append(np.concatenate([PL(a_re), PL(a_im), PL(ldt)], axis=1))
        Bb = np.zeros((2, 32, 128, 128), np.float32)
        Cb = np.zeros((2, 32, 128, 128), np.float32)
        for k, (bb, cc) in enumerate(((b_re, c_re), (b_im, c_im))):
            bb = np.asarray(bb)
            cc = np.asarray(cc)
            for g in range(64):
                gp, g2, g8 = g // 2, g % 2, g % 8
                Bb[k, gp, g8 * 16:(g8 + 1) * 16, g2 * 64:(g2 + 1) * 64] = bb[g].T
                Cb[k, gp, g2 * 64:(g2 + 1) * 64, g8 * 16:(g8 + 1) * 16] = cc[g].T
        Bs.append(Bb.transpose(0, 2, 1, 3).reshape(2, 128, 32 * 128))
        Cs.append(Cb.transpose(0, 2, 1, 3).reshape(2, 128, 32 * 128))
        ds.append(np.asarray(d).reshape(8, 128).T)
        wos.append(np.asarray(w_out).reshape(8, 128, 2 * D).transpose(1, 0, 2).reshape(128, 8 * 2 * D))
    c = np.ascontiguousarray
    return c(np.stack(pls)), c(np.stack(Bs)), c(np.stack(Cs)), c(np.stack(ds)), c(np.stack(wos))


def full_plan():
    plan = []
    for l in range(DEPTH):
        plan.append(("ffn", l, 1))
        plan.append(("fox", l) if l % 2 == 0 else ("s5", l))
        plan.append(("ffn", l, 2))
    return plan


def kernel(x, ffn1_w_in, ffn1_w_out, ln1_g, ln1_b, lnm_g, lnm_b,
           ffn2_w_in, ffn2_w_out, ln2_g, ln2_b,
           fox_w_in, fox_b_f, fox_w_o,
           s5_a_re, s5_a_im, s5_log_dt, s5_b_re, s5_b_im, s5_c_re, s5_c_im,
           s5_d, s5_w_out):
    x = np.asarray(x, dtype=np.float32)
    nb, T, _ = x.shape
    mk = MK(T, full_plan())
    nc = mk.build()
    f32 = lambda a: np.asarray(a, dtype=np.float32)
    w_in_l, w_out_l = [], []
    for l in range(DEPTH):
        w_in_l += [f32(ffn1_w_in[l]), f32(ffn2_w_in[l])]
        w_out_l += [f32(ffn1_w_out[l]), f32(ffn2_w_out[l])]
    win, wout = prep_ffn_weights(w_in_l, w_out_l)
    lnp = prep_lnp(f32(ln1_g), f32(ln1_b), f32(lnm_g), f32(lnm_b), f32(ln2_g), f32(ln2_b))
    nA = fox_w_in.shape[0]
    wqkv, wf, bf, wo = prep_fox_weights([f32(fox_w_in[i]) for i in range(nA)],
                                        [f32(fox_b_f[i]) for i in range(nA)],
                                        [f32(fox_w_o[i]) for i in range(nA)])
    nB = s5_a_re.shape[0]
    L = lambda a: [f32(a[i]) for i in range(nB)]
    pl, Bs, Cs, ds, wos = prep_s5_weights(L(s5_a_re), L(s5_a_im), L(s5_log_dt), L(s5_b_re), L(s5_b_im),
                                          L(s5_c_re), L(s5_c_im), L(s5_d), L(s5_w_out))
    shared = {"lnp": lnp, "ffn_win": win, "ffn_wout": wout,
              "fox_wqkv": wqkv, "fox_wf": wf, "fox_bf": bf, "fox_wo": wo,
              "s5_pl": pl, "s5_B": Bs, "s5_C": Cs, "s5_d": ds, "s5_wo": wos}
    in_maps = []
    for b in range(nb):
        m = dict(shared)
        m["xT"] = x_to_dev(x[b])
        in_maps.append(m)
    res = run_bass_kernel_spmd(nc, in_maps, core_ids=list(range(nb)))
    out = np.stack([x_from_dev(np.asarray(r["yT"])) for r in res.results]).astype(np.float32)
    return out
```

```python
import contextlib
import math
import numpy as np
import concourse.bass as bass
import concourse.mybir as mybir
from concourse.bass_utils import run_bass_kernel_spmd

F32 = mybir.dt.float32
BF16 = mybir.dt.bfloat16
AF = mybir.ActivationFunctionType
ALU = mybir.AluOpType

D = 1024
NDC = 8
FF = 2816
NFC = 22
NH = 16
HD = 64
DEPTH = 4
ALPHA = (2.0 * DEPTH) ** 0.25
LN_EPS = 1e-5
TS = 1024
TB = 512


class Res:
    __slots__ = ("name", "lw", "readers", "slot", "epoch", "last_tok")

    def __init__(self, name=""):
        self.name = name
        self.lw = None
        self.readers = []
        self.slot = None
        self.epoch = -1
        self.last_tok = None


class Op:
    __slots__ = ("eng", "idx", "fn", "deps", "flag", "dma_tok", "waits")

    def __init__(self, eng, idx, fn):
        self.eng = eng
        self.idx = idx
        self.fn = fn
        self.deps = []
        self.flag = False
        self.dma_tok = None
        self.waits = None


COMPUTE = ("pe", "act", "dve", "pool")


class Sched:
    def __init__(self, nc):
        self.nc = nc
        self.ops = {e: [] for e in ("pe", "act", "dve", "pool", "sp")}
        self.slot_cnt = []
        self.slot_kind = []
        self.free_slots = []
        self.epoch = 0
        self.bar = {e: [] for e in self.ops}

    def _collect(self, op, reads, writes):
        deps = []
        for r in reads:
            if r.lw is not None:
                deps.append(r.lw)
        for w in writes:
            if w.lw is not None:
                deps.append(w.lw)
            deps.extend(w.readers)
        if self.bar[op.eng]:
            deps.extend(self.bar[op.eng])
            self.bar[op.eng] = []
        op.deps = deps
        for r in reads:
            r.readers.append(op)
        for w in writes:
            w.lw = op
            w.readers = []

    def op(self, eng, fn, reads=(), writes=()):
        o = Op(eng, len(self.ops[eng]), fn)
        self._collect(o, reads, writes)
        self.ops[eng].append(o)
        return o

    def dma(self, queue, fn, dst, reads=(), writes=()):
        o = Op(queue, len(self.ops[queue]), fn)
        ws = list(writes)
        if dst not in ws:
            ws.append(dst)
        self._collect(o, reads, ws)
        kind = "sw" if queue == "pool" else "hw"
        if dst.epoch != self.epoch or self.slot_kind[dst.slot] != kind:
            fl = [i for i in self.free_slots if self.slot_kind[i] == kind]
            if fl:
                dst.slot = fl[-1]
                self.free_slots.remove(dst.slot)
            else:
                dst.slot = len(self.slot_cnt)
                self.slot_cnt.append(0)
                self.slot_kind.append(kind)
            dst.epoch = self.epoch
        self.slot_cnt[dst.slot] += 16
        o.dma_tok = (dst.slot, self.slot_cnt[dst.slot])
        dst.last_tok = o.dma_tok
        self.ops[queue].append(o)
        return o

    def barrier(self):
        last = []
        for e, lst in self.ops.items():
            for o in reversed(lst):
                if o.dma_tok is None:
                    last.append(o)
                    break
        seen = set()
        for e, lst in self.ops.items():
            for o in reversed(lst):
                if o.dma_tok is not None and o.dma_tok[0] not in seen:
                    seen.add(o.dma_tok[0])
                    last.append(o)
        for e in self.ops:
            self.bar[e] = list(last)
        self.epoch += 1
        self.free_slots = list(range(len(self.slot_cnt)))

    def emit(self, final_wait=()):
        nc = self.nc
        for e, lst in self.ops.items():
            seen_eng = {}
            seen_dma = {}
            for o in lst:
                need_eng = {}
                need_dma = {}
                for d in o.deps:
                    if d.dma_tok is not None:
                        r, v = d.dma_tok
                        if seen_dma.get(r, 0) < v and need_dma.get(r, 0) < v:
                            need_dma[r] = v
                    else:
                        if d.eng == e and (e == "pe" or d.idx >= o.idx):
                            continue
                        if seen_eng.get(d.eng, -1) < d.idx and need_eng.get(d.eng, -1) < d.idx:
                            need_eng[d.eng] = d.idx
                for s, i in need_eng.items():
                    seen_eng[s] = i
                    self.ops[s][i].flag = True
                for r, v in need_dma.items():
                    seen_dma[r] = v
                o.waits = (need_eng, need_dma)
        cnt = {}
        for e, lst in self.ops.items():
            c = 0
            arr = []
            for o in lst:
                if o.flag:
                    c += 1
                arr.append(c)
            cnt[e] = arr
        with contextlib.ExitStack() as st:
            esem = {e: st.enter_context(nc.semaphore("s_" + e)) for e in COMPUTE}
            dsem = [st.enter_context(nc.semaphore("d%d" % i)) for i in range(len(self.slot_cnt))]
            block = st.enter_context(nc.Block())
            ops = self.ops

            def run(engname, eng):
                for o in ops[engname]:
                    need_eng, need_dma = o.waits
                    for s, i in need_eng.items():
                        eng.wait_ge(esem[s], cnt[s][i])
                    for r, v in need_dma.items():
                        eng.wait_ge(dsem[r], v)
                    ins = o.fn(eng)
                    if o.dma_tok is not None:
                        ins.then_inc(dsem[o.dma_tok[0]], 16)
                    elif o.flag:
                        ins.then_inc(esem[engname], 1)
                if engname == "sp":
                    for r in final_wait:
                        eng.wait_ge(dsem[r.last_tok[0]], r.last_tok[1])

            @block.tensor
            def _(eng):
                run("pe", eng)

            @block.scalar
            def _(eng):
                run("act", eng)

            @block.vector
            def _(eng):
                run("dve", eng)

            @block.gpsimd
            def _(eng):
                run("pool", eng)

            @block.sync
            def _(eng):
                run("sp", eng)


class Rot:
    def __init__(self, items):
        self.items = items
        self.i = 0

    def next(self):
        it = self.items[self.i % len(self.items)]
        self.i += 1
        return it


class Prefetch:
    def __init__(self):
        self.tasks = []
        self.issued = 0

    def add(self, fn):
        self.tasks.append(fn)
        return len(self.tasks) - 1

    def need(self, idx, ahead):
        upto = min(len(self.tasks), idx + 1 + ahead)
        while self.issued < upto:
            self.tasks[self.issued]()
            self.issued += 1


class MK:
    def __init__(self, T, plan):
        self.T = T
        self.plan = plan
        self.nc = nc = bass.Bass("TRN2", target_bir_lowering=False)
        self.S = Sched(nc)
        self.st = contextlib.ExitStack()
        dt = nc.dram_tensor
        self.x_in = dt("xT", [NDC, 128, T], F32, kind="ExternalInput").ap()
        self.y_out = dt("yT", [NDC, 128, T], F32, kind="ExternalOutput").ap()
        self.scr = [dt("scr%d" % i, [NDC, 128, T], F32, kind="Internal").ap() for i in range(2)]
        self.lnp_d = dt("lnp", [128, DEPTH * 3 * 2 * NDC], F32, kind="ExternalInput").ap()
        self.w = {}
        need = set(p[0] for p in plan)
        self.nffn = 1 + max([p[1] * 2 + p[2] - 1 for p in plan if p[0] == "ffn"], default=-1)
        self.nfox = 1 + max([p[1] // 2 for p in plan if p[0] == "fox"], default=-1)
        self.ns5 = 1 + max([p[1] // 2 for p in plan if p[0] == "s5"], default=-1)
        if "ffn" in need:
            self.w["ffn_win"] = dt("ffn_win", [self.nffn, NFC, 128, 2 * 8 * 128], F32, kind="ExternalInput").ap()
            self.w["ffn_wout"] = dt("ffn_wout", [self.nffn, NDC, 128, NFC * 128], F32, kind="ExternalInput").ap()
        if "fox" in need:
            self.w["fox_wqkv"] = dt("fox_wqkv", [self.nfox, 8, 128, 3 * 8 * 128], F32, kind="ExternalInput").ap()
            self.w["fox_wf"] = dt("fox_wf", [self.nfox, 128, 8 * NH], F32, kind="ExternalInput").ap()
            self.w["fox_bf"] = dt("fox_bf", [self.nfox, NH, 1], F32, kind="ExternalInput").ap()
            self.w["fox_wo"] = dt("fox_wo", [self.nfox, 128, 8 * D], F32, kind="ExternalInput").ap()
            self.cs_d = dt("cs_d", [NH, 6, T], BF16, kind="Internal").ap()
            self.o_d = dt("o_d", [8, 128, T], BF16, kind="Internal").ap()
            self.r_cs_d = Res("cs_d")
            self.r_o_d = Res("o_d")
        if "s5" in need:
            self.w["s5_pl"] = dt("s5_pl", [self.ns5, 128, 96], F32, kind="ExternalInput").ap()
            self.w["s5_B"] = dt("s5_B", [self.ns5, 2, 128, 32 * 128], F32, kind="ExternalInput").ap()
            self.w["s5_C"] = dt("s5_C", [self.ns5, 2, 128, 32 * 128], F32, kind="ExternalInput").ap()
            self.w["s5_d"] = dt("s5_d", [self.ns5, 128, 8], F32, kind="ExternalInput").ap()
            self.w["s5_wo"] = dt("s5_wo", [self.ns5, 128, 8 * 2 * D], F32, kind="ExternalInput").ap()
            self.yg_d = dt("yg_d", [8, 128, T], BF16, kind="Internal").ap()
            self.r_yg_d = Res("yg_d")
        self.r_x_in = Res("x_in")
        self.r_y = Res("y_out")
        self.r_scr = [Res("scr0"), Res("scr1")]

    def sb(self, st, name, shape, dtype):
        return st.enter_context(self.nc.sbuf_tensor(name, shape, dtype))

    def build(self):
        nc, S = self.nc, self.S
        with contextlib.ExitStack() as st:
            self.ps = []
            for i in range(8):
                t = st.enter_context(nc.psum_tensor("ps%d" % i, [128, TB], F32))
                self.ps.append((t, Res("ps%d" % i)))
            self.lnp = self.sb(st, "lnp_sb", [128, DEPTH * 3 * 2 * NDC], F32)
            self.r_lnp = Res("lnp")
            S.dma("sp", lambda e: e.dma_start(out=self.lnp[:], in_=self.lnp_d), self.r_lnp)
            self.ones = self.sb(st, "ones_bf", [128, 128], BF16)
            self.r_ones = Res("ones")
            S.op("dve", lambda e: e.memset(self.ones[:], 1.0), writes=[self.r_ones])

            cur, rcur = self.x_in, self.r_x_in
            for pi, ph in enumerate(self.plan):
                last = pi == len(self.plan) - 1
                if last:
                    dst, rdst = self.y_out, self.r_y
                else:
                    dst, rdst = self.scr[pi % 2], self.r_scr[pi % 2]
                if ph[0] == "ffn":
                    self.ffn_phase(cur, rcur, dst, rdst, ph[1], ph[2])
                elif ph[0] == "fox":
                    self.fox_phase(cur, rcur, dst, rdst, ph[1])
                elif ph[0] == "s5":
                    self.s5_phase(cur, rcur, dst, rdst, ph[1])
                S.barrier()
                cur, rcur = dst, rdst
            S.emit(final_wait=[self.r_y])
        return nc

    def ln_setup(self, st, pfx):
        L = {}
        L["wb"] = self.sb(st, pfx + "wb", [128, NDC, TB], BF16)
        L["w2b"] = self.sb(st, pfx + "w2b", [128, NDC, TB], BF16)
        L["mean"] = self.sb(st, pfx + "mean", [128, TB], F32)
        L["msq"] = self.sb(st, pfx + "msq", [128, TB], F32)
        L["var"] = self.sb(st, pfx + "var", [128, TB], F32)
        L["rstd"] = self.sb(st, pfx + "rstd", [128, TB], F32)
        for k in ("mean2", "msq2", "var2", "rstd2"):
            L[k] = self.sb(st, pfx + k, [128, TB], F32)
        for k in ("wb", "w2b", "mean", "msq", "var", "rstd", "mean2", "msq2", "var2", "rstd2"):
            L["r_" + k] = Res(pfx + k)
        return L

    def ln_block(self, L, wv, r_w, lnidx, eps, dst, rdst, tok0, psS, psQ, k=0):
        for f in self.ln_pieces(L, wv, r_w, lnidx, eps, dst, rdst, tok0, psS, psQ, k):
            f()

    def ln_pieces(self, L, wv, r_w, lnidx, eps, dst, rdst, tok0, psS, psQ, k=0):
        S = self.S
        wb, w2b = L["wb"], L["w2b"]
        sfx = "" if k == 0 else "2"
        mean, msq, var, rstd = L["mean" + sfx], L["msq" + sfx], L["var" + sfx], L["rstd" + sfx]
        r_mean, r_msq, r_var, r_rstd = L["r_mean" + sfx], L["r_msq" + sfx], L["r_var" + sfx], L["r_rstd" + sfx]
        r_dc = [Res("wdc") for _ in range(NDC)]

        def conv():
            S.op("dve", lambda e: e.tensor_copy(wb[:], wv), reads=[r_w], writes=[L["r_wb"]])
            S.op("act", lambda e: e.activation(w2b[:], wv, AF.Square), reads=[r_w], writes=[L["r_w2b"]])

        def stats():
            (pS, rS), (pQ, rQ) = psS(), psQ()
            for dc in range(NDC):
                S.op("pe", lambda e, dc=dc: e.matmul(pS[:], lhsT=self.ones[:], rhs=wb[:, dc, :],
                                                      start=(dc == 0), stop=(dc == NDC - 1)),
                     reads=[L["r_wb"], self.r_ones], writes=[rS])
            for dc in range(NDC):
                S.op("pe", lambda e, dc=dc: e.matmul(pQ[:], lhsT=self.ones[:], rhs=w2b[:, dc, :],
                                                      start=(dc == 0), stop=(dc == NDC - 1)),
                     reads=[L["r_w2b"], self.r_ones], writes=[rQ])
            S.op("dve", lambda e: e.tensor_scalar(mean[:], pS[:], 1.0 / D, None, ALU.mult),
                 reads=[rS], writes=[r_mean])
            S.op("dve", lambda e: e.tensor_tensor(msq[:], mean[:], mean[:], ALU.mult),
                 reads=[r_mean], writes=[r_msq])
            S.op("dve", lambda e: e.tensor_scalar(var[:], pQ[:], 1.0 / D, eps, ALU.mult, ALU.add),
                 reads=[rQ], writes=[r_var])
            S.op("dve", lambda e: e.tensor_tensor(var[:], var[:], msq[:], ALU.subtract),
                 reads=[r_var, r_msq], writes=[r_var])
            S.op("act", lambda e: e.activation(rstd[:], var[:], AF.Ln), reads=[r_var], writes=[r_rstd])
            S.op("act", lambda e: e.activation(rstd[:], rstd[:], AF.Exp, scale=-0.5),
                 reads=[r_rstd], writes=[r_rstd])

        gofs = (lnidx * 2) * NDC
        bofs = (lnidx * 2 + 1) * NDC

        def norm(dc):
            v = wv[:, dc, :]
            S.op("dve", lambda e: e.tensor_tensor(v, v, mean[:], ALU.subtract),
                 reads=[r_w, r_mean], writes=[r_dc[dc]])
            S.op("dve", lambda e: e.tensor_tensor(v, v, rstd[:], ALU.mult),
                 reads=[r_dc[dc], r_rstd], writes=[r_dc[dc]])
            g = self.lnp[:, gofs + dc:gofs + dc + 1]
            b = self.lnp[:, bofs + dc:bofs + dc + 1]
            S.op("act", lambda e: e.activation(v, v, AF.Identity, bias=b, scale=g),
                 reads=[r_dc[dc], self.r_lnp], writes=[r_dc[dc]])
            if dc == NDC - 1:
                dv = dst[:, :, tok0:tok0 + TB].rearrange("c p t -> p c t")
                S.dma("sp", lambda e: e.dma_start(out=dv, in_=wv), rdst, reads=r_dc, writes=[r_w])
        return [conv, stats] + [(lambda dc=dc: norm(dc)) for dc in range(NDC)]

    def ffn_phase(self, src, rsrc, dst, rdst, layer, which):
        S, T = self.S, self.T
        widx = layer * 2 + (which - 1)
        lnidx = layer * 3 + (0 if which == 1 else 2)
        win = self.w["ffn_win"]
        wout = self.w["ffn_wout"]
        nsb = T // TS
        ntb = TS // TB
        with contextlib.ExitStack() as st:
            pfx = "f%d_" % widx
            xs = self.sb(st, pfx + "xs", [128, NDC, TS], F32)
            r_xs = [Res("xs%d" % i) for i in range(ntb)]
            xbf = [(self.sb(st, pfx + "xbf%d" % i, [128, NDC, TS], BF16), [Res("xbf%d" % i) for _ in range(ntb)])
                   for i in range(2)]
            aT = self.sb(st, pfx + "aT", [128, NFC, TS], BF16)
            r_aT = [[Res("aT") for _ in range(ntb)] for _ in range(NFC)]
            wgu = Rot([(self.sb(st, pfx + "wgu%d" % i, [128, 2, 8, 128], BF16), Res("wgu%d" % i)) for i in range(4)])
            wo = Rot([(self.sb(st, pfx + "wo%d" % i, [128, NFC, 128], BF16), Res("wo%d" % i)) for i in range(3)])
            sg = Rot([(self.sb(st, pfx + "sg%d" % i, [128, TB], BF16), Res("sg%d" % i)) for i in range(4)])
            L = self.ln_setup(st, pfx)
            psA = Rot(self.ps[0:6])
            psB = Rot(self.ps[6:8])

            pf = Prefetch()
            plan = []
            for sbk in range(nsb):
                info = {}
                t0 = sbk * TS
                xb, r_xb = xbf[sbk % 2]
                info["xbf"] = (xb, r_xb)

                def ld_xbf(xb=xb, r_xb=r_xb, t0=t0):
                    for tb in range(ntb):
                        sv = src[:, :, t0 + tb * TB:t0 + (tb + 1) * TB].rearrange("c p t -> p c t")
                        S.dma("pool", lambda e, sv=sv, tb=tb: e.dma_start(out=xb[:, :, tb * TB:(tb + 1) * TB], in_=sv),
                              r_xb[tb], reads=[rsrc])
                info["t_xbf"] = pf.add(ld_xbf)
                info["wgu"] = []
                for fc in range(NFC):
                    buf, rb = wgu.next()

                    def ld_w(buf=buf, rb=rb, fc=fc):
                        S.dma("pool", lambda e: e.dma_start(
                            out=buf[:].rearrange("p a k m -> p (a k m)"), in_=win[widx, fc]), rb)
                    info["wgu"].append((pf.add(ld_w), buf, rb))
                    if fc == 1:
                        def ld_xs(t0=t0):
                            for tb in range(ntb):
                                sv = src[:, :, t0 + tb * TB:t0 + (tb + 1) * TB].rearrange("c p t -> p c t")
                                S.dma("sp", lambda e, sv=sv, tb=tb: e.dma_start(
                                    out=xs[:, :, tb * TB:(tb + 1) * TB], in_=sv), r_xs[tb], reads=[rsrc])
                        info["ld_xs"] = ld_xs
                info["wo"] = []
                for dc in range(NDC):
                    buf, rb = wo.next()

                    def ld_wo(buf=buf, rb=rb, dc=dc):
                        S.dma("pool", lambda e: e.dma_start(
                            out=buf[:].rearrange("p f m -> p (f m)"), in_=wout[widx, dc]), rb)
                    info["wo"].append((pf.add(ld_wo), buf, rb))
                plan.append(info)

            pending_ln = []
            for sbk in range(nsb):
                info = plan[sbk]
                t0 = sbk * TS
                xb, r_xb = info["xbf"]
                pf.need(info["t_xbf"], 2)
                for fc in range(NFC):
                    tid, wbuf, rwb = info["wgu"][fc]
                    pf.need(tid, 3)
                    if fc >= 1:
                        for _ in range(2):
                            if pending_ln:
                                pending_ln.pop(0)()
                    if fc == 12:
                        assert not pending_ln
                        info["ld_xs"]()
                    for tb in range(ntb):
                        ts_ = slice(tb * TB, (tb + 1) * TB)
                        pg, rpg = psA.next()
                        pu, rpu = psA.next()
                        for gu, (pp, rpp) in enumerate(((pg, rpg), (pu, rpu))):
                            for kc in range(8):
                                S.op("pe", lambda e, pp=pp, gu=gu, kc=kc, ts_=ts_, wbuf=wbuf, xb=xb: e.matmul(
                                    pp[:], lhsT=wbuf[:, gu, kc, :], rhs=xb[:, kc, ts_],
                                    start=(kc == 0), stop=(kc == 7)),
                                    reads=[rwb, r_xb[tb]], writes=[rpp])
                        sgb, rsg = sg.next()
                        S.op("act", lambda e, sgb=sgb, pg=pg: e.activation(sgb[:], pg[:], AF.Silu),
                             reads=[rpg], writes=[rsg])
                        S.op("dve", lambda e, sgb=sgb, pu=pu, fc=fc, ts_=ts_: e.tensor_tensor(
                            aT[:, fc, ts_], sgb[:], pu[:], ALU.mult),
                            reads=[rsg, rpu], writes=[r_aT[fc][tb]])
                for dc in range(NDC):
                    tid, wbuf, rwb = info["wo"][dc]
                    pf.need(tid, 2)
                    for tb in range(ntb):
                        ts_ = slice(tb * TB, (tb + 1) * TB)
                        py, rpy = psA.next()
                        for fc in range(NFC):
                            S.op("pe", lambda e, py=py, fc=fc, ts_=ts_, wbuf=wbuf: e.matmul(
                                py[:], lhsT=wbuf[:, fc, :], rhs=aT[:, fc, ts_],
                                start=(fc == 0), stop=(fc == NFC - 1)),
                                reads=[rwb, r_aT[fc][tb]], writes=[rpy])
                        xv = xs[:, dc, ts_]
                        S.op("dve", lambda e, xv=xv, py=py: e.scalar_tensor_tensor(
                            xv, xv, 2.0 * ALPHA, py[:], ALU.mult, ALU.add),
                            reads=[rpy, r_xs[tb]], writes=[r_xs[tb]])
                while pending_ln:
                    pending_ln.pop(0)()
                for tb in range(ntb):
                    ts_ = slice(tb * TB, (tb + 1) * TB)
                    pcs = self.ln_pieces(L, xs[:, :, ts_], r_xs[tb], lnidx, 4.0 * LN_EPS, dst, rdst,
                                         t0 + tb * TB, psB.next, psB.next, k=tb % 2)
                    if tb == 0:
                        first = pcs
                    else:
                        pending_ln.extend([first[0], first[1], pcs[0], first[2], pcs[1]] + first[3:] + pcs[2:])
                if sbk + 1 < nsb:
                    pf.need(plan[sbk + 1]["t_xbf"], 2)
            while pending_ln:
                pending_ln.pop(0)()


    def fox_phase(self, src, rsrc, dst, rdst, layer):
        S, T, nc = self.S, self.T, self.nc
        li = layer // 2
        lnidx = layer * 3 + 1
        NTB = T // TB
        NKB = T // 128
        wqkv, wf_d, bf_d, wo_d = self.w["fox_wqkv"], self.w["fox_wf"], self.w["fox_bf"], self.w["fox_wo"]
        cs_d, o_d, r_cs_d, r_o_d = self.cs_d, self.o_d, self.r_cs_d, self.r_o_d
        pfx = "x%d_" % layer

        stx = contextlib.ExitStack()
        xb_all = self.sb(stx, pfx + "xball", [128, 8, T], BF16)
        r_xall = [Res("xall") for _ in range(NTB)]
        for tb in range(NTB):
            sv = src[:, :, tb * TB:(tb + 1) * TB].rearrange("c p t -> p c t")
            S.dma("pool", lambda e, sv=sv, tb=tb: e.dma_start(out=xb_all[:, :, tb * TB:(tb + 1) * TB], in_=sv),
                  r_xall[tb], reads=[rsrc])

        with contextlib.ExitStack() as st:
            wf = self.sb(st, pfx + "wf", [128, 8, NH], BF16)
            r_wf = Res("wf")
            bf = self.sb(st, pfx + "bf", [NH, 1], F32)
            r_bf = Res("bf")
            lsp = self.sb(st, pfx + "lsp", [NH, T], F32)
            r_lsp = Res("lsp")
            ncm = self.sb(st, pfx + "ncm", [NH, T], F32)
            r_ncm = Res("ncm")
            one = self.sb(st, pfx + "one", [NH, T], F32)
            r_one = Res("one")
            r1 = self.sb(st, pfx + "r1", [NH, T], F32)
            r_r1 = Res("r1")
            csb = self.sb(st, pfx + "csb", [NH, 6, T], BF16)
            r_csb = Res("csb")
            etmp = Rot([(self.sb(st, pfx + "etmp%d" % i, [NH, TB], F32), Res("etmp")) for i in range(2)])
            S.dma("pool", lambda e: e.dma_start(out=wf[:].rearrange("p k h -> p (k h)"), in_=wf_d[li]), r_wf)
            S.dma("sp", lambda e: e.dma_start(out=bf[:], in_=bf_d[li]), r_bf)
            S.op("dve", lambda e: e.tensor_scalar(bf[:], bf[:], -1.0, None, ALU.mult), reads=[r_bf], writes=[r_bf])
            S.op("dve", lambda e: e.memset(one[:], 1.0), writes=[r_one])
            psr = Rot(self.ps[0:2])
            for tb in range(NTB):
                xb, r_xb = xb_all[:, :, tb * TB:(tb + 1) * TB], r_xall[tb]
                pp, rpp = psr.next()
                for kc in range(8):
                    S.op("pe", lambda e, pp=pp, kc=kc, xb=xb: e.matmul(
                        pp[0:NH, :], lhsT=wf[:, kc, :], rhs=xb[:, kc, :], start=(kc == 0), stop=(kc == 7)),
                        reads=[r_wf, r_xb], writes=[rpp])
                et, ret = etmp.next()
                S.op("act", lambda e, et=et, pp=pp: e.activation(et[:], pp[0:NH, :], AF.Exp, bias=bf[:, 0:1], scale=-1.0),
                     reads=[rpp, r_bf], writes=[ret])
                S.op("act", lambda e, et=et, tb=tb: e.activation(lsp[:, tb * TB:(tb + 1) * TB], et[:], AF.Ln, bias=1.0),
                     reads=[ret], writes=[r_lsp])
            S.op("dve", lambda e: e.tensor_tensor_scan(ncm[:], one[:], lsp[:], 0.0, ALU.mult, ALU.add),
                 reads=[r_one, r_lsp], writes=[r_ncm])
            S.op("dve", lambda e: e.tensor_copy(csb[:, 3, :], ncm[:]), reads=[r_ncm], writes=[r_csb])
            S.op("dve", lambda e: e.tensor_tensor(r1[:], ncm[:], csb[:, 3, :], ALU.subtract),
                 reads=[r_ncm, r_csb], writes=[r_r1])
            S.op("dve", lambda e: e.tensor_copy(csb[:, 4, :], r1[:]), reads=[r_r1], writes=[r_csb])
            S.op("dve", lambda e: e.tensor_tensor(r1[:], r1[:], csb[:, 4, :], ALU.subtract),
                 reads=[r_r1, r_csb], writes=[r_r1])
            S.op("dve", lambda e: e.tensor_copy(csb[:, 5, :], r1[:]), reads=[r_r1], writes=[r_csb])
            S.op("dve", lambda e: e.tensor_scalar(csb[:, 0:3, :], csb[:, 3:6, :], -1.0, None, ALU.mult),
                 reads=[r_csb], writes=[r_csb])
            S.dma("sp", lambda e: e.dma_start(out=cs_d, in_=csb[:]), r_cs_d, reads=[r_csb])
        S.barrier()

        with contextlib.ExitStack() as st:
            wq = Rot([(self.sb(st, pfx + "wq%d" % i, [128, 3, 8, 128], BF16), Res("wq")) for i in range(2)])
            QK = {}
            for nm in ("QA", "QB", "KA", "KB"):
                QK[nm] = (self.sb(st, pfx + nm, [128, T], BF16), Res(nm))
            VA = self.sb(st, pfx + "VA", [128, NKB, 128], BF16)
            VB = self.sb(st, pfx + "VB", [128, NKB, 128], BF16)
            r_VA, r_VB = Res("VA"), Res("VB")
            Eb = Rot([(self.sb(st, pfx + "E%d" % i, [128, TB], BF16), Res("E")) for i in range(6)])
            oT = Rot([(self.sb(st, pfx + "oT%d" % i, [128, T], BF16), Res("oT")) for i in range(2)])
            rc = Rot([(self.sb(st, pfx + "rc%d" % i, [128, TB], F32), Res("rc")) for i in range(2)])
            tri = self.sb(st, pfx + "tri", [128, 128], BF16)
            r_tri = Res("tri")
            S.op("pool", lambda e: e.memset(tri[:], 1.0), writes=[r_tri])
            S.op("pool", lambda e: e.affine_select(tri[:], tri[:], [[1, 128]], ALU.is_ge, 0.0, base=0,
                                                   channel_multiplier=-1), reads=[r_tri], writes=[r_tri])
            S.op("dve", lambda e: e.memset(VA[:], 1.0), writes=[r_VA])
            S.op("dve", lambda e: e.memset(VB[:], 1.0), writes=[r_VB])
            for nm in ("QA", "QB", "KA", "KB"):
                t_, r_ = QK[nm]
                S.op("dve", lambda e, t_=t_: e.memset(t_[64:70, :], 1.0), writes=[r_])
            psS = Rot(self.ps[0:4])
            psO = Rot(self.ps[4:6])
            psP = Rot(self.ps[6:8])

            import os
            dbg = os.environ.get("FOXDBG", "")
            for j in range(0 if dbg == "noA3" else 8):
                wqb, r_wq = wq.next()
                S.dma("pool", lambda e, wqb=wqb, j=j: e.dma_start(
                    out=wqb[:].rearrange("p s k m -> p (s k m)"), in_=wqkv[li, j]), r_wq)
                for hh, (qn, kn) in enumerate((("QA", "KA"), ("QB", "KB"))):
                    h = 2 * j + hh
                    qt, rq = QK[qn]
                    kt, rk = QK[kn]
                    S.dma("sp", lambda e, qt=qt, h=h: e.dma_start(out=qt[64:67, :], in_=cs_d[h, 0:3, :]),
                          rq, reads=[r_cs_d])
                    S.dma("sp", lambda e, kt=kt, h=h: e.dma_start(out=kt[67:70, :], in_=cs_d[h, 3:6, :]),
                          rk, reads=[r_cs_d])
                for tb in range(NTB):
                    xb, r_xb = xb_all[:, :, tb * TB:(tb + 1) * TB], r_xall[tb]
                    tsl = slice(tb * TB, (tb + 1) * TB)
                    for s_, (na, nb_) in enumerate((("QA", "QB"), ("KA", "KB"))):
                        pp, rpp = psP.next()
                        for kc in range(8):
                            S.op("pe", lambda e, pp=pp, kc=kc, xb=xb, s_=s_, wqb=wqb: e.matmul(
                                pp[:], lhsT=wqb[:, s_, kc, :], rhs=xb[:, kc, :], start=(kc == 0), stop=(kc == 7)),
                                reads=[r_wq, r_xb], writes=[rpp])
                        ta, ra = QK[na]
                        tb_, rb2 = QK[nb_]
                        sc = 0.125 if s_ == 0 else 1.0
                        S.op("dve", lambda e, ta=ta, pp=pp, sc=sc, tsl=tsl: e.tensor_scalar(
                            ta[0:64, tsl], pp[0:64, :], sc, None, ALU.mult), reads=[rpp], writes=[ra])
                        S.op("dve", lambda e, tb_=tb_, pp=pp, sc=sc, tsl=tsl: e.tensor_scalar(
                            tb_[0:64, tsl], pp[64:128, :], sc, None, ALU.mult), reads=[rpp], writes=[rb2])
                    pp, rpp = psP.next()
                    for i4 in range(4):
                        for kc in range(8):
                            S.op("pe", lambda e, pp=pp, kc=kc, xb=xb, i4=i4, wqb=wqb: e.matmul(
                                pp[:, i4 * 128:(i4 + 1) * 128], lhsT=xb[:, kc, i4 * 128:(i4 + 1) * 128],
                                rhs=wqb[:, 2, kc, :], start=(kc == 0), stop=(kc == 7)),
                                reads=[r_wq, r_xb], writes=[rpp])
                    pv = pp[:].rearrange("p (a m) -> p a m", a=4)
                    S.op("dve", lambda e, pv=pv, tb=tb: e.tensor_copy(VA[:, tb * 4:(tb + 1) * 4, 0:64], pv[:, :, 0:64]),
                         reads=[rpp], writes=[r_VA])
                    S.op("dve", lambda e, pv=pv, tb=tb: e.tensor_copy(VB[:, tb * 4:(tb + 1) * 4, 64:128], pv[:, :, 64:128]),
                         reads=[rpp], writes=[r_VB])
                oTb, r_oT = oT.next()
                steps = []
                for hh in range(2):
                    for qb in range(NTB):
                        nkb = 4 * (qb + 1)
                        for kb in range(nkb):
                            steps.append((hh, qb, kb, nkb))
                state = {}

                def emit_S(i):
                    hh, qb, kb, nkb = steps[i]
                    qt, rq = QK["QA" if hh == 0 else "QB"]
                    kt, rk = QK["KA" if hh == 0 else "KB"]
                    r = kb - 4 * qb
                    c0 = 128 * r if r > 0 else 0
                    ps_, rps = psS.next()
                    S.op("pe", lambda e: e.matmul(ps_[:, c0:TB], lhsT=kt[0:70, kb * 128:(kb + 1) * 128],
                                                  rhs=qt[0:70, qb * TB + c0:(qb + 1) * TB], start=True, stop=True),
                         reads=[rq, rk], writes=[rps])
                    state[i] = (ps_, rps, c0, r)

                def emit_rest(i, oTb=oTb, r_oT=r_oT):
                    hh, qb, kb, nkb = steps[i]
                    ps_, rps, c0, r = state.pop(i)
                    Et, rE = Eb.next()
                    S.op("act", lambda e: e.activation(Et[:, c0:TB], ps_[:, c0:TB], AF.Exp),
                         reads=[rps], writes=[rE])
                    if r >= 0:
                        S.op("dve", lambda e: e.tensor_tensor(Et[:, c0:c0 + 128], Et[:, c0:c0 + 128], tri[:], ALU.mult),
                             reads=[rE, r_tri], writes=[rE])
                    if kb == 0:
                        state["o", hh, qb] = psO.next()
                    po, rpo = state["o", hh, qb]
                    Vt, rV = (VA, r_VA) if hh == 0 else (VB, r_VB)
                    S.op("pe", lambda e: e.matmul(po[:, c0:TB], lhsT=Vt[:, kb, :], rhs=Et[:, c0:TB],
                                                  start=(kb == 0), stop=(kb == nkb - 1), skip_group_check=True),
                         reads=[rV, rE], writes=[rpo])
                    if kb == nkb - 1:
                        rct, rrc = rc.next()
                        orow = slice(0, 64) if hh == 0 else slice(64, 128)
                        drow = slice(64, 128) if hh == 0 else slice(0, 64)
                        S.op("dve", lambda e: e.reciprocal(rct[drow, :], po[drow, :]), reads=[rpo], writes=[rrc])
                        S.op("dve", lambda e: e.tensor_tensor(oTb[orow, qb * TB:(qb + 1) * TB], po[orow, :], rct[drow, :], ALU.mult),
                             reads=[rpo, rrc], writes=[r_oT])
                        del state["o", hh, qb]

                LOOK = 3
                if dbg == "noattn":
                    steps = []
                n = len(steps)
                for i in range(min(LOOK, n)):
                    emit_S(i)
                for i in range(n):
                    if i + LOOK < n:
                        emit_S(i + LOOK)
                    emit_rest(i)
                S.dma("sp", lambda e, oTb=oTb, j=j: e.dma_start(out=o_d[j], in_=oTb[:]), r_o_d, reads=[r_oT])
        S.barrier()

        with contextlib.ExitStack() as st:
            wo = self.sb(st, pfx + "wo", [128, 8, D], BF16)
            r_wo = Res("wo")
            S.dma("pool", lambda e: e.dma_start(out=wo[:].rearrange("p j n -> p (j n)"), in_=wo_d[li]), r_wo)
            obr = Rot([(self.sb(st, pfx + "ob%d" % i, [128, 8, TB], BF16), Res("ob")) for i in range(3)])
            xsr = Rot([(self.sb(st, pfx + "xs%d" % i, [128, 8, TB], F32), Res("xs")) for i in range(3)])
            L = self.ln_setup(st, pfx)
            psA = Rot(self.ps[0:6])
            psB = Rot(self.ps[6:8])

            def ld(tb):
                ob, r_ob = obr.next()
                xs, r_xs = xsr.next()
                ov = o_d[:, :, tb * TB:(tb + 1) * TB].rearrange("c p t -> p c t")
                sv = src[:, :, tb * TB:(tb + 1) * TB].rearrange("c p t -> p c t")
                S.dma("sp", lambda e: e.dma_start(out=ob[:], in_=ov), r_ob, reads=[r_o_d])
                S.dma("sp", lambda e: e.dma_start(out=xs[:], in_=sv), r_xs, reads=[rsrc])
                return ob, r_ob, xs, r_xs
            nxt = ld(0)
            pend = []
            for tb in range(NTB):
                ob, r_ob, xs, r_xs = nxt
                if tb + 1 < NTB:
                    nxt = ld(tb + 1)
                for dc in range(NDC):
                    pp, rpp = psA.next()
                    for j in range(8):
                        S.op("pe", lambda e, pp=pp, j=j, dc=dc, ob=ob: e.matmul(
                            pp[:], lhsT=wo[:, j, dc * 128:(dc + 1) * 128], rhs=ob[:, j, :],
                            start=(j == 0), stop=(j == 7)), reads=[r_wo, r_ob], writes=[rpp])
                    xv = xs[:, dc, :]
                    S.op("dve", lambda e, xv=xv, pp=pp: e.scalar_tensor_tensor(
                        xv, xv, ALPHA, pp[:], ALU.mult, ALU.add), reads=[rpp, r_xs], writes=[r_xs])
                    for _ in range(2):
                        if pend:
                            pend.pop(0)()
                while pend:
                    pend.pop(0)()
                pend = self.ln_pieces(L, xs[:], r_xs, lnidx, LN_EPS, dst, rdst, tb * TB, psB.next, psB.next, k=tb % 2)
            while pend:
                pend.pop(0)()
        stx.close()

    def rr_sin(self, out, ph, nf, ni, shift, r_out, r_ph, r_nf, r_ni, extra=None):
        if extra is not None:
            self.rr_sin(extra[0], extra[1], extra[2], extra[3], shift, r_out, r_ph, r_nf, r_ni)
        S = self.S
        TWO_PI = 2.0 * math.pi
        if shift != 0.0:
            S.op("dve", lambda e: e.tensor_scalar(ph, ph, shift, None, ALU.add), reads=[r_ph], writes=[r_ph])
        S.op("dve", lambda e: e.tensor_scalar(nf, ph, 1.0 / TWO_PI, None, ALU.mult), reads=[r_ph], writes=[r_nf])
        S.op("dve", lambda e: e.tensor_copy(ni, nf), reads=[r_nf], writes=[r_ni])
        S.op("dve", lambda e: e.tensor_copy(nf, ni), reads=[r_ni], writes=[r_nf])
        S.op("dve", lambda e: e.scalar_tensor_tensor(ph, nf, -TWO_PI, ph, ALU.mult, ALU.add),
             reads=[r_nf, r_ph], writes=[r_ph])
        S.op("dve", lambda e: e.tensor_scalar(nf, ph, math.pi, TWO_PI, ALU.is_gt, ALU.mult),
             reads=[r_ph], writes=[r_nf])
        S.op("dve", lambda e: e.tensor_tensor(ph, ph, nf, ALU.subtract), reads=[r_nf, r_ph], writes=[r_ph])
        PS = 3.141592
        S.op("dve", lambda e: e.tensor_scalar(ph, ph, PS, -PS, ALU.min, ALU.max), reads=[r_ph], writes=[r_ph])
        S.op("act", lambda e: e.activation(out, ph, AF.Sin), reads=[r_ph], writes=[r_out])

    def s5_phase(self, src, rsrc, dst, rdst, layer):
        S, T, nc = self.S, self.T, self.nc
        li = layer // 2
        lnidx = layer * 3 + 1
        LC = 256
        NCH = T // LC
        NTB = T // TB
        I32 = mybir.dt.int32
        pl_d, B_d, C_d, dd_d, wo_d = (self.w["s5_pl"], self.w["s5_B"], self.w["s5_C"],
                                      self.w["s5_d"], self.w["s5_wo"])
        yg_d, r_yg_d = self.yg_d, self.r_yg_d
        pfx = "s%d_" % layer
        with contextlib.ExitStack() as st0:
            tc = self.sb(st0, pfx + "tc", [128, 32, LC], BF16)
            ts = self.sb(st0, pfx + "ts", [128, 32, LC], BF16)
            r_tc, r_ts = Res("tc"), Res("ts")
            rotc = self.sb(st0, pfx + "rotc", [128, 32], F32)
            rots = self.sb(st0, pfx + "rots", [128, 32], F32)
            Bre = self.sb(st0, pfx + "Bre", [128, 32, 128], BF16)
            Bim = self.sb(st0, pfx + "Bim", [128, 32, 128], BF16)
            r_B = Res("B")
            Cpr = self.sb(st0, pfx + "Cpr", [128, 32, 128], BF16)
            Cpi = self.sb(st0, pfx + "Cpi", [128, 32, 128], BF16)
            Cprn = self.sb(st0, pfx + "Cprn", [128, 32, 128], BF16)
            r_Cp = Res("Cp")
            rpl = self.sb(st0, pfx + "rpl", [128, 32], F32)
            r_rpl = Res("rpl")
            dsb = self.sb(st0, pfx + "dsb", [128, 8], F32)
            r_dsb = Res("dsb")
            ini_re = self.sb(st0, pfx + "inire", [128, 32], F32)
            ini_im = self.sb(st0, pfx + "iniim", [128, 32], F32)
            r_ini = Res("ini")
            S.dma("pool", lambda e: e.dma_start(out=Bre[:].rearrange("p g m -> p (g m)"), in_=B_d[li, 0]), r_B)
            S.dma("pool", lambda e: e.dma_start(out=Bim[:].rearrange("p g m -> p (g m)"), in_=B_d[li, 1]), r_B)
            S.dma("sp", lambda e: e.dma_start(out=dsb[:], in_=dd_d[li]), r_dsb)
            S.op("dve", lambda e: e.memset(ini_re[:], 0.0), writes=[r_ini])
            S.op("dve", lambda e: e.memset(ini_im[:], 0.0), writes=[r_ini])

            with contextlib.ExitStack() as st:
                def small(name, dt_=F32, n=32):
                    return self.sb(st, pfx + name, [128, n], dt_), Res(name)
                pl, r_pl = small("pl", F32, 96)
                S.dma("sp", lambda e: e.dma_start(out=pl[:], in_=pl_d[li]), r_pl)
                are, aim, ldt = pl[:, 0:32], pl[:, 32:64], pl[:, 64:96]
                dtt, r_dtt = small("dtt")
                th, r_th = small("th")
                cs, r_cs = small("cs")
                sn, r_sn = small("sn")
                ph, r_ph = small("ph")
                nf, r_nf = small("nf")
                ni, r_ni = small("ni", I32)
                t1, r_t1 = small("t1")
                t2, r_t2 = small("t2")
                den, r_den = small("den")
                zr, r_zr = small("zr")
                zi, r_zi = small("zi")
                nzr, r_nzr = small("nzr")
                S.op("act", lambda e: e.activation(dtt[:], ldt, AF.Exp), reads=[r_pl], writes=[r_dtt])
                S.op("dve", lambda e: e.tensor_tensor(th[:], aim, dtt[:], ALU.mult), reads=[r_pl, r_dtt], writes=[r_th])
                S.op("dve", lambda e: e.tensor_tensor(t1[:], are, dtt[:], ALU.mult), reads=[r_pl, r_dtt], writes=[r_t1])
                S.op("act", lambda e: e.activation(rpl[:], t1[:], AF.Exp), reads=[r_t1], writes=[r_rpl])
                S.op("dve", lambda e: e.tensor_copy(ph[:], th[:]), reads=[r_th], writes=[r_ph])
                self.rr_sin(sn[:], ph[:], nf[:], ni[:], 0.0, r_sn, r_ph, r_nf, r_ni)
                S.op("dve", lambda e: e.tensor_copy(ph[:], th[:]), reads=[r_th, r_sn], writes=[r_ph])
                self.rr_sin(cs[:], ph[:], nf[:], ni[:], math.pi / 2, r_cs, r_ph, r_nf, r_ni)
                S.op("dve", lambda e: e.tensor_tensor(cs[:], cs[:], rpl[:], ALU.mult), reads=[r_cs, r_rpl], writes=[r_cs])
                S.op("dve", lambda e: e.tensor_scalar(cs[:], cs[:], -1.0, None, ALU.add), reads=[r_cs], writes=[r_cs])
                S.op("dve", lambda e: e.tensor_tensor(sn[:], sn[:], rpl[:], ALU.mult), reads=[r_sn, r_rpl], writes=[r_sn])
                S.op("dve", lambda e: e.tensor_tensor(den[:], are, are, ALU.mult), reads=[r_pl], writes=[r_den])
                S.op("dve", lambda e: e.tensor_tensor(t1[:], aim, aim, ALU.mult), reads=[r_pl, r_rpl], writes=[r_t1])
                S.op("dve", lambda e: e.tensor_tensor(den[:], den[:], t1[:], ALU.add), reads=[r_den, r_t1], writes=[r_den])
                S.op("dve", lambda e: e.reciprocal(den[:], den[:]), reads=[r_den], writes=[r_den])
                S.op("dve", lambda e: e.tensor_tensor(t1[:], cs[:], are, ALU.mult), reads=[r_cs, r_pl, r_den], writes=[r_t1])
                S.op("dve", lambda e: e.tensor_tensor(t2[:], sn[:], aim, ALU.mult), reads=[r_sn, r_pl], writes=[r_t2])
                S.op("dve", lambda e: e.tensor_tensor(t1[:], t1[:], t2[:], ALU.add), reads=[r_t1, r_t2], writes=[r_t1])
                S.op("dve", lambda e: e.tensor_tensor(zr[:], t1[:], den[:], ALU.mult), reads=[r_t1, r_den], writes=[r_zr])
                S.op("dve", lambda e: e.tensor_scalar(nzr[:], zr[:], -1.0, None, ALU.mult), reads=[r_zr], writes=[r_nzr])
                S.op("dve", lambda e: e.tensor_tensor(t1[:], sn[:], are, ALU.mult), reads=[r_sn, r_pl, r_zr], writes=[r_t1])
                S.op("dve", lambda e: e.tensor_tensor(t2[:], cs[:], aim, ALU.mult), reads=[r_cs, r_pl, r_t1], writes=[r_t2])
                S.op("dve", lambda e: e.tensor_tensor(t1[:], t1[:], t2[:], ALU.subtract), reads=[r_t1, r_t2], writes=[r_t1])
                S.op("dve", lambda e: e.tensor_tensor(zi[:], t1[:], den[:], ALU.mult), reads=[r_t1, r_den], writes=[r_zi])
                Cre = self.sb(st, pfx + "Cre", [128, 32, 128], F32)
                Cim = self.sb(st, pfx + "Cim", [128, 32, 128], F32)
                r_C = Res("C")
                S.dma("sp", lambda e: e.dma_start(out=Cre[:].rearrange("p g m -> p (g m)"), in_=C_d[li, 0]), r_C)
                S.dma("sp", lambda e: e.dma_start(out=Cim[:].rearrange("p g m -> p (g m)"), in_=C_d[li, 1]), r_C)
                tq = [(self.sb(st, pfx + "tq%d" % i, [128, 128], F32), Res("tq")) for i in range(2)]
                for gp in range(32):
                    eng = "dve"
                    tt, r_tt = tq[gp % 2]
                    S.op(eng, lambda e, gp=gp, tt=tt: e.tensor_scalar(tt[:], Cim[:, gp, :], zi[:, gp:gp + 1], None, ALU.mult),
                         reads=[r_C, r_zi], writes=[r_tt])
                    S.op(eng, lambda e, gp=gp, tt=tt: e.scalar_tensor_tensor(
                        Cpr[:, gp, :], Cre[:, gp, :], zr[:, gp:gp + 1], tt[:], ALU.mult, ALU.subtract),
                        reads=[r_C, r_zr, r_tt], writes=[r_Cp])
                    S.op(eng, lambda e, gp=gp, tt=tt: e.tensor_scalar(tt[:], Cre[:, gp, :], zi[:, gp:gp + 1], None, ALU.mult),
                         reads=[r_C, r_zi, r_Cp], writes=[r_tt])
                    S.op(eng, lambda e, gp=gp, tt=tt: e.scalar_tensor_tensor(
                        Cpi[:, gp, :], Cim[:, gp, :], nzr[:, gp:gp + 1], tt[:], ALU.mult, ALU.subtract),
                        reads=[r_C, r_nzr, r_tt], writes=[r_Cp])
                S.op("dve", lambda e: e.tensor_scalar(Cprn[:], Cpr[:], -1.0, None, ALU.mult), reads=[r_Cp], writes=[r_Cp])
                ii = self.sb(st, pfx + "ii", [128, LC + 1], I32)
                io = self.sb(st, pfx + "io", [128, LC + 1], F32)
                r_io = Res("io")
                S.op("pool", lambda e: e.iota(ii[:], [[1, LC + 1]], base=0, channel_multiplier=0), writes=[r_io])
                S.op("pool", lambda e: e.tensor_copy(io[:], ii[:]), reads=[r_io], writes=[r_io])
                GB = 8
                phb = self.sb(st, pfx + "phb", [128, GB, LC + 1], F32)
                nfb = self.sb(st, pfx + "nfb", [128, GB, LC + 1], F32)
                nib = self.sb(st, pfx + "nib", [128, GB, LC + 1], I32)
                r_phb, r_nfb, r_nib = Res("phb"), Res("nfb"), Res("nib")
                TWO_PI = 2.0 * math.pi
                PS = 3.141592
                pv_, nv_, iv_ = phb[:], nfb[:], nib[:]
                for g0 in range(0, 32, GB):
                    for k in range(GB):
                        S.op("dve", lambda e, k=k, g0=g0: e.tensor_scalar(
                            phb[:, k, :], io[:], th[:, g0 + k:g0 + k + 1], None, ALU.mult),
                            reads=[r_io, r_th, r_ts, r_tc], writes=[r_phb])
                    S.op("dve", lambda e: e.tensor_scalar(nv_, pv_, 1.0 / TWO_PI, None, ALU.mult), reads=[r_phb], writes=[r_nfb])
                    S.op("dve", lambda e: e.tensor_copy(iv_, nv_), reads=[r_nfb], writes=[r_nib])
                    S.op("dve", lambda e: e.tensor_copy(nv_, iv_), reads=[r_nib], writes=[r_nfb])
                    S.op("dve", lambda e: e.scalar_tensor_tensor(pv_, nv_, -TWO_PI, pv_, ALU.mult, ALU.add),
                         reads=[r_nfb, r_phb], writes=[r_phb])
                    S.op("dve", lambda e: e.tensor_scalar(nv_, pv_, math.pi, TWO_PI, ALU.is_gt, ALU.mult),
                         reads=[r_phb], writes=[r_nfb])
                    S.op("dve", lambda e: e.tensor_tensor(pv_, pv_, nv_, ALU.subtract), reads=[r_nfb, r_phb], writes=[r_phb])
                    S.op("dve", lambda e: e.tensor_scalar(pv_, pv_, PS, -PS, ALU.min, ALU.max), reads=[r_phb], writes=[r_phb])
                    S.op("act", lambda e, g0=g0: e.activation(ts[:, g0:g0 + GB, :], phb[:, :, 0:LC], AF.Sin),
                         reads=[r_phb], writes=[r_ts])
                    S.op("act", lambda e, g0=g0: e.activation(rots[:, g0:g0 + GB], phb[:, :, LC], AF.Sin),
                         reads=[r_phb], writes=[r_ts])
                    S.op("dve", lambda e: e.tensor_scalar(pv_, pv_, math.pi / 2, None, ALU.add), reads=[r_phb], writes=[r_phb])
                    S.op("dve", lambda e: e.tensor_scalar(nv_, pv_, math.pi, TWO_PI, ALU.is_gt, ALU.mult),
                         reads=[r_phb], writes=[r_nfb])
                    S.op("dve", lambda e: e.tensor_tensor(pv_, pv_, nv_, ALU.subtract), reads=[r_nfb, r_phb], writes=[r_phb])
                    S.op("dve", lambda e: e.tensor_scalar(pv_, pv_, PS, -PS, ALU.min, ALU.max), reads=[r_phb], writes=[r_phb])
                    S.op("act", lambda e, g0=g0: e.activation(tc[:, g0:g0 + GB, :], phb[:, :, 0:LC], AF.Sin),
                         reads=[r_phb], writes=[r_tc])
                    S.op("act", lambda e, g0=g0: e.activation(rotc[:, g0:g0 + GB], phb[:, :, LC], AF.Sin),
                         reads=[r_phb], writes=[r_tc])
            S.barrier()

            with contextlib.ExitStack() as st:
                ubr = Rot([(self.sb(st, pfx + "ub%d" % i, [128, T], BF16), Res("ub")) for i in range(2)])
                u32r = Rot([(self.sb(st, pfx + "u32%d" % i, [128, T], F32), Res("u32")) for i in range(2)])
                ygr = Rot([(self.sb(st, pfx + "yg%d" % i, [128, T], BF16), Res("yg")) for i in range(2)])
                Wk = Rot([[(self.sb(st, pfx + "w%d_%d" % (i, k), [128, 2, LC], F32), Res("w")) for k in range(4)]
                          for i in range(4)])
                Wb = Rot([[(self.sb(st, pfx + "wb%d_%d" % (i, k), [128, 2, LC], BF16), Res("wb")) for k in range(8)]
                          for i in range(4)])
                yvr = Rot([(self.sb(st, pfx + "yv%d" % i, [128, LC], F32), Res("yv")) for i in range(2)])
                cc = self.sb(st, pfx + "cc", [128, 8], F32)
                r_cc = Res("cc")
                psBU = Rot(self.ps[0:6])
                psY = Rot(self.ps[6:8])

                def ld_tile(jt):
                    ub, r_ub = ubr.next()
                    u32, r_u32 = u32r.next()
                    S.dma("pool", lambda e: e.dma_start(out=ub[:], in_=src[jt]), r_ub, reads=[rsrc])
                    S.dma("sp", lambda e: e.dma_start(out=u32[:], in_=src[jt]), r_u32, reads=[rsrc])
                    return ub, r_ub, u32, r_u32

                tiles = {}

                def get_tile(jt):
                    if jt not in tiles and jt < 8:
                        tiles[jt] = ld_tile(jt) + ygr.next()
                    return tiles.get(jt)

                def make_unit(jt, c, q):
                    U = {}
                    gp0 = 4 * jt + 2 * q
                    csl = slice(c * LC, (c + 1) * LC)
                    tcv = tc[:, gp0:gp0 + 2, :]
                    tsv = ts[:, gp0:gp0 + 2, :]

                    def f0():
                        ub, r_ub, u32, r_u32, yg, r_yg = get_tile(jt)
                        if c == min(2, NCH - 1) and q == 0:
                            get_tile(jt + 1)
                        U["k"] = Wk.next()
                        U["b"] = Wb.next()
                        (pre, rpre), (pim, rpim) = psBU.next(), psBU.next()
                        for g_ in range(2):
                            gp = gp0 + g_
                            S.op("pe", lambda e, g_=g_, gp=gp: e.matmul(pre[:, g_ * LC:(g_ + 1) * LC], lhsT=Bre[:, gp, :],
                                                                       rhs=ub[:, csl], start=True, stop=True),
                                 reads=[r_B, r_ub], writes=[rpre])
                            S.op("pe", lambda e, g_=g_, gp=gp: e.matmul(pim[:, g_ * LC:(g_ + 1) * LC], lhsT=Bim[:, gp, :],
                                                                       rhs=ub[:, csl], start=True, stop=True),
                                 reads=[r_B, r_ub], writes=[rpim])
                        prev = pre[:].rearrange("p (a t) -> p a t", a=2)
                        pimv = pim[:].rearrange("p (a t) -> p a t", a=2)
                        (breb, r_breb), (bimb, r_bimb) = U["b"][0], U["b"][1]
                        S.op("act", lambda e: e.activation(breb[:], prev, AF.Copy), reads=[rpre], writes=[r_breb])
                        S.op("act", lambda e: e.activation(bimb[:], pimv, AF.Copy), reads=[rpim], writes=[r_bimb])

                    def f1():
                        (G1, rG1), (G2, rG2), _, _ = U["k"]
                        ((breb, r_breb), (bimb, r_bimb), (Ab, rAb), (Bb, rBb), (Cb, rCb), (Db, rDb), _, _) = U["b"]
                        S.op("dve", lambda e: e.tensor_tensor(Ab[:], breb[:], tcv, ALU.mult), reads=[r_breb, r_tc], writes=[rAb])
                        S.op("dve", lambda e: e.tensor_tensor(Bb[:], bimb[:], tsv, ALU.mult), reads=[r_bimb, r_ts], writes=[rBb])
                        S.op("dve", lambda e: e.tensor_tensor(Cb[:], bimb[:], tcv, ALU.mult), reads=[r_bimb, r_tc], writes=[rCb])
                        S.op("dve", lambda e: e.tensor_tensor(Db[:], breb[:], tsv, ALU.mult), reads=[r_breb, r_ts], writes=[rDb])
                        S.op("pool", lambda e: e.tensor_tensor(G1[:], Ab[:], Bb[:], ALU.add), reads=[rAb, rBb], writes=[rG1])
                        S.op("pool", lambda e: e.tensor_tensor(G2[:], Cb[:], Db[:], ALU.subtract), reads=[rCb, rDb], writes=[rG2])

                    def f2():
                        (G1, rG1), (G2, rG2), (S1, rS1), (S2, rS2) = U["k"]
                        (s1b, r_s1b), (s2b, r_s2b) = U["b"][6], U["b"][7]
                        for g_ in range(2):
                            gp = gp0 + g_
                            rb = rpl[:, gp:gp + 1].to_broadcast([128, LC])
                            S.op("dve", lambda e, g_=g_, gp=gp, rb=rb: e.tensor_tensor_scan(
                                S1[:, g_, :], rb, G1[:, g_, :], ini_re[:, gp:gp + 1], ALU.mult, ALU.add),
                                reads=[rG1, r_rpl, r_ini], writes=[rS1])
                            S.op("dve", lambda e, g_=g_, gp=gp, rb=rb: e.tensor_tensor_scan(
                                S2[:, g_, :], rb, G2[:, g_, :], ini_im[:, gp:gp + 1], ALU.mult, ALU.add),
                                reads=[rG2, r_rpl, r_ini], writes=[rS2])
                        if c + 1 < NCH:
                            glr = S1[:, :, LC - 1]
                            gli = S2[:, :, LC - 1]
                            rc_ = rotc[:, gp0:gp0 + 2]
                            rs_ = rots[:, gp0:gp0 + 2]
                            S.op("pool", lambda e: e.tensor_tensor(cc[:, 0:2], glr, rc_, ALU.mult), reads=[rS1, r_tc], writes=[r_cc])
                            S.op("pool", lambda e: e.tensor_tensor(cc[:, 2:4], gli, rs_, ALU.mult), reads=[rS2, r_ts], writes=[r_cc])
                            S.op("pool", lambda e: e.tensor_tensor(ini_re[:, gp0:gp0 + 2], cc[:, 0:2], cc[:, 2:4], ALU.subtract),
                                 reads=[r_cc], writes=[r_ini])
                            S.op("pool", lambda e: e.tensor_tensor(cc[:, 4:6], glr, rs_, ALU.mult), reads=[rS1, r_ts], writes=[r_cc])
                            S.op("pool", lambda e: e.tensor_tensor(cc[:, 6:8], gli, rc_, ALU.mult), reads=[rS2, r_tc], writes=[r_cc])
                            S.op("pool", lambda e: e.tensor_tensor(ini_im[:, gp0:gp0 + 2], cc[:, 4:6], cc[:, 6:8], ALU.add),
                                 reads=[r_cc], writes=[r_ini])
                        S.op("act", lambda e: e.activation(s1b[:], S1[:], AF.Copy), reads=[rS1], writes=[r_s1b])
                        S.op("act", lambda e: e.activation(s2b[:], S2[:], AF.Copy), reads=[rS2], writes=[r_s2b])

                    def f3():
                        ub, r_ub, u32, r_u32, yg, r_yg = get_tile(jt)
                        (_, _, (Ab, rAb), (Bb, rBb), (Cb, rCb), (Db, rDb), (s1b, r_s1b), (s2b, r_s2b)) = U["b"]
                        S.op("dve", lambda e: e.tensor_tensor(Ab[:], s1b[:], tcv, ALU.mult), reads=[r_s1b, r_tc], writes=[rAb])
                        S.op("pool", lambda e: e.tensor_tensor(Cb[:], s2b[:], tsv, ALU.mult), reads=[r_s2b, r_ts], writes=[rCb])
                        S.op("dve", lambda e: e.tensor_tensor(Bb[:], s1b[:], tsv, ALU.mult), reads=[r_s1b, r_ts], writes=[rBb])
                        S.op("pool", lambda e: e.tensor_tensor(Db[:], s2b[:], tcv, ALU.mult), reads=[r_s2b, r_tc], writes=[rDb])
                        if q == 0:
                            ystate[jt, c] = psY.next()
                        py, rpy = ystate[jt, c]
                        terms = ((Cpr, Ab, rAb), (Cprn, Cb, rCb), (Cpi, Bb, rBb), (Cpi, Db, rDb))
                        for g_ in range(2):
                            gp = gp0 + g_
                            for ti, (Wt, Xt, rXt) in enumerate(terms):
                                first = (q == 0 and g_ == 0 and ti == 0)
                                last = (q == 1 and g_ == 1 and ti == 3)
                                S.op("pe", lambda e, g_=g_, gp=gp, first=first, last=last, Wt=Wt, Xt=Xt: e.matmul(
                                    py[:, 0:LC], lhsT=Wt[:, gp, :], rhs=Xt[:, g_, :], start=first, stop=last),
                                    reads=[r_Cp, rXt], writes=[rpy])
                        if q == 1:
                            yv, r_yv = yvr.next()
                            S.op("dve", lambda e: e.scalar_tensor_tensor(
                                yv[:], u32[:, csl], dsb[:, jt:jt + 1], py[:, 0:LC], ALU.mult, ALU.add),
                                reads=[r_u32, r_dsb, rpy], writes=[r_yv])
                            S.op("act", lambda e: e.activation(yg[:, csl], yv[:], AF.Gelu_apprx_tanh),
                                 reads=[r_yv], writes=[r_yg])
                            del ystate[jt, c]
                            if c == NCH - 1:
                                S.dma("sp", lambda e: e.dma_start(out=yg_d[jt], in_=yg[:]), r_yg_d, reads=[r_yg])
                    return [f0, f1, f2, f3]

                ystate = {}
                units = [make_unit(jt, c, q) for jt in range(8) for c in range(NCH) for q in range(2)]
                NST = 4
                for step in range(len(units) + NST - 1):
                    for st_ in range(NST):
                        n = step - st_
                        if 0 <= n < len(units):
                            units[n][st_]()
            S.barrier()

        with contextlib.ExitStack() as st:
            wo = self.sb(st, pfx + "wo", [128, 8, 2 * D], BF16)
            r_wo = Res("wo")
            S.dma("pool", lambda e: e.dma_start(out=wo[:].rearrange("p j n -> p (j n)"), in_=wo_d[li]), r_wo)
            ybr = Rot([(self.sb(st, pfx + "yb%d" % i, [128, 8, TB], BF16), Res("yb")) for i in range(3)])
            xsr = Rot([(self.sb(st, pfx + "xs%d" % i, [128, 8, TB], F32), Res("xs")) for i in range(3)])
            sgr = Rot([(self.sb(st, pfx + "sg%d" % i, [128, TB], F32), Res("sg")) for i in range(4)])
            L = self.ln_setup(st, pfx)
            psA = Rot(self.ps[0:6])
            psB = Rot(self.ps[6:8])

            def ld(tb):
                yb, r_yb = ybr.next()
                xs, r_xs = xsr.next()
                yv_ = yg_d[:, :, tb * TB:(tb + 1) * TB].rearrange("c p t -> p c t")
                sv = src[:, :, tb * TB:(tb + 1) * TB].rearrange("c p t -> p c t")
                S.dma("sp", lambda e: e.dma_start(out=yb[:], in_=yv_), r_yb, reads=[r_yg_d])
                S.dma("sp", lambda e: e.dma_start(out=xs[:], in_=sv), r_xs, reads=[rsrc])
                return yb, r_yb, xs, r_xs
            nxt = ld(0)
            pend = []
            for tb in range(NTB):
                yb, r_yb, xs, r_xs = nxt
                if tb + 1 < NTB:
                    nxt = ld(tb + 1)
                for dc in range(NDC):
                    pv, rpv = psA.next()
                    pg, rpg = psA.next()
                    for half, (pp, rpp) in enumerate(((pv, rpv), (pg, rpg))):
                        for j in range(8):
                            S.op("pe", lambda e, pp=pp, j=j, dc=dc, yb=yb, half=half: e.matmul(
                                pp[:], lhsT=wo[:, j, half * D + dc * 128:half * D + (dc + 1) * 128], rhs=yb[:, j, :],
                                start=(j == 0), stop=(j == 7)), reads=[r_wo, r_yb], writes=[rpp])
                    sg, r_sg = sgr.next()
                    S.op("act", lambda e, sg=sg, pg=pg: e.activation(sg[:], pg[:], AF.Sigmoid), reads=[rpg], writes=[r_sg])
                    S.op("dve", lambda e, sg=sg, pv=pv: e.tensor_tensor(sg[:], pv[:], sg[:], ALU.mult),
                         reads=[rpv, r_sg], writes=[r_sg])
                    xv = xs[:, dc, :]
                    S.op("dve", lambda e, xv=xv, sg=sg: e.scalar_tensor_tensor(
                        xv, xv, ALPHA, sg[:], ALU.mult, ALU.add), reads=[r_sg, r_xs], writes=[r_xs])
                    for _ in range(2):
                        if pend:
                            pend.pop(0)()
                while pend:
                    pend.pop(0)()
                pend = self.ln_pieces(L, xs[:], r_xs, lnidx, LN_EPS, dst, rdst, tb * TB, psB.next, psB.next, k=tb % 2)
            while pend:
                pend.pop(0)()


def prep_ffn_weights(w_in_list, w_out_list):
    wins, wouts = [], []
    for w_in, w_out in zip(w_in_list, w_out_list):
        a = np.asarray(w_in).reshape(8, 128, 2, NFC, 128)
        a = a.transpose(3, 1, 2, 0, 4).reshape(NFC, 128, 2 * 8 * 128)
        wins.append(a)
        b = np.asarray(w_out).reshape(NFC, 128, NDC, 128)
        b = b.transpose(2, 1, 0, 3).reshape(NDC, 128, NFC * 128)
        wouts.append(b)
    return np.ascontiguousarray(np.stack(wins)), np.ascontiguousarray(np.stack(wouts))


def prep_lnp(ln1_g, ln1_b, lnm_g, lnm_b, ln2_g, ln2_b):
    arr = np.zeros((DEPTH, 3, 2, NDC, 128), np.float32)
    for l in range(DEPTH):
        for i, (g, b) in enumerate(((ln1_g, ln1_b), (lnm_g, lnm_b), (ln2_g, ln2_b))):
            arr[l, i, 0] = np.asarray(g[l]).reshape(NDC, 128)
            arr[l, i, 1] = np.asarray(b[l]).reshape(NDC, 128)
    return np.ascontiguousarray(arr.reshape(DEPTH * 3 * 2 * NDC, 128).T)


def x_to_dev(xb):
    T = xb.shape[0]
    return np.ascontiguousarray(np.asarray(xb).T.reshape(NDC, 128, T))


def x_from_dev(y):
    T = y.shape[-1]
    return np.ascontiguousarray(y.reshape(D, T).T)


def prep_fox_weights(w_in_list, b_f_list, w_o_list):
    wqkv, wf, bf, wo = [], [], [], []
    for w_in, b_f, w_o in zip(w_in_list, b_f_list, w_o_list):
        w_in = np.asarray(w_in)
        a = w_in[:, :3 * D].reshape(8, 128, 3, 8, 128)
        wqkv.append(a.transpose(3, 1, 2, 0, 4).reshape(8, 128, 3 * 8 * 128))
        f = w_in[:, 3 * D:].reshape(8, 128, NH)
        wf.append(f.transpose(1, 0, 2).reshape(128, 8 * NH))
        bf.append(np.asarray(b_f).reshape(NH, 1))
        o = np.asarray(w_o).reshape(8, 128, D)
        wo.append(o.transpose(1, 0, 2).reshape(128, 8 * D))
    c = np.ascontiguousarray
    return c(np.stack(wqkv)), c(np.stack(wf)), c(np.stack(bf)), c(np.stack(wo))


def prep_s5_weights(a_re_l, a_im_l, log_dt_l, b_re_l, b_im_l, c_re_l, c_im_l, d_l, w_out_l):
    pls, Bs, Cs, ds, wos = [], [], [], [], []
    for a_re, a_im, log_dt, b_re, b_im, c_re, c_im, d, w_out in zip(
            a_re_l, a_im_l, log_dt_l, b_re_l, b_im_l, c_re_l, c_im_l, d_l, w_out_l):
        def PL(a):
            return np.asarray(a).reshape(32, 2, 64).transpose(1, 2, 0).reshape(128, 32)
        ldt = np.repeat(np.asarray(log_dt).reshape(64, 1), 64, axis=1)
        pls.append(np.concatenate([PL(a_re), PL(a_im), PL(ldt)], axis=1))
        Bb = np.zeros((2, 32, 128, 128), np.float32)
        Cb = np.zeros((2, 32, 128, 128), np.float32)
        for k, (bb, cc) in enumerate(((b_re, c_re), (b_im, c_im))):
            bb = np.asarray(bb)
            cc = np.asarray(cc)
            for g in range(64):
                gp, g2, g8 = g // 2, g % 2, g % 8
                Bb[k, gp, g8 * 16:(g8 + 1) * 16, g2 * 64:(g2 + 1) * 64] = bb[g].T
                Cb[k, gp, g2 * 64:(g2 + 1) * 64, g8 * 16:(g8 + 1) * 16] = cc[g].T
        Bs.append(Bb.transpose(0, 2, 1, 3).reshape(2, 128, 32 * 128))
        Cs.append(Cb.transpose(0, 2, 1, 3).reshape(2, 128, 32 * 128))
        ds.append(np.asarray(d).reshape(8, 128).T)
        wos.append(np.asarray(w_out).reshape(8, 128, 2 * D).transpose(1, 0, 2).reshape(128, 8 * 2 * D))
    c = np.ascontiguousarray
    return c(np.stack(pls)), c(np.stack(Bs)), c(np.stack(Cs)), c(np.stack(ds)), c(np.stack(wos))


def full_plan():
    plan = []
    for l in range(DEPTH):
        plan.append(("ffn", l, 1))
        plan.append(("fox", l) if l % 2 == 0 else ("s5", l))
        plan.append(("ffn", l, 2))
    return plan


def kernel(x, ffn1_w_in, ffn1_w_out, ln1_g, ln1_b, lnm_g, lnm_b,
           ffn2_w_in, ffn2_w_out, ln2_g, ln2_b,
           fox_w_in, fox_b_f, fox_w_o,
           s5_a_re, s5_a_im, s5_log_dt, s5_b_re, s5_b_im, s5_c_re, s5_c_im,
           s5_d, s5_w_out):
    x = np.asarray(x, dtype=np.float32)
    nb, T, _ = x.shape
    mk = MK(T, full_plan())
    nc = mk.build()
    f32 = lambda a: np.asarray(a, dtype=np.float32)
    w_in_l, w_out_l = [], []
    for l in range(DEPTH):
        w_in_l += [f32(ffn1_w_in[l]), f32(ffn2_w_in[l])]
        w_out_l += [f32(ffn1_w_out[l]), f32(ffn2_w_out[l])]
    win, wout = prep_ffn_weights(w_in_l, w_out_l)
    lnp = prep_lnp(f32(ln1_g), f32(ln1_b), f32(lnm_g), f32(lnm_b), f32(ln2_g), f32(ln2_b))
    nA = fox_w_in.shape[0]
    wqkv, wf, bf, wo = prep_fox_weights([f32(fox_w_in[i]) for i in range(nA)],
                                        [f32(fox_b_f[i]) for i in range(nA)],
                                        [f32(fox_w_o[i]) for i in range(nA)])
    nB = s5_a_re.shape[0]
    L = lambda a: [f32(a[i]) for i in range(nB)]
    pl, Bs, Cs, ds, wos = prep_s5_weights(L(s5_a_re), L(s5_a_im), L(s5_log_dt), L(s5_b_re), L(s5_b_im),
                                          L(s5_c_re), L(s5_c_im), L(s5_d), L(s5_w_out))
    shared = {"lnp": lnp, "ffn_win": win, "ffn_wout": wout,
              "fox_wqkv": wqkv, "fox_wf": wf, "fox_bf": bf, "fox_wo": wo,
              "s5_pl": pl, "s5_B": Bs, "s5_C": Cs, "s5_d": ds, "s5_wo": wos}
    in_maps = []
    for b in range(nb):
        m = dict(shared)
        m["xT"] = x_to_dev(x[b])
        in_maps.append(m)
    res = run_bass_kernel_spmd(nc, in_maps, core_ids=list(range(nb)))
    out = np.stack([x_from_dev(np.asarray(r["yT"])) for r in res.results]).astype(np.float32)
    return out
```
